# Optimizing a Trainium2 kernel written in Bass

```python
import jax, jax.numpy as jnp
from jax import lax
import numpy as np

D_MODEL = 1024
BATCH = 16
SEQ = 4096
DEPTH = 2
DEC_BATCH = 8
DEC_SEQ = 4096
PAST_LEN = 128

N_MEM = 256
MLSTM_HEADS = 4
MLSTM_DK = D_MODEL // 4
MLSTM_DV = D_MODEL // 2
MLSTM_QK = MLSTM_HEADS * MLSTM_DK
MLSTM_V = MLSTM_HEADS * MLSTM_DV
MLSTM_CHUNK = 128
D_RNN = D_MODEL
LRU_BLOCKS = 8
LRU_BW = D_RNN // LRU_BLOCKS
LRU_C = 8.0
CONV_W = 4
XATTN_HEADS = 4
XATTN_HD = D_MODEL // XATTN_HEADS
D_FF = 2816
DN_ALPHA = (2.0 * DEPTH) ** 0.25
DN_BETA = (8.0 * DEPTH) ** -0.25
LN_EPS = 1e-5

OFF_V = 2 * MLSTM_QK
OFF_O = OFF_V + MLSTM_V
OFF_GATE = OFF_O + MLSTM_V
OFF_XR = OFF_GATE + 4 * MLSTM_HEADS
OFF_YR = OFF_XR + D_RNN
OFF_MG = OFF_YR + D_RNN
D_IN = OFF_MG + 2 * D_MODEL
SPLITS = (OFF_V, OFF_O, OFF_GATE, OFF_XR, OFF_YR, OFF_MG)

kernel_name = 'hybrid_mlstm_rglru_encoder'


def layer_norm(x, g, b):
    xf = x.astype(jnp.float32)
    mu = xf.mean(-1, keepdims=True)
    var = jnp.square(xf - mu).mean(-1, keepdims=True)
    y = (xf - mu) * lax.rsqrt(var + LN_EPS)
    return (y * g.astype(jnp.float32) + b.astype(jnp.float32)).astype(x.dtype)


def swiglu_ffn(x, w_in, w_out):
    gate, up = jnp.split(x @ w_in, 2, axis=-1)
    return (jax.nn.silu(gate) * up) @ w_out


def centred_dwconv(x, w, b):
    S = x.shape[1]
    xp = jnp.pad(x, ((0, 0), (CONV_W // 2 - 1, CONV_W // 2), (0, 0)))
    out = b
    for j in range(CONV_W):
        out = out + xp[:, j:j + S] * w[j]
    return out


def mlstm_scan(q, k, v, ig, lf):
    B, S, H, DK = q.shape
    DV = v.shape[-1]
    NC = S // MLSTM_CHUNK

    def to_chunks(a):
        a = a.reshape((B, NC, MLSTM_CHUNK) + a.shape[2:])
        return jnp.moveaxis(jnp.swapaxes(a, 2, 3), 1, 0)

    causal = jnp.tril(jnp.ones((MLSTM_CHUNK, MLSTM_CHUNK), dtype=bool))

    def step(carry, xs):
        C, n, m = carry
        qc, kc, vc, ic, fc = xs
        g = jnp.cumsum(fc, axis=-1)
        G = g[..., -1]
        log_d = jnp.where(causal, g[..., :, None] - g[..., None, :] + ic[..., None, :], -jnp.inf)
        log_inter = g + m[..., None]
        m_t = jnp.maximum(log_inter, log_d.max(-1))
        s = jnp.einsum('bhtd,bhsd->bhts', qc, kc) * jnp.exp(log_d - m_t[..., None])
        w_inter = jnp.exp(log_inter - m_t)
        num = jnp.einsum('bhts,bhsv->bhtv', s, vc) + w_inter[..., None] * jnp.einsum('bhtd,bhdv->bhtv', qc, C)
        den = s.sum(-1) + w_inter * jnp.einsum('bhtd,bhd->bht', qc, n)
        h = num / jnp.maximum(jnp.abs(den), jnp.exp(-m_t))[..., None]
        log_w = G[..., None] - g + ic
        m_new = jnp.maximum(G + m, log_w.max(-1))
        wk = jnp.exp(log_w - m_new[..., None])[..., None] * kc
        decay = jnp.exp(G + m - m_new)
        C = decay[..., None, None] * C + jnp.einsum('bhsd,bhsv->bhdv', wk, vc)
        n = decay[..., None] * n + wk.sum(-2)
        return (C, n, m_new), h

    init = (jnp.zeros((B, H, DK, DV), jnp.float32), jnp.zeros((B, H, DK), jnp.float32),
            jnp.zeros((B, H), jnp.float32))
    _, h = lax.scan(step, init, (to_chunks(q), to_chunks(k), to_chunks(v), to_chunks(ig), to_chunks(lf)))
    h = jnp.swapaxes(jnp.moveaxis(h, 0, 1), 2, 3)
    return h.reshape(B, S, H, DV)


def rglru(x, w_a, b_a, w_x, b_x, lam, reverse):
    B, S, _ = x.shape
    xb = x.reshape(B, S, LRU_BLOCKS, LRU_BW)
    r = jax.nn.sigmoid(jnp.einsum('bsni,nij->bsnj', xb, w_a).reshape(B, S, D_RNN) + b_a)
    i = jax.nn.sigmoid(jnp.einsum('bsni,nij->bsnj', xb, w_x).reshape(B, S, D_RNN) + b_x)
    log_a = -LRU_C * r * jax.nn.softplus(-lam)
    a = jnp.exp(log_a)
    u = jnp.sqrt(-jnp.expm1(2.0 * log_a)) * (i * x)

    def combine(left, right):
        a1, b1 = left
        a2, b2 = right
        return a1 * a2, a2 * b1 + b2

    _, h = lax.associative_scan(combine, (a, u), reverse=reverse, axis=1)
    return h


def parallel_mixer(x, w_in, b_in, w_conv_qk, b_conv_qk, w_conv_r, b_conv_r, mh_gain,
                   lru_wa, lru_ba, lru_wx, lru_bx, lru_lam, w_pm, w_pr, w_out):
    B, S, _ = x.shape
    f32 = jnp.float32
    u = x @ w_in + b_in
    qk, v, o, gates, xr, yr, mg = jnp.split(u, SPLITS, axis=-1)
    qk = jax.nn.silu(centred_dwconv(qk, w_conv_qk, b_conv_qk)).astype(f32)
    q, k = jnp.split(qk, 2, axis=-1)
    q = q.reshape(B, S, MLSTM_HEADS, MLSTM_DK) * (MLSTM_DK ** -0.5)
    k = k.reshape(B, S, MLSTM_HEADS, MLSTM_DK)
    v = v.astype(f32).reshape(B, S, MLSTM_HEADS, MLSTM_DV)
    gates = gates.astype(f32).reshape(B, S, 4, MLSTM_HEADS)
    h_f = mlstm_scan(q, k, v, gates[:, :, 0], jax.nn.log_sigmoid(gates[:, :, 1]))
    rev = lambda a: jnp.flip(a, axis=1)
    h_b = rev(mlstm_scan(rev(q), rev(k), rev(v), rev(gates[:, :, 2]), rev(jax.nn.log_sigmoid(gates[:, :, 3]))))
    h = h_f + h_b
    mu = h.mean(-1, keepdims=True)
    var = jnp.square(h - mu).mean(-1, keepdims=True)
    h = ((h - mu) * lax.rsqrt(var + LN_EPS)).reshape(B, S, MLSTM_V) * mh_gain.astype(f32)
    h_m = (jax.nn.sigmoid(o.astype(f32)) * h).astype(x.dtype)
    xr = centred_dwconv(xr, w_conv_r, b_conv_r).astype(f32)
    h_r = (rglru(xr, lru_wa[0], lru_ba[0], lru_wx[0], lru_bx[0], lru_lam[0], False)
           + rglru(xr, lru_wa[1], lru_ba[1], lru_wx[1], lru_bx[1], lru_lam[1], True))
    h_r = (h_r * jax.nn.gelu(yr.astype(f32))).astype(x.dtype)
    g_m, g_r = jnp.split(jax.nn.sigmoid(mg), 2, axis=-1)
    merged = g_m * (h_m @ w_pm) + g_r * (h_r @ w_pr)
    return merged @ w_out


def memory_cross_attention(x, mem, w_q, w_kv, w_o):
    B, S, _ = x.shape
    M = mem.shape[1]
    q = (x @ w_q).reshape(B, S, XATTN_HEADS, XATTN_HD)
    k, v = jnp.split(mem @ w_kv, 2, axis=-1)
    k = k.reshape(B, M, XATTN_HEADS, XATTN_HD)
    v = v.reshape(B, M, XATTN_HEADS, XATTN_HD)
    s = jnp.einsum('bshd,bmhd->bhsm', q, k).astype(jnp.float32) * (XATTN_HD ** -0.5)
    p = jax.nn.softmax(s, axis=-1).astype(x.dtype)
    out = jnp.einsum('bhsm,bmhd->bshd', p, v).reshape(B, S, D_MODEL)
    return out @ w_o


def encoder_trunk(x, mem, p):
    for l in range(DEPTH):
        x = layer_norm(DN_ALPHA * x + 0.5 * swiglu_ffn(x, p['ff1_in'][l], p['ff1_out'][l]),
                       p['ln_g'][l, 0], p['ln_b'][l, 0])
        mix = parallel_mixer(x, p['w_in'][l], p['b_in'][l], p['w_conv_qk'][l], p['b_conv_qk'][l],
                             p['w_conv_r'][l], p['b_conv_r'][l], p['mh_gain'][l],
                             p['lru_wa'][l], p['lru_ba'][l], p['lru_wx'][l], p['lru_bx'][l], p['lru_lam'][l],
                             p['w_pm'][l], p['w_pr'][l], p['w_out'][l])
        x = layer_norm(DN_ALPHA * x + mix, p['ln_g'][l, 1], p['ln_b'][l, 1])
        m = layer_norm(mem, p['mem_ln_g'][l], p['mem_ln_b'][l])
        xa = memory_cross_attention(x, m, p['xa_wq'][l], p['xa_wkv'][l], p['xa_wo'][l])
        x = layer_norm(DN_ALPHA * x + xa, p['ln_g'][l, 2], p['ln_b'][l, 2])
        x = layer_norm(DN_ALPHA * x + 0.5 * swiglu_ffn(x, p['ff2_in'][l], p['ff2_out'][l]),
                       p['ln_g'][l, 3], p['ln_b'][l, 3])
    return x


def setup_inputs(seed: int = 0) -> dict:
    key = jax.random.key(seed)
    ks = iter(jax.random.split(key, 40))
    f32 = jnp.float32
    D = D_MODEL

    def nrm(shape, scale):
        return jax.random.normal(next(ks), shape, f32) * scale

    x_prompt = nrm((BATCH, SEQ, D), 1.0)
    x_sample = nrm((DEC_BATCH, DEC_SEQ, D), 1.0)
    mem_prompt = nrm((BATCH, N_MEM, D), 1.0)
    mem_sample = nrm((DEC_BATCH, N_MEM, D), 1.0)

    w_in = nrm((DEPTH, D, D_IN), D ** -0.5)
    fgate_bias = jnp.linspace(3.0, 6.0, MLSTM_HEADS, dtype=f32)
    gate_bias = jnp.concatenate([nrm((MLSTM_HEADS,), 0.1), fgate_bias,
                                 nrm((MLSTM_HEADS,), 0.1), fgate_bias])
    b_in = nrm((DEPTH, D_IN), 0.02).at[:, OFF_GATE:OFF_XR].add(gate_bias)
    w_conv_qk = nrm((DEPTH, CONV_W, 2 * MLSTM_QK), CONV_W ** -0.5)
    b_conv_qk = nrm((DEPTH, 2 * MLSTM_QK), 0.02)
    w_conv_r = nrm((DEPTH, CONV_W, D_RNN), CONV_W ** -0.5)
    b_conv_r = nrm((DEPTH, D_RNN), 0.02)
    mh_gain = 1.0 + nrm((DEPTH, MLSTM_V), 0.02)
    lru_wa = nrm((DEPTH, 2, LRU_BLOCKS, LRU_BW, LRU_BW), LRU_BW ** -0.5)
    lru_ba = nrm((DEPTH, 2, D_RNN), 0.02)
    lru_wx = nrm((DEPTH, 2, LRU_BLOCKS, LRU_BW, LRU_BW), LRU_BW ** -0.5)
    lru_bx = nrm((DEPTH, 2, D_RNN), 0.02)
    a_c = jax.random.uniform(next(ks), (DEPTH, 2, D_RNN), f32, minval=0.9, maxval=0.999)
    sig = a_c ** (1.0 / LRU_C)
    lru_lam = jnp.log(sig) - jnp.log1p(-sig)
    w_pm = nrm((DEPTH, MLSTM_V, D), MLSTM_V ** -0.5)
    w_pr = nrm((DEPTH, D_RNN, D), D_RNN ** -0.5)
    w_out = nrm((DEPTH, D, D), D ** -0.5 * DN_BETA)
    xa_wq = nrm((DEPTH, D, D), D ** -0.5)
    xa_wkv = nrm((DEPTH, D, 2 * D), D ** -0.5)
    xa_wo = nrm((DEPTH, D, D), D ** -0.5 * DN_BETA)
    mem_ln_g = 1.0 + nrm((DEPTH, D), 0.02)
    mem_ln_b = nrm((DEPTH, D), 0.02)
    ff1_in = nrm((DEPTH, D, 2 * D_FF), D ** -0.5)
    ff1_out = nrm((DEPTH, D_FF, D), D_FF ** -0.5 * DN_BETA)
    ff2_in = nrm((DEPTH, D, 2 * D_FF), D ** -0.5)
    ff2_out = nrm((DEPTH, D_FF, D), D_FF ** -0.5 * DN_BETA)
    ln_g = 1.0 + nrm((DEPTH, 4, D), 0.02)
    ln_b = nrm((DEPTH, 4, D), 0.02)
    return {'x_prompt': x_prompt, 'x_sample': x_sample, 'mem_prompt': mem_prompt, 'mem_sample': mem_sample,
            'w_in': w_in, 'b_in': b_in, 'w_conv_qk': w_conv_qk, 'b_conv_qk': b_conv_qk,
            'w_conv_r': w_conv_r, 'b_conv_r': b_conv_r, 'mh_gain': mh_gain,
            'lru_wa': lru_wa, 'lru_ba': lru_ba, 'lru_wx': lru_wx, 'lru_bx': lru_bx, 'lru_lam': lru_lam,
            'w_pm': w_pm, 'w_pr': w_pr, 'w_out': w_out,
            'xa_wq': xa_wq, 'xa_wkv': xa_wkv, 'xa_wo': xa_wo, 'mem_ln_g': mem_ln_g, 'mem_ln_b': mem_ln_b,
            'ff1_in': ff1_in, 'ff1_out': ff1_out, 'ff2_in': ff2_in, 'ff2_out': ff2_out,
            'ln_g': ln_g, 'ln_b': ln_b}


def reference(x_prompt, x_sample, mem_prompt, mem_sample, w_in, b_in, w_conv_qk, b_conv_qk,
              w_conv_r, b_conv_r, mh_gain, lru_wa, lru_ba, lru_wx, lru_bx, lru_lam,
              w_pm, w_pr, w_out, xa_wq, xa_wkv, xa_wo, mem_ln_g, mem_ln_b,
              ff1_in, ff1_out, ff2_in, ff2_out, ln_g, ln_b):
    p = dict(w_in=w_in, b_in=b_in, w_conv_qk=w_conv_qk, b_conv_qk=b_conv_qk,
             w_conv_r=w_conv_r, b_conv_r=b_conv_r, mh_gain=mh_gain,
             lru_wa=lru_wa, lru_ba=lru_ba, lru_wx=lru_wx, lru_bx=lru_bx, lru_lam=lru_lam,
             w_pm=w_pm, w_pr=w_pr, w_out=w_out, xa_wq=xa_wq, xa_wkv=xa_wkv, xa_wo=xa_wo,
             mem_ln_g=mem_ln_g, mem_ln_b=mem_ln_b, ff1_in=ff1_in, ff1_out=ff1_out,
             ff2_in=ff2_in, ff2_out=ff2_out, ln_g=ln_g, ln_b=ln_b)
    y_prompt = encoder_trunk(x_prompt, mem_prompt, p)
    y_sample = encoder_trunk(x_sample, mem_sample, p)
    return (y_prompt, y_sample)
```

```python
import math
import os
import numpy as np
from contextlib import ExitStack
import concourse.bass as bass
import concourse.mybir as mybir
from concourse.bass_utils import run_bass_kernel_spmd

F32 = mybir.dt.float32
BF16 = mybir.dt.bfloat16
AF = mybir.ActivationFunctionType
ALU = mybir.AluOpType

D = 1024
S = 4096
T = 512
NT = S // T
DFF = 2816
DIN = 10256
OFF_V, OFF_O, OFF_GATE, OFF_XR, OFF_YR, OFF_MG = 2048, 4096, 6144, 6160, 7184, 8208
NMEM = 256
ALPHA = 4.0 ** 0.25
LN_EPS = 1e-5
EPS_S = LN_EPS / (ALPHA * ALPHA)
C_FF = 0.5 / ALPHA
C_MIX = 1.0 / ALPHA
LN16 = math.log(16.0)


class Buf:
    __slots__ = ("name", "lw", "rd", "excl")

    def __init__(self, name="", excl=False):
        self.name = name
        self.lw = None
        self.rd = {}
        self.excl = excl


def _flat(xs):
    out = []
    for x in xs:
        if x is None:
            continue
        if isinstance(x, Buf):
            out.append(x)
        elif isinstance(x, (list, tuple)):
            out.extend(_flat(x))
        else:
            out.extend(x.bufs)
    return out


class Prog:
    ENG = ("pe", "act", "dve", "pool")
    DMAQ = ("sp", "actq", "poolq")

    def __init__(self, nc, n_dma_sems=16):
        self.nc = nc
        self.es = ExitStack()
        self.ops = {e: [] for e in ("pe", "act", "dve", "pool", "sp")}
        self.sem = {}
        self.cnt = {}
        for e in self.ENG:
            self.sem[e] = self.es.enter_context(nc.semaphore("s_" + e))
            self.cnt[e] = 0
        self.dsem = {}
        self.dcnt = {}
        self.dnext = {}
        for q in self.DMAQ:
            self.dsem[q] = [self.es.enter_context(nc.semaphore(f"d_{q}_{i}")) for i in range(n_dma_sems)]
            self.dcnt[q] = [0] * n_dma_sems
            self.dnext[q] = 0
        self.waited = {}
        self.nops = 0
        self.nwaits = 0

    @staticmethod
    def _issuer(stream):
        return {"sp": "sp", "actq": "act", "poolq": "pool"}.get(stream, stream)

    def _need(self, issuer, waits, dep):
        if dep is None:
            return
        key, val = dep
        if issuer == "pe" and key == ("e", "pe"):
            return
        if self.waited.get((issuer, key), 0) >= val:
            return
        if waits.get(key, 0) < val:
            waits[key] = val

    def _deps(self, issuer, waits, reads, writes):
        for b in reads:
            self._need(issuer, waits, b.lw)
        for b in writes:
            self._need(issuer, waits, b.lw)
            for v in b.rd.values():
                self._need(issuer, waits, v)

    def op(self, eng, fn, reads=(), writes=()):
        reads = _flat(reads)
        writes = _flat(writes)
        ex = [b for b in reads if b.excl]
        if ex:
            reads = [b for b in reads if not b.excl]
            writes = writes + ex
        waits = {}
        self._deps(eng, waits, reads, writes)
        for key, val in waits.items():
            self.waited[(eng, key)] = val
        self.cnt[eng] += 1
        key = ("e", eng)
        me = (key, self.cnt[eng])
        self.ops[eng].append((fn, list(waits.items()), key, 1))
        for b in reads:
            b.rd[key] = me
        for b in writes:
            b.lw = me
            b.rd = {}
        self.nops += 1
        self.nwaits += len(waits)

    def dma(self, q, fn, reads=(), writes=()):
        reads = _flat(reads)
        writes = _flat(writes)
        issuer = self._issuer(q)
        waits = {}
        i = self.dnext[q]
        self.dnext[q] = (i + 1) % len(self.dsem[q])
        key = ("d", q, i)
        if self.dcnt[q][i] > 0:
            self._need(issuer, waits, (key, self.dcnt[q][i]))
        self._deps(issuer, waits, reads, writes)
        for k2, val in waits.items():
            self.waited[(issuer, k2)] = val
        self.dcnt[q][i] += 16
        me = (key, self.dcnt[q][i])
        self.ops[issuer].append((fn, list(waits.items()), key, 16))
        for b in reads:
            b.rd[key] = me
        for b in writes:
            b.lw = me
            b.rd = {}
        self.nops += 1
        self.nwaits += len(waits)

    def _semof(self, key):
        if key[0] == "e":
            return self.sem[key[1]]
        return self.dsem[key[1]][key[2]]

    def final_wait_all(self, eng="sp"):
        waits = []
        for e in self.ENG:
            if self.cnt[e]:
                waits.append((("e", e), self.cnt[e]))
        for q in self.DMAQ:
            for i, c in enumerate(self.dcnt[q]):
                if c:
                    waits.append((("d", q, i), c))
        self.ops[eng].append((None, waits, None, 0))

    def emit(self):
        nc = self.nc
        engobj = {"pe": "tensor", "act": "scalar", "dve": "vector", "pool": "gpsimd", "sp": "sync"}
        with nc.allow_non_contiguous_dma(reason="small strided parameter / halo transfers"), nc.Block() as block:
            for e, attr in engobj.items():
                lst = self.ops[e]
                if not lst:
                    continue

                def body(eng, lst=lst):
                    for fn, waits, key, inc in lst:
                        for k2, val in waits:
                            eng.wait_ge(self._semof(k2), val)
                        if fn is not None:
                            fn(eng).then_inc(self._semof(key), inc)

                getattr(block, attr)(body)


class Vw:
    __slots__ = ("ap", "bufs")

    def __init__(self, ap, bufs):
        self.ap = ap
        self.bufs = bufs if isinstance(bufs, list) else [bufs]

    def __getitem__(self, idx):
        return Vw(self.ap[idx], self.bufs)


class Builder:
    def __init__(self, nseq=3, nlayers=2, stop=None, dbg=()):
        self.nseq = nseq
        self.nlayers = nlayers
        self.stop = stop
        self.dbg = set(dbg)
        nc = self.nc = bass.Bass("TRN2", target_bir_lowering=False)
        self.P = Prog(nc)
        self.es = self.P.es
        self.dbg_outs = {}
        self._decl_dram()
        self._decl_sbuf()

    def _decl_dram(self):
        nc = self.nc
        n = self.nseq
        L = 2
        di = lambda name, shape: nc.dram_tensor(name, list(shape), F32, kind="ExternalInput").ap()
        self.x_in = di("x", (n, S, D))
        self.mem_in = di("mem", (n, NMEM, D))
        self.win = {}
        shapes = dict(w_in=(L, D, DIN), b_in=(L, DIN), w_conv_qk=(L, 4, 2048), b_conv_qk=(L, 2048),
                      w_conv_r=(L, 4, 1024), b_conv_r=(L, 1024), mh_gain=(L, 2048),
                      lru_wa=(L, 2, 8, 128, 128), lru_ba=(L, 2, 1024), lru_wx=(L, 2, 8, 128, 128),
                      lru_bx=(L, 2, 1024), lru_lam=(L, 2, 1024), w_pm=(L, 2048, D), w_pr=(L, D, D),
                      w_out=(L, D, D), xa_wq=(L, D, D), xa_wkv=(L, D, 2 * D), xa_wo=(L, D, D),
                      mem_ln_g=(L, D), mem_ln_b=(L, D), ff1_in=(L, D, 2 * DFF), ff1_out=(L, DFF, D),
                      ff2_in=(L, D, 2 * DFF), ff2_out=(L, DFF, D), ln_g=(L, 4, D), ln_b=(L, 4, D))
        for k, shp in shapes.items():
            self.win[k] = di(k, shp)
        self.y_out = nc.dram_tensor("y", [n, S, D], F32, kind="ExternalOutput").ap()
        self.wb = {}
        self.wb_buf = {}
        for k in ("w_in", "w_pm", "w_pr", "w_out", "xa_wq", "xa_wkv", "xa_wo", "ff1_in", "ff1_out", "ff2_in", "ff2_out"):
            shp = shapes[k]
            self.wb[k] = nc.dram_tensor("wb_" + k, list(shp), BF16, kind="Internal").ap()
            self.wb_buf[k] = []
        for k in ("lru_wa", "lru_wx"):
            self.wb[k] = nc.dram_tensor("wb_" + k, [L, 2, 8, 128, 128], BF16, kind="Internal").ap()
            self.wb_buf[k] = []
        sc = lambda name, shape, dt: nc.dram_tensor(name, list(shape), dt, kind="Internal").ap()
        self.X1 = sc("X1", (128, 8, S), F32)
        self.XN = sc("XN", (128, 8, S), F32)
        self.X1B = sc("X1B", (128, 8, S), BF16)
        self.XNB = sc("XNB", (128, 8, S), BF16)
        self.QKP = sc("QKP", (128, 16, S + 4), BF16)
        self.XRP = sc("XRP", (128, 8, S + 4), BF16)
        self.QKC = sc("QKC", (128, 16, S), BF16)
        self.VS = sc("VS", (S, 2048), BF16)
        self.SIGO = sc("SIGO", (S, 2048), BF16)
        self.HF = sc("HF", (S, 2048), F32)
        self.HMT = sc("HMT", (128, 16, S), BF16)
        self.HRT = sc("HRT", (128, 8, S), BF16)
        self.GSC = sc("GSC", (2, 64, S), F32)
        self.dbuf = {k: Buf(k) for k in ("X1", "XN", "QKP", "XRP", "QKC", "VS", "SIGO", "HF", "HMT", "HRT", "GSC")}
        self.x1_b = [Buf(f"X1_{i}") for i in range(NT)]
        self.hmt_b = [Buf(f"HMT_{i}") for i in range(32)]
        self.xn_b = [Buf(f"XN_{i}") for i in range(NT)]

    def dump(self, name, src_ap, shape, dt, bufs, q="sp"):
        if name not in self.dbg:
            return
        d = self.nc.dram_tensor("dbg_" + name, list(shape), dt, kind="ExternalOutput").ap()
        self.dbg_outs[name] = d
        self.P.dma(q, lambda e: e.dma_start(out=d, in_=src_ap), reads=bufs)

    def _decl_sbuf(self):
        nc = self.nc
        es = self.es
        sb = lambda name, shape, dt=F32: es.enter_context(nc.sbuf_tensor(name, list(shape), dt))
        self.AR_BYTES = 116736
        self.arena = sb("arena", (128, self.AR_BYTES // 4))
        self.abuf = [Buf(f"ar{i}") for i in range(self.AR_BYTES // 2048)]
        self.NW = 3
        self.WSLOT = 8192
        self.wpool = sb("wpool", (128, self.NW * self.WSLOT), BF16)
        self.wbufs = [Buf(f"w{i}") for i in range(self.NW)]
        self.wnext = 0
        self.prm = sb("prm", (128, 2 * 320))
        self.prm_b = Buf("prm")
        self.prm_off = {}
        self.idb = sb("idb", (128, 128), BF16)
        self.idf = sb("idf", (128, 128))
        self.onesf = sb("onesf", (128, 128))
        self.onesm = sb("onesm", (128, 128), BF16)
        self.ones1 = sb("ones1", (128, 128), BF16)
        self.maskf = sb("maskf", (128, 128), BF16)
        self.maskb = sb("maskb", (128, 128), BF16)
        self.const_b = Buf("const")
        self.stg = sb("stg", (128, 128))
        self.stg_b = Buf("stg")
        self.gwi = sb("gwi", (128, 8, 64), BF16)
        self.gwf = sb("gwf", (128, 8, 64), BF16)
        self.gw_b = Buf("gw")
        self.gbias = sb("gbias", (64, 2))
        self.gb_b = Buf("gb")
        self.tg = sb("tg", (128, 32, 32))
        self.tg_b = Buf("tg")
        self.rr = sb("rr", (64, 2, 32))
        self.rr_b = Buf("rr")
        self.cst_f = sb("cst_f", (128, 2, 512))
        self.cst_b = sb("cst_b", (128, 2, 512), BF16)
        self.nst_f = sb("nst_f", (128, 2))
        self.nst_b = sb("nst_b", (128, 2), BF16)
        self.cst_fb = Buf("cst_f")
        self.cst_bb = Buf("cst_b")
        self.nst_fb = Buf("nst_f")
        self.nst_bb = Buf("nst_b")
        self.sm = sb("sm", (128, 128))
        self.sm_b = [Buf(f"sm{i}") for i in range(8)]
        self.dg = sb("dg", (128, 4, 128), BF16)
        self.dg_b = Buf("dg")
        self.lw = sb("lw", (128, 4, 128), BF16)
        self.lw_b = Buf("lw")
        self.lsc = sb("lsc", (128, 3, 16))
        self.lsc_b = Buf("lsc")
        self.lt = sb("lt", (128, 3, 16))
        self.lt_b = Buf("lt")
        self.akT = sb("akT", (128, 8, 256), BF16)
        self.av = sb("av", (128, 2, 1024), BF16)
        self.akv_b = Buf("akv")
        self.psum = [es.enter_context(nc.psum_tensor(f"ps{i}", [128, 512], F32)) for i in range(8)]
        self.ps_b = [Buf(f"ps{i}", excl=True) for i in range(8)]

    def av_(self, off, dt, shape):
        esz = 2 if dt == BF16 else 4
        nel = int(np.prod(shape))
        nb = nel * esz
        assert off % 4 == 0 and nb % 4 == 0 and off + nb <= self.AR_BYTES, (off, nb)
        ap = self.arena[:, off // 4:(off + nb) // 4]
        if dt == BF16:
            ap = ap.bitcast(BF16)
        if len(shape) == 2:
            ap = ap.rearrange("p (a b) -> p a b", a=shape[0])
        elif len(shape) == 3:
            ap = ap.rearrange("p (a b c) -> p a b c", a=shape[0], b=shape[1])
        bufs = self.abuf[off // 2048:(off + nb + 2047) // 2048]
        return Vw(ap, list(bufs))

    def avk(self, off, dt, shape, k):
        esz = 2 if dt == BF16 else 4
        inner = int(np.prod(shape[1:]))
        v = self.av_(off + k * inner * esz, dt, shape[1:])
        return v

    def ps(self, i, dt=F32):
        ap = self.psum[i][:]
        if dt == BF16:
            ap = ap.bitcast(BF16)
        return Vw(ap, [self.ps_b[i]])

    def wslot(self):
        i = self.wnext
        self.wnext = (i + 1) % self.NW
        return i

    def wview(self, i, shape):
        nel = int(np.prod(shape))
        assert nel <= self.WSLOT
        ap = self.wpool[:, i * self.WSLOT:i * self.WSLOT + nel]
        if len(shape) == 2:
            ap = ap.rearrange("p (a b) -> p a b", a=shape[0])
        elif len(shape) == 3:
            ap = ap.rearrange("p (a b c) -> p a b c", a=shape[0], b=shape[1])
        return Vw(ap, [self.wbufs[i]])

    def load_w(self, key, l, r0, kt, c0, nc_, q="sp"):
        i = self.wslot()
        v = self.wview(i, (kt, nc_))
        src = self.wb[key][l, r0:r0 + kt * 128, c0:c0 + nc_].rearrange("(k p) n -> p k n", p=128)
        self.P.dma(q, lambda e: e.dma_start(out=v.ap, in_=src), reads=[self.wb_buf[key]], writes=v)
        return v

    def setup(self):
        P = self.P
        cb = self.const_b
        P.op("pool", lambda e: e.memset(self.onesf[:], 1.0), writes=[cb])
        P.op("pool", lambda e: e.memset(self.ones1[:], 1.0), writes=[cb])
        P.op("pool", lambda e: e.memset(self.onesm[:], 1.0 / 1024.0), writes=[cb])
        sel = lambda out, op, cm, pat: (lambda e: e.affine_select(out=out[:], in_=self.onesf[:], pattern=[[pat, 128]],
                                                                    compare_op=op, fill=0.0, base=0, channel_multiplier=cm))
        P.op("pool", sel(self.idf, ALU.is_equal, -1, 1), reads=[cb], writes=[cb])
        P.op("pool", sel(self.idb, ALU.is_equal, -1, 1), reads=[cb], writes=[cb])
        P.op("pool", sel(self.maskf, ALU.is_ge, -1, 1), reads=[cb], writes=[cb])
        P.op("pool", sel(self.maskb, ALU.is_ge, 1, -1), reads=[cb], writes=[cb])
        z = self.av_(0, BF16, (16, 4))
        P.op("pool", lambda e: e.memset(z.ap, 0.0), writes=z)
        for dst, nj, bk in ((self.QKP, 16, "QKP"), (self.XRP, 8, "XRP")):
            P.dma("sp", lambda e, dst=dst, nj=nj: e.dma_start(out=dst[:, :, 0:1], in_=z.ap[:, 0:nj, 0:1]), reads=z, writes=[self.dbuf[bk]])
            P.dma("sp", lambda e, dst=dst, nj=nj: e.dma_start(out=dst[:, :, S + 1:S + 4], in_=z.ap[:, 0:nj, 0:3]), reads=z, writes=[self.dbuf[bk]])
        for key in self.wb:
            src = self.win[key]
            dst = self.wb[key]
            if key in ("lru_wa", "lru_wx"):
                s2 = src.rearrange("l d n i j -> (l d n i) j")
                d2 = dst.rearrange("l d n i j -> (l d n i) j")
            else:
                s2 = src.rearrange("l r c -> (l r) c")
                d2 = dst.rearrange("l r c -> (l r) c")
            R, C = s2.shape
            for r0 in range(0, R, 512):
                r1 = min(R, r0 + 512)
                for c0 in range(0, C, 2048):
                    c1 = min(C, c0 + 2048)
                    bb = Buf("wbc")
                    self.wb_buf[key].append(bb)
                    P.dma("poolq", lambda e, s2=s2, d2=d2, r0=r0, r1=r1, c0=c0, c1=c1:
                          e.dma_start(out=d2[r0:r1, c0:c1], in_=s2[r0:r1, c0:c1]), writes=[bb])
        segs = []
        for l in range(self.nlayers):
            w = self.win
            segs += [(("ln_g", l), w["ln_g"][l].rearrange("g (k p) -> (g k) p", p=128)),
                     (("ln_b", l), w["ln_b"][l].rearrange("g (k p) -> (g k) p", p=128)),
                     (("b_qk", l), w["b_in"][l, 0:2048].rearrange("(k p) -> k p", p=128)),
                     (("b_xr", l), w["b_in"][l, OFF_XR:OFF_YR].rearrange("(k p) -> k p", p=128)),
                     (("b_yr", l), w["b_in"][l, OFF_YR:OFF_MG].rearrange("(k p) -> k p", p=128)),
                     (("b_mg", l), w["b_in"][l, OFF_MG:DIN].rearrange("(k p) -> k p", p=128)),
                     (("wc_qk", l), w["w_conv_qk"][l].rearrange("t (k p) -> (t k) p", p=128)),
                     (("bc_qk", l), w["b_conv_qk"][l].rearrange("(k p) -> k p", p=128)),
                     (("wc_r", l), w["w_conv_r"][l].rearrange("t (k p) -> (t k) p", p=128)),
                     (("bc_r", l), w["b_conv_r"][l].rearrange("(k p) -> k p", p=128)),
                     (("lru_ba", l), w["lru_ba"][l].rearrange("d (k p) -> (d k) p", p=128)),
                     (("lru_bx", l), w["lru_bx"][l].rearrange("d (k p) -> (d k) p", p=128)),
                     (("lru_lam", l), w["lru_lam"][l].rearrange("d (k p) -> (d k) p", p=128)),
                     (("mem_g", l), w["mem_ln_g"][l].rearrange("(k p) -> k p", p=128)),
                     (("mem_b", l), w["mem_ln_b"][l].rearrange("(k p) -> k p", p=128))]
        off = 0
        for i, (key, src) in enumerate(segs):
            J = src.shape[0]
            self.prm_off[key] = off
            pi = i % 2
            P.dma("sp", lambda e, src=src, J=J: e.dma_start(out=self.stg[0:J, :], in_=src), writes=[self.stg_b])
            pv = self.ps(pi)
            P.op("pe", lambda e, J=J, pv=pv: e.transpose(out=pv.ap[:, 0:J], in_=self.stg[0:J, :], identity=self.idf[0:J, 0:J]),
                 reads=[self.stg_b, cb], writes=pv)
            P.op("dve", lambda e, J=J, pv=pv, off=off: e.tensor_copy(out=self.prm[:, off:off + J], in_=pv.ap[:, 0:J]),
                 reads=pv, writes=[self.prm_b])
            off += J
        assert off <= 640, off

    def pc(self, key, l, j=0, n=1):
        o = self.prm_off[(key, l)] + j
        return self.prm[:, o:o + n]

    def layer_setup(self, l):
        P = self.P
        w = self.win
        lam = self.pc("lru_lam", l, 0, 16)
        lt = self.lt
        P.op("act", lambda e: e.activation(out=lt[:, 0, :], in_=lam, func=AF.Exp, scale=-1.0), reads=[self.prm_b], writes=[self.lt_b])
        P.op("dve", lambda e: e.tensor_scalar(out=lt[:, 1, :], in0=lt[:, 0, :], scalar1=-0.25, scalar2=1.0 / 3.0, op0=ALU.mult, op1=ALU.add), reads=[self.lt_b], writes=[self.lt_b])
        P.op("dve", lambda e: e.tensor_tensor(out=lt[:, 1, :], in0=lt[:, 1, :], in1=lt[:, 0, :], op=ALU.mult), reads=[self.lt_b], writes=[self.lt_b])
        P.op("dve", lambda e: e.tensor_scalar(out=lt[:, 1, :], in0=lt[:, 1, :], scalar1=-0.5, scalar2=None, op0=ALU.add), reads=[self.lt_b], writes=[self.lt_b])
        P.op("dve", lambda e: e.tensor_tensor(out=lt[:, 1, :], in0=lt[:, 1, :], in1=lt[:, 0, :], op=ALU.mult), reads=[self.lt_b], writes=[self.lt_b])
        P.op("dve", lambda e: e.tensor_scalar(out=lt[:, 1, :], in0=lt[:, 1, :], scalar1=1.0, scalar2=None, op0=ALU.add), reads=[self.lt_b], writes=[self.lt_b])
        P.op("dve", lambda e: e.tensor_tensor(out=lt[:, 2, :], in0=lt[:, 1, :], in1=lt[:, 0, :], op=ALU.mult), reads=[self.lt_b], writes=[self.lt_b])
        P.op("dve", lambda e: e.tensor_scalar(out=self.lsc[:, 0, :], in0=lt[:, 2, :], scalar1=-8.0, scalar2=None, op0=ALU.mult), reads=[self.lt_b], writes=[self.lsc_b])
        P.op("dve", lambda e: e.tensor_scalar(out=self.lsc[:, 1, :], in0=lt[:, 2, :], scalar1=8.0, scalar2=None, op0=ALU.mult), reads=[self.lt_b], writes=[self.lsc_b])
        P.op("dve", lambda e: e.tensor_scalar(out=self.lsc[:, 2, :], in0=lt[:, 2, :], scalar1=-16.0, scalar2=None, op0=ALU.mult), reads=[self.lt_b], writes=[self.lsc_b])
        P.op("pool", lambda e: e.memset(self.gwi[:], 0.0), writes=[self.gw_b])
        P.op("pool", lambda e: e.memset(self.gwf[:], 0.0), writes=[self.gw_b])
        P.op("pool", lambda e: e.memset(self.gbias[:], 0.0), writes=[self.gb_b])
        gsrc = lambda c: self.wb["w_in"][l, :, OFF_GATE + c:OFF_GATE + c + 4].rearrange("(k p) n -> p k n", p=128)
        nc = self.nc
        with nc.allow_non_contiguous_dma(reason="tiny gate weight columns"):
            for (dst, c, col) in ((self.gwi, 0, 0), (self.gwf, 4, 0), (self.gwi, 8, 32), (self.gwf, 12, 32)):
                P.dma("sp", lambda e, dst=dst, c=c, col=col: e.dma_start(out=dst[:, :, col:col + 4], in_=gsrc(c)),
                      reads=[self.wb_buf["w_in"]], writes=[self.gw_b])
            for (bcol, c, row) in ((0, 0, 0), (1, 4, 0), (0, 8, 32), (1, 12, 32)):
                P.dma("sp", lambda e, bcol=bcol, c=c, row=row: e.dma_start(
                    out=self.gbias[row:row + 4, bcol:bcol + 1],
                    in_=w["b_in"][l, OFF_GATE + c:OFF_GATE + c + 4].rearrange("(p o) -> p o", o=1)), writes=[self.gb_b])

    O_XF = 0
    O_XB = 16384
    O_ZB = 24576
    O_ZQ = 32768
    O_MEAN = 40960
    O_RSTD = 43008
    O_TMP = 45056
    O_TMP2 = 47104
    O_H = 49152
    O_ST = 71680
    O_X1B = 79872
    O_G = 88064
    O_Q = 104448
    O_E = 112640

    def layer_norm_fm(self, l, gi):
        P = self.P
        ps_s, ps_q = self.ps(6), self.ps(7)
        for k in range(8):
            zk = self.avk(self.O_XF, F32, (8, 512), k)
            zb = self.avk(self.O_ZB, BF16, (8, 512), k)
            zq = self.avk(self.O_ZQ, BF16, (8, 512), k)
            P.op("dve", lambda e, zb=zb, zk=zk: e.tensor_copy(out=zb.ap, in_=zk.ap), reads=zk, writes=zb)
            P.op("act", lambda e, zq=zq, zk=zk: e.activation(out=zq.ap, in_=zk.ap, func=AF.Square), reads=zk, writes=zq)
        for k in range(8):
            zb = self.avk(self.O_ZB, BF16, (8, 512), k)
            P.op("pe", lambda e, zb=zb, k=k: e.matmul(ps_s.ap, lhsT=self.onesm[:], rhs=zb.ap, start=(k == 0), stop=(k == 7)),
                 reads=[zb, self.const_b], writes=ps_s)
        for k in range(8):
            zq = self.avk(self.O_ZQ, BF16, (8, 512), k)
            P.op("pe", lambda e, zq=zq, k=k: e.matmul(ps_q.ap, lhsT=self.onesm[:], rhs=zq.ap, start=(k == 0), stop=(k == 7)),
                 reads=[zq, self.const_b], writes=ps_q)
        mean = self.av_(self.O_MEAN, F32, (512,))
        rstd = self.av_(self.O_RSTD, F32, (512,))
        tmp = self.av_(self.O_TMP, F32, (512,))
        P.op("act", lambda e: e.activation(out=mean.ap, in_=ps_s.ap, func=AF.Copy), reads=ps_s, writes=mean)
        P.op("act", lambda e: e.activation(out=tmp.ap, in_=ps_s.ap, func=AF.Square), reads=ps_s, writes=tmp)
        P.op("dve", lambda e: e.tensor_tensor(out=tmp.ap, in0=ps_q.ap, in1=tmp.ap, op=ALU.subtract), reads=[ps_q, tmp], writes=tmp)
        P.op("dve", lambda e: e.tensor_scalar(out=tmp.ap, in0=tmp.ap, scalar1=0.0, scalar2=EPS_S, op0=ALU.max, op1=ALU.add), reads=tmp, writes=tmp)
        P.op("act", lambda e: e.activation(out=tmp.ap, in_=tmp.ap, func=AF.Ln), reads=tmp, writes=tmp)
        P.op("act", lambda e: e.activation(out=rstd.ap, in_=tmp.ap, func=AF.Exp, scale=-0.5), reads=tmp, writes=rstd)
        for k in range(8):
            zk = self.avk(self.O_XF, F32, (8, 512), k)
            xb = self.avk(self.O_XB, BF16, (8, 512), k)
            P.op("dve", lambda e, zk=zk: e.tensor_tensor(out=zk.ap, in0=zk.ap, in1=mean.ap, op=ALU.subtract), reads=[zk, mean], writes=zk)
            P.op("dve", lambda e, zk=zk: e.tensor_tensor(out=zk.ap, in0=zk.ap, in1=rstd.ap, op=ALU.mult), reads=[zk, rstd], writes=zk)
            g = self.pc("ln_g", l, gi * 8 + k)
            b = self.pc("ln_b", l, gi * 8 + k)
            P.op("act", lambda e, zk=zk, xb=xb, g=g, b=b: e.activation(out=xb.ap, in_=zk.ap, func=AF.Identity, scale=g, bias=b),
                 reads=[zk, self.prm_b], writes=xb)
            P.op("act", lambda e, zk=zk, g=g, b=b: e.activation(out=zk.ap, in_=zk.ap, func=AF.Identity, scale=g, bias=b),
                 reads=[zk, self.prm_b], writes=zk)

    def ffn_fm(self, l, which):
        P = self.P
        kin, kout = which + "_in", which + "_out"
        pi = 0
        for g in range(11):
            wg = self.load_w(kin, l, 0, 8, g * 256, 256, q="sp")
            wu = self.load_w(kin, l, 0, 8, DFF + g * 256, 256, q="sp")
            for j in range(2):
                m = g * 2 + j
                pg, pu = self.ps(pi % 6), self.ps((pi + 1) % 6)
                pi += 2
                for k in range(8):
                    xb = self.avk(self.O_XB, BF16, (8, 512), k)
                    P.op("pe", lambda e, pg=pg, wg=wg, xb=xb, k=k, j=j: e.matmul(pg.ap, lhsT=wg.ap[:, k, j * 128:(j + 1) * 128], rhs=xb.ap,
                                                                                 start=(k == 0), stop=(k == 7)), reads=[wg, xb], writes=pg)
                for k in range(8):
                    xb = self.avk(self.O_XB, BF16, (8, 512), k)
                    P.op("pe", lambda e, pu=pu, wu=wu, xb=xb, k=k, j=j: e.matmul(pu.ap, lhsT=wu.ap[:, k, j * 128:(j + 1) * 128], rhs=xb.ap,
                                                                                 start=(k == 0), stop=(k == 7)), reads=[wu, xb], writes=pu)
                tmp = self.av_(self.O_TMP2 if m % 2 else self.O_TMP, F32, (512,))
                hm = self.avk(self.O_H, BF16, (22, 512), m)
                P.op("act", lambda e, tmp=tmp, pg=pg: e.activation(out=tmp.ap, in_=pg.ap, func=AF.Silu), reads=pg, writes=tmp)
                P.op("dve", lambda e, hm=hm, tmp=tmp, pu=pu: e.tensor_tensor(out=hm.ap, in0=pu.ap, in1=tmp.ap, op=ALU.mult), reads=[pu, tmp], writes=hm)
        for n4 in range(4):
            wo = self.load_w(kout, l, 0, 22, n4 * 256, 256, q="sp")
            for j in range(2):
                m = n4 * 2 + j
                py = self.ps(pi % 6)
                pi += 1
                for k in range(22):
                    hk = self.avk(self.O_H, BF16, (22, 512), k)
                    P.op("pe", lambda e, py=py, wo=wo, hk=hk, k=k, j=j: e.matmul(py.ap, lhsT=wo.ap[:, k, j * 128:(j + 1) * 128], rhs=hk.ap,
                                                                                 start=(k == 0), stop=(k == 21)), reads=[wo, hk], writes=py)
                xk = self.avk(self.O_XF, F32, (8, 512), m)
                P.op("dve", lambda e, xk=xk, py=py: e.scalar_tensor_tensor(out=xk.ap, in0=py.ap, scalar=C_FF, in1=xk.ap, op0=ALU.mult, op1=ALU.add),
                     reads=[py, xk], writes=xk)

    def proj_fm(self, key, l, c0, nchunks, kt, rhs_of, evac, r0=0, ps_cycle=6):
        P = self.P
        per = min(self.WSLOT // (kt * 128), nchunks)
        per = max(1, per)
        m = 0
        pi = getattr(self, "_pi", 0)
        while m < nchunks:
            n_here = min(per, nchunks - m)
            wv = self.load_w(key, l, r0, kt, c0 + m * 128, n_here * 128)
            for j in range(n_here):
                pv = self.ps(pi % ps_cycle)
                pi += 1
                for k in range(kt):
                    rk = rhs_of(k)
                    P.op("pe", lambda e, pv=pv, wv=wv, rk=rk, k=k, j=j: e.matmul(pv.ap, lhsT=wv.ap[:, k, j * 128:(j + 1) * 128], rhs=rk.ap,
                                                                                 start=(k == 0), stop=(k == kt - 1)), reads=[wv, rk], writes=pv)
                evac(m + j, pv)
            m += n_here
        self._pi = pi

    def phase_a(self, s, l):
        P = self.P
        w = self.win
        bcv = self.av_(self.O_X1B, BF16, (2048,))
        bco = self.av_(self.O_X1B + 4096, BF16, (2048,))
        P.dma("poolq", lambda e: e.dma_start(out=bcv.ap, in_=w["b_in"][l, OFF_V:OFF_O].partition_broadcast(128)), writes=bcv)
        P.dma("poolq", lambda e: e.dma_start(out=bco.ap, in_=w["b_in"][l, OFF_O:OFF_GATE].partition_broadcast(128)), writes=bco)
        bcgA = self.av_(self.O_Q, F32, (2048,))
        P.dma("sp", lambda e: e.dma_start(out=bcgA.ap, in_=w["mh_gain"][l].partition_broadcast(128)), writes=bcgA)
        for tt in range(NT):
            t0 = tt * T
            xf = self.av_(self.O_XF, F32, (8, 512))
            if l == 0:
                xt = self.av_(self.O_G, F32, (4, 1024))
                P.dma("actq", lambda e, xt=xt, t0=t0: e.dma_start(out=xt.ap, in_=self.x_in[s, t0:t0 + T, :].rearrange("(a p) d -> p a d", p=128)), writes=xt)
                for k in range(8):
                    pv = self.ps(k % 6)
                    for a in range(4):
                        P.op("pe", lambda e, pv=pv, xt=xt, a=a, k=k: e.transpose(out=pv.ap[:, a * 128:(a + 1) * 128], in_=xt.ap[:, a, k * 128:(k + 1) * 128], identity=self.idf[:]),
                             reads=[xt, self.const_b], writes=pv)
                    xk = self.avk(self.O_XF, F32, (8, 512), k)
                    xb = self.avk(self.O_XB, BF16, (8, 512), k)
                    P.op("act", lambda e, xk=xk, pv=pv: e.activation(out=xk.ap, in_=pv.ap, func=AF.Copy), reads=pv, writes=xk)
                    P.op("dve", lambda e, xb=xb, pv=pv: e.tensor_copy(out=xb.ap, in_=pv.ap), reads=pv, writes=xb)
            else:
                P.dma("actq", lambda e, xf=xf, t0=t0: e.dma_start(out=xf.ap, in_=self.XN[:, :, t0:t0 + T]), reads=[self.xn_b[tt]], writes=xf)
                xbv = self.av_(self.O_XB, BF16, (8, 512))
                P.dma("actq", lambda e, xbv=xbv, t0=t0: e.dma_start(out=xbv.ap, in_=self.XNB[:, :, t0:t0 + T]), reads=[self.xn_b[tt]], writes=xbv)
            self.ffn_fm(l, "ff1")
            self.layer_norm_fm(l, 0)
            P.dma("actq", lambda e, xf=xf, t0=t0: e.dma_start(out=self.X1[:, :, t0:t0 + T], in_=xf.ap), reads=xf, writes=[self.x1_b[tt]])
            xbv_ = self.av_(self.O_XB, BF16, (8, 512))
            P.dma("actq", lambda e, xbv_=xbv_, t0=t0: e.dma_start(out=self.X1B[:, :, t0:t0 + T], in_=xbv_.ap), reads=xbv_, writes=[self.x1_b[tt]])
            if self.stop == "A1":
                continue
            xbk = lambda k: self.avk(self.O_XB, BF16, (8, 512), k)
            st = self.av_(self.O_H, BF16, (16, 512))

            def ev_qk(m, pv, st=st):
                o = self.avk(self.O_H, BF16, (16, 512), m)
                b = self.pc("b_qk", l, m)
                P.op("act", lambda e: e.activation(out=o.ap, in_=pv.ap, func=AF.Identity, bias=b), reads=[pv, self.prm_b], writes=o)
            self.proj_fm("w_in", l, 0, 16, 8, xbk, ev_qk)
            P.dma("actq", lambda e, st=st, t0=t0: e.dma_start(out=self.QKP[:, :, 1 + t0:1 + t0 + T], in_=st.ap), reads=st, writes=[self.dbuf["QKP"]])
            st2 = self.av_(self.O_ST, BF16, (8, 512))

            def ev_xr(m, pv):
                o = self.avk(self.O_ST, BF16, (8, 512), m)
                b = self.pc("b_xr", l, m)
                P.op("act", lambda e: e.activation(out=o.ap, in_=pv.ap, func=AF.Identity, bias=b), reads=[pv, self.prm_b], writes=o)
            self.proj_fm("w_in", l, OFF_XR, 8, 8, xbk, ev_xr)
            P.dma("actq", lambda e, st2=st2, t0=t0: e.dma_start(out=self.XRP[:, :, 1 + t0:1 + t0 + T], in_=st2.ap), reads=st2, writes=[self.dbuf["XRP"]])
            for (gw, col) in ((self.gwi, 0), (self.gwf, 1)):
                pv = self.ps(6)
                gst = self.av_(self.O_ZB + col * 2048, F32, (512,))
                for k in range(8):
                    xb = xbk(k)
                    P.op("pe", lambda e, pv=pv, gw=gw, xb=xb, k=k: e.matmul(pv.ap[0:64, :], lhsT=gw[:, k, :], rhs=xb.ap, start=(k == 0), stop=(k == 7)),
                         reads=[xb, self.gw_b], writes=pv)
                P.op("act", lambda e, pv=pv, gst=gst, col=col: e.activation(out=gst.ap[0:64, :], in_=pv.ap[0:64, :], func=AF.Identity,
                                                                              bias=self.gbias[:, col:col + 1]), reads=[pv, self.gb_b], writes=gst)
                P.dma("actq", lambda e, gst=gst, col=col, t0=t0: e.dma_start(out=self.GSC[col, :, t0:t0 + T], in_=gst.ap[0:64, :]), reads=gst, writes=[self.dbuf["GSC"]])
            for (c0, bc, dstd, bk, sig) in ((OFF_V, bcv.ap, self.VS, "VS", False), (OFF_O, bco.ap, self.SIGO, "SIGO", True)):
                stv = self.av_(self.O_H if not sig else self.O_G, BF16, (4, 2048))
                for cg in range(4):
                    wv = self.load_w("w_in", l, 0, 8, c0 + cg * 512, 512)
                    for a in range(4):
                        pv = self.ps((cg * 4 + a) % 6)
                        for k in range(8):
                            xb = xbk(k)
                            P.op("pe", lambda e, pv=pv, wv=wv, xb=xb, k=k, a=a: e.matmul(pv.ap, lhsT=xb.ap[:, a * 128:(a + 1) * 128], rhs=wv.ap[:, k, :],
                                                                                         start=(k == 0), stop=(k == 7)), reads=[wv, xb], writes=pv)
                        o = Vw(stv.ap[:, a, cg * 512:(cg + 1) * 512], stv.bufs)
                        if not sig:
                            P.op("dve", lambda e, o=o, pv=pv, bc=bc, cg=cg: e.tensor_tensor(out=o.ap, in0=pv.ap, in1=bc[:, cg * 512:(cg + 1) * 512], op=ALU.add),
                                 reads=[pv, bcv, bco], writes=o)
                        else:
                            tmp = self.av_(self.O_TMP if a % 2 else self.O_TMP2, F32, (512,))
                            P.op("dve", lambda e, tmp=tmp, pv=pv, bc=bc, cg=cg: e.tensor_tensor(out=tmp.ap, in0=pv.ap, in1=bc[:, cg * 512:(cg + 1) * 512], op=ALU.add),
                                 reads=[pv, bcv, bco], writes=tmp)
                            P.op("act", lambda e, tmp=tmp: e.activation(out=tmp.ap, in_=tmp.ap, func=AF.Sigmoid), reads=tmp, writes=tmp)
                            P.op("dve", lambda e, o=o, tmp=tmp, cg=cg: e.tensor_tensor(out=o.ap, in0=tmp.ap, in1=bcgA.ap[:, cg * 512:(cg + 1) * 512], op=ALU.mult), reads=[tmp, bcgA], writes=o)
                P.dma("actq", lambda e, stv=stv, dstd=dstd, t0=t0: e.dma_start(out=dstd[t0:t0 + T, :].rearrange("(a p) c -> p a c", p=128), in_=stv.ap),
                      reads=stv, writes=[self.dbuf[bk]])

    def phase_b(self, s, l):
        self.conv_qk(l)
        if self.stop == "B0":
            return
        self.rglru(l)
        if self.stop == "B1":
            return
        self.gate_prep()
        if self.stop == "B2g":
            return
        self.mlstm(l)

    def build_diag(self, wkey, l, j, nj):
        P = self.P
        for tap in range(4):
            wcol = self.pc(wkey, l, tap * nj + j)
            P.op("dve", lambda e, tap=tap, wcol=wcol: e.tensor_scalar(out=self.dg[:, tap, :], in0=self.idf[:], scalar1=wcol, scalar2=None, op0=ALU.mult),
                 reads=[self.prm_b, self.const_b], writes=[self.dg_b])

    def conv_qk(self, l):
        P = self.P
        for j in range(16):
            par = j % 2
            row = self.av_(par * 8208, BF16, (S + 4,))
            out = self.av_(16416 + par * 8192, BF16, (S,))
            P.dma("sp", lambda e, row=row, j=j: e.dma_start(out=row.ap, in_=self.QKP[:, j, :]), reads=[self.dbuf["QKP"]], writes=row)
            self.build_diag("wc_qk", l, j, 16)
            b = self.pc("bc_qk", l, j)
            for tt in range(NT):
                pv = self.ps(tt % 6)
                for tap in range(4):
                    P.op("pe", lambda e, pv=pv, row=row, tap=tap, tt=tt: e.matmul(pv.ap, lhsT=self.dg[:, tap, :], rhs=row.ap[:, tt * T + tap:tt * T + tap + T],
                                                                                  start=(tap == 0), stop=(tap == 3)), reads=[row, self.dg_b], writes=pv)
                P.op("act", lambda e, pv=pv, out=out, tt=tt, b=b: e.activation(out=out.ap[:, tt * T:(tt + 1) * T], in_=pv.ap, func=AF.Silu, bias=b),
                     reads=[pv, self.prm_b], writes=out)
            P.dma("actq", lambda e, out=out, j=j: e.dma_start(out=self.QKC[:, j, :], in_=out.ap), reads=out, writes=[self.dbuf["QKC"]])

    def rglru(self, l):
        P = self.P
        A = self.av_
        O_XCF, O_XCB, O_ROW, O_R, O_I, O_A, O_T1, O_HF = 0, 16384, 24576, 34816, 51200, 67584, 83968, 100352
        NQ = 4
        QN = S // NQ
        xcf = A(O_XCF, F32, (S,))
        xcb = A(O_XCB, BF16, (S,))
        row = A(O_ROW, BF16, (S + 4,))
        rT = A(O_R, F32, (S,))
        hf = A(O_HF, F32, (S,))
        f32q = lambda off, q: A(off + q * QN * 4, F32, (QN,))
        f32t = lambda off, tt: A(off + tt * T * 4, F32, (T,))
        for n in range(8):
            P.dma("sp", lambda e, n=n: e.dma_start(out=row.ap, in_=self.XRP[:, n, :]), reads=[self.dbuf["XRP"]], writes=row)
            self.build_diag("wc_r", l, n, 8)
            b = self.pc("bc_r", l, n)
            for tt in range(NT):
                pv = self.ps(tt % 6)
                for tap in range(4):
                    P.op("pe", lambda e, pv=pv, tap=tap, tt=tt: e.matmul(pv.ap, lhsT=self.dg[:, tap, :], rhs=row.ap[:, tt * T + tap:tt * T + tap + T],
                                                                         start=(tap == 0), stop=(tap == 3)), reads=[row, self.dg_b], writes=pv)
                xf_t = f32t(O_XCF, tt)
                xb_t = A(O_XCB + tt * T * 2, BF16, (T,))
                P.op("act", lambda e, pv=pv, xf_t=xf_t, b=b: e.activation(out=xf_t.ap, in_=pv.ap, func=AF.Identity, bias=b), reads=[pv, self.prm_b], writes=xf_t)
                P.op("dve", lambda e, pv=pv, xb_t=xb_t, b=b: e.tensor_scalar(out=xb_t.ap, in0=pv.ap, scalar1=b, scalar2=None, op0=ALU.add), reads=[pv, self.prm_b], writes=xb_t)
            if "xrc" in self.dbg and n == 0:
                self.dump("xrc", xcf.ap, (128, S), F32, xcf)
            for d in range(2):
                for wi, key in enumerate(("lru_wa", "lru_wx")):
                    P.dma("sp", lambda e, d=d, wi=wi, key=key, n=n: e.dma_start(out=self.lw[:, d * 2 + wi, :], in_=self.wb[key][l, d, n]),
                          reads=[self.wb_buf[key]], writes=[self.lw_b])
            for d in range(2):
                for wi, (doff, bkey) in enumerate(((O_R, "lru_ba"), (O_I, "lru_bx"))):
                    bb = self.pc(bkey, l, d * 8 + n)
                    for tt in range(NT):
                        pv = self.ps((tt + wi * 3) % 6)
                        xb_t = A(O_XCB + tt * T * 2, BF16, (T,))
                        dst = f32t(doff, tt)
                        P.op("pe", lambda e, pv=pv, d=d, wi=wi, xb_t=xb_t: e.matmul(pv.ap, lhsT=self.lw[:, d * 2 + wi, :], rhs=xb_t.ap, start=True, stop=True),
                             reads=[xb_t, self.lw_b], writes=pv)
                        P.op("act", lambda e, pv=pv, dst=dst, bb=bb: e.activation(out=dst.ap, in_=pv.ap, func=AF.Sigmoid, bias=bb), reads=[pv, self.prm_b], writes=dst)
                s1 = self.lsc[:, 0, d * 8 + n:d * 8 + n + 1]
                s1n = self.lsc[:, 1, d * 8 + n:d * 8 + n + 1]
                s2 = self.lsc[:, 2, d * 8 + n:d * 8 + n + 1]
                qs = list(range(NQ)) if d == 0 else list(range(NQ - 1, -1, -1))
                R = lambda q: f32q(O_R, q)
                I = lambda q: f32q(O_I, q)
                Aq = lambda q: f32q(O_A, q)
                T1 = lambda q: f32q(O_T1, q)
                X = lambda q: f32q(O_XCF, q)
                for q in qs:
                    P.op("act", lambda e, s1=s1, a=Aq(q), r=R(q): e.activation(out=a.ap, in_=r.ap, func=AF.Exp, scale=s1), reads=[R(q), self.lsc_b], writes=Aq(q))
                for q in qs:
                    P.op("act", lambda e, s1n=s1n, t=T1(q), r=R(q): e.activation(out=t.ap, in_=r.ap, func=AF.Tanh, scale=s1n), reads=[R(q), self.lsc_b], writes=T1(q))
                for q in qs:
                    P.op("act", lambda e, s2=s2, r=R(q): e.activation(out=r.ap, in_=r.ap, func=AF.Exp, scale=s2), reads=[R(q), self.lsc_b], writes=R(q))
                for q in qs:
                    P.op("dve", lambda e, t=T1(q), r=R(q): e.scalar_tensor_tensor(out=t.ap, in0=r.ap, scalar=1.0, in1=t.ap, op0=ALU.add, op1=ALU.mult), reads=[R(q), T1(q)], writes=T1(q))
                for q in qs:
                    P.op("dve", lambda e, i_=I(q), x=X(q): e.tensor_tensor(out=i_.ap, in0=i_.ap, in1=x.ap, op=ALU.mult), reads=[I(q), X(q)], writes=I(q))
                for q in qs:
                    P.op("act", lambda e, t=T1(q): e.activation(out=t.ap, in_=t.ap, func=AF.Sqrt), reads=T1(q), writes=T1(q))
                for q in qs:
                    P.op("dve", lambda e, t=T1(q), i_=I(q): e.tensor_tensor(out=t.ap, in0=t.ap, in1=i_.ap, op=ALU.mult), reads=[T1(q), I(q)], writes=T1(q))
                prev = None
                for q in qs:
                    if d == 0:
                        o = f32q(O_HF, q)
                        init = 0.0 if prev is None else prev.ap[:, QN - 1:QN]
                        P.op("dve", lambda e, o=o, a=Aq(q), t=T1(q), init=init: e.tensor_tensor_scan(out=o.ap, data0=a.ap, data1=t.ap, initial=init, op0=ALU.mult, op1=ALU.add),
                             reads=[Aq(q), T1(q), prev], writes=o)
                    else:
                        o = R(q)
                        init = 0.0 if prev is None else prev.ap[:, 0:1]
                        P.op("dve", lambda e, o=o, a=Aq(q), t=T1(q), init=init: e.tensor_tensor_scan(out=o.ap[:, ::-1], data0=a.ap[:, ::-1], data1=t.ap[:, ::-1], initial=init, op0=ALU.mult, op1=ALU.add),
                             reads=[Aq(q), T1(q), prev], writes=o)
                    prev = o
            hsum = A(O_ROW, BF16, (S,))
            for q in range(NQ):
                hq = A(O_ROW + q * QN * 2, BF16, (QN,))
                P.op("dve", lambda e, hq=hq, a=f32q(O_HF, q), r=f32q(O_R, q): e.tensor_tensor(out=hq.ap, in0=a.ap, in1=r.ap, op=ALU.add), reads=[f32q(O_HF, q), f32q(O_R, q)], writes=[hq])
            P.dma("actq", lambda e, n=n: e.dma_start(out=self.HRT[:, n, :], in_=hsum.ap), reads=hsum, writes=[self.dbuf["HRT"], row])

    def gate_prep(self):
        P = self.P
        K16 = 16384
        GI = self.av_(2 * K16, F32, (S,))
        GF = self.av_(3 * K16, F32, (S,))
        P.dma("sp", lambda e: e.dma_start(out=GI.ap[0:64, :], in_=self.GSC[0]), reads=[self.dbuf["GSC"]], writes=GI)
        P.dma("sp", lambda e: e.dma_start(out=GF.ap[0:64, :], in_=self.GSC[1]), reads=[self.dbuf["GSC"]], writes=GF)
        MG = self.av_(0, F32, (S,))
        TQ = self.av_(K16, F32, (S,))
        R2 = lambda v: v.ap[0:64, :]
        P.op("act", lambda e: e.activation(out=R2(GF), in_=R2(GF), func=AF.Exp, scale=-1.0), reads=GF, writes=GF)
        P.op("act", lambda e: e.activation(out=R2(GF), in_=R2(GF), func=AF.Ln, bias=1.0), reads=GF, writes=GF)
        P.op("dve", lambda e: e.memset(R2(MG), 1.0), reads=[], writes=MG)
        P.op("dve", lambda e: e.tensor_tensor_scan(out=TQ.ap[0:4, :], data0=MG.ap[0:4, :], data1=GF.ap[0:4, :], initial=0.0, op0=ALU.mult, op1=ALU.add),
             reads=[GF, MG], writes=TQ)
        P.op("dve", lambda e: e.tensor_tensor_scan(out=TQ.ap[32:36, ::-1], data0=MG.ap[32:36, ::-1], data1=GF.ap[32:36, ::-1], initial=0.0, op0=ALU.mult, op1=ALU.add),
             reads=[GF, MG], writes=TQ)
        P.op("dve", lambda e: e.tensor_tensor(out=R2(GI), in0=R2(GI), in1=R2(TQ), op=ALU.add), reads=[GI, TQ], writes=GI)
        P.op("pool", lambda e: e.tensor_copy(out=R2(GF), in_=R2(TQ)), reads=TQ, writes=GF)
        P.op("dve", lambda e: e.tensor_tensor_scan(out=MG.ap[0:4, :], data0=GI.ap[0:4, :], data1=GI.ap[0:4, :], initial=0.0, op0=ALU.max, op1=ALU.max), reads=GI, writes=MG)
        P.op("dve", lambda e: e.tensor_tensor_scan(out=MG.ap[32:36, ::-1], data0=GI.ap[32:36, ::-1], data1=GI.ap[32:36, ::-1], initial=0.0, op0=ALU.max, op1=ALU.max), reads=GI, writes=MG)
        rr = self.rr
        mg3 = MG.ap.rearrange("p (c t) -> p c t", t=128)
        P.op("dve", lambda e: e.memset(rr[:], 0.0), writes=[self.rr_b])
        P.op("dve", lambda e: e.tensor_copy(out=rr[0:4, 1, :], in_=mg3[0:4, :, 127]), reads=MG, writes=[self.rr_b])
        P.op("dve", lambda e: e.tensor_copy(out=rr[32:36, 1, :], in_=mg3[32:36, :, 0]), reads=MG, writes=[self.rr_b])
        P.op("dve", lambda e: e.tensor_copy(out=rr[0:4, 0, 1:32], in_=rr[0:4, 1, 0:31]), reads=[self.rr_b], writes=[self.rr_b])
        P.op("dve", lambda e: e.tensor_copy(out=rr[32:36, 0, 0:31], in_=rr[32:36, 1, 1:32]), reads=[self.rr_b], writes=[self.rr_b])
        v3 = lambda v: v.ap[0:64, :].rearrange("p (c t) -> p c t", t=128)
        bc3 = lambda q: rr[0:64, q, :].rearrange("p (c o) -> p c o", o=1).to_broadcast([64, 32, 128])
        for q in range(4):
            if q == 0:
                P.op("dve", lambda e: e.tensor_tensor(out=v3(TQ), in0=v3(GI), in1=bc3(0), op=ALU.subtract), reads=[GI, self.rr_b], writes=TQ)
                P.op("dve", lambda e: e.tensor_scalar(out=R2(TQ), in0=R2(TQ), scalar1=-LN16, scalar2=None, op0=ALU.add), reads=TQ, writes=TQ)
                P.op("act", lambda e: e.activation(out=R2(TQ), in_=R2(TQ), func=AF.Exp), reads=TQ, writes=TQ)
            elif q == 1:
                P.op("dve", lambda e: e.tensor_tensor(out=v3(TQ), in0=v3(GI), in1=bc3(1), op=ALU.subtract), reads=[GI, self.rr_b], writes=TQ)
                P.op("dve", lambda e: e.tensor_scalar(out=R2(TQ), in0=R2(TQ), scalar1=-LN16, scalar2=None, op0=ALU.add), reads=TQ, writes=TQ)
                P.op("act", lambda e: e.activation(out=R2(TQ), in_=R2(TQ), func=AF.Exp), reads=TQ, writes=TQ)
            elif q == 2:
                P.op("dve", lambda e: e.tensor_tensor(out=v3(TQ), in0=v3(GF), in1=bc3(0), op=ALU.subtract), reads=[GF, self.rr_b], writes=TQ)
                P.op("act", lambda e: e.activation(out=R2(TQ), in_=R2(TQ), func=AF.Exp), reads=TQ, writes=TQ)
            else:
                P.op("dve", lambda e: e.memset(R2(TQ), 0.0), writes=TQ)
                P.op("dve", lambda e: e.tensor_tensor(out=v3(TQ), in0=v3(TQ), in1=bc3(0), op=ALU.add), reads=[TQ, self.rr_b], writes=TQ)
                P.op("dve", lambda e: e.tensor_tensor(out=v3(TQ), in0=v3(TQ), in1=bc3(1), op=ALU.subtract), reads=[TQ, self.rr_b], writes=TQ)
                P.op("act", lambda e: e.activation(out=R2(TQ), in_=R2(TQ), func=AF.Exp), reads=TQ, writes=TQ)
            for c in range(32):
                pv = self.ps(c % 6)
                P.op("pe", lambda e, pv=pv, c=c: e.transpose(out=pv.ap[:, 0:64], in_=TQ.ap[0:64, c * 128:(c + 1) * 128], identity=self.idf[0:64, 0:64]),
                     reads=[TQ, self.const_b], writes=pv)
                P.op("dve", lambda e, pv=pv, c=c, q=q: e.tensor_copy(out=self.tg[:, c, q * 8:q * 8 + 4], in_=pv.ap[:, 0:4]), reads=pv, writes=[self.tg_b])
                P.op("act", lambda e, pv=pv, c=c, q=q: e.activation(out=self.tg[:, c, q * 8 + 4:q * 8 + 8], in_=pv.ap[:, 32:36], func=AF.Copy), reads=pv, writes=[self.tg_b])

    def mlstm(self, l):
        P = self.P
        A = self.av_
        CH = 28672
        hf_b = [[Buf(f"hf{h}_{c}") for c in range(32)] for h in range(4)]
        sm = self.sm
        pending = []
        for hp in range(2):
            chains = [(hh, d) for hh in range(2) for d in range(2)]
            st = {}
            for ci_, (hh, d) in enumerate(chains):
                o = ci_ * CH
                cf = A(o, F32, (2, 512))
                cb = A(o + 4096, BF16, (2, 512))
                nf = A(o + 6144, F32, (2,))
                nb = A(o + 6152, BF16, (2,))
                st[ci_] = (cf, cb, nf, nb, o + 8192)
                P.op("pool", lambda e, cf=cf: e.memset(cf.ap, 0.0), writes=cf)
                P.op("pool", lambda e, cb=cb: e.memset(cb.ap, 0.0), writes=cb)
                P.op("pool", lambda e, nf=nf: e.memset(nf.ap, 0.0), writes=[nf, nb])
                P.op("pool", lambda e, nb=nb: e.memset(nb.ap, 0.0), writes=[nf, nb])
            for i in range(32):
                par = i % 2
                combine = i >= 16
                S_ = {k: {} for k in ("p1", "A", "E1", "E2", "G", "H", "I", "J", "K", "L", "M")}
                for ci_, (hh, d) in enumerate(chains):
                    h = hp * 2 + hh
                    c = i if d == 0 else 31 - i
                    c0 = c * 128
                    cf, cb, nf, nb, ot = st[ci_]
                    ot = ot + par * 10240
                    colv = [self.tg[:, c, q * 8 + d * 4 + h:q * 8 + d * 4 + h + 1] for q in range(4)]
                    qc = A(ot, BF16, (2, 128))
                    kc = A(ot + 512, BF16, (2, 128))
                    vch = A(ot + 1024, BF16, (512,))
                    ktl = A(ot + 5120, BF16, (256,))
                    pT = A(ot + 5632, BF16, (128,))
                    hst = A(ot + 6144, F32, (512,))
                    hfl = A(ot + 2048, F32, (512,))
                    sgo = A(ot + 4096, BF16, (512,))
                    hmb = A(ot + 8192, BF16, (512,))
                    hmt = A(ot + 9216, BF16, (4, 128))
                    pb = (ci_ % 2) * 4
                    psm_b = self.ps(pb, BF16)
                    psm = self.ps(pb)
                    pA = self.ps(pb + 1)
                    pCs = [self.ps(pb + 2), self.ps(pb + 3)]
                    mask = self.maskf if d == 0 else self.maskb
                    sbase = ci_ * 32 + par * 16
                    sb_ = self.sm_b[ci_ * 2 + par]
                    den = sm[:, sbase:sbase + 1]
                    st6 = sm[:, sbase + 2:sbase + 8]
                    mv = sm[:, sbase + 8:sbase + 10]
                    rs = sm[:, sbase + 10:sbase + 11]
                    nmr = sm[:, sbase + 11:sbase + 12]
                    hfb = hf_b[h][c]

                    def p1(qc=qc, kc=kc, vch=vch, ktl=ktl, pT=pT, psm=psm, psm_b=psm_b, colv=colv, mask=mask, c0=c0, h=h, hfl=hfl, sgo=sgo, hfb=hfb):
                        P.dma("sp", lambda e: e.dma_start(out=qc.ap, in_=self.QKC[:, 2 * h:2 * h + 2, c0:c0 + 128]), reads=[self.dbuf["QKC"]], writes=qc)
                        P.dma("sp", lambda e: e.dma_start(out=kc.ap, in_=self.QKC[:, 8 + 2 * h:8 + 2 * h + 2, c0:c0 + 128]), reads=[self.dbuf["QKC"]], writes=kc)
                        P.dma("sp", lambda e: e.dma_start(out=vch.ap, in_=self.VS[c0:c0 + 128, h * 512:(h + 1) * 512]), reads=[self.dbuf["VS"]], writes=vch)
                        if combine:
                            P.dma("sp", lambda e: e.dma_start(out=hfl.ap, in_=self.HF[c0:c0 + 128, h * 512:(h + 1) * 512]), reads=[hfb], writes=hfl)
                            P.dma("sp", lambda e: e.dma_start(out=sgo.ap, in_=self.SIGO[c0:c0 + 128, h * 512:(h + 1) * 512]), reads=[self.dbuf["SIGO"]], writes=sgo)
                        for dc in range(2):
                            P.op("pe", lambda e, dc=dc: e.transpose(out=psm_b.ap[:, dc * 128:(dc + 1) * 128], in_=kc.ap[:, dc, :], identity=self.idb[:]),
                                 reads=[kc, self.const_b], writes=psm_b)
                        for dc in range(2):
                            P.op("pe", lambda e, dc=dc: e.matmul(psm.ap[:, 256:384], lhsT=kc.ap[:, dc, :], rhs=qc.ap[:, dc, :], start=(dc == 0), stop=(dc == 1)),
                                 reads=[kc, qc], writes=psm)
                        P.op("act", lambda e: e.activation(out=ktl.ap, in_=psm_b.ap[:, 0:256], func=AF.Copy, scale=colv[1]), reads=[psm_b, self.tg_b], writes=ktl)
                        P.op("dve", lambda e: e.scalar_tensor_tensor(out=pT.ap, in0=psm.ap[:, 256:384], scalar=colv[0], in1=mask[:], op0=ALU.mult, op1=ALU.mult),
                             reads=[psm, self.tg_b, self.const_b], writes=pT)

                    def stA(qc=qc, vch=vch, ktl=ktl, pT=pT, psm=psm, pA=pA, pCs=pCs, cb=cb, nb=nb):
                        P.op("pe", lambda e: e.matmul(pA.ap, lhsT=pT.ap, rhs=vch.ap, start=True, stop=False), reads=[pT, vch], writes=pA)
                        for dc in range(2):
                            P.op("pe", lambda e, dc=dc: e.matmul(pA.ap, lhsT=qc.ap[:, dc, :], rhs=cb.ap[:, dc, :], start=False, stop=(dc == 1)), reads=[qc, cb], writes=pA)
                        P.op("pe", lambda e: e.matmul(psm.ap[:, 384:385], lhsT=pT.ap, rhs=self.ones1[:, 0:1], start=True, stop=False), reads=[pT, self.const_b], writes=psm)
                        for dc in range(2):
                            P.op("pe", lambda e, dc=dc: e.matmul(psm.ap[:, 384:385], lhsT=qc.ap[:, dc, :], rhs=nb.ap[:, dc:dc + 1], start=False, stop=(dc == 1)), reads=[qc, nb], writes=psm)
                        for dc in range(2):
                            P.op("pe", lambda e, dc=dc: e.matmul(psm.ap[:, 386 + dc:387 + dc], lhsT=ktl.ap[:, dc * 128:(dc + 1) * 128], rhs=self.ones1[:, 0:1], start=True, stop=True),
                                 reads=[ktl, self.const_b], writes=psm)
                        for dc in range(2):
                            P.op("pe", lambda e, dc=dc: e.matmul(pCs[dc].ap, lhsT=ktl.ap[:, dc * 128:(dc + 1) * 128], rhs=vch.ap, start=True, stop=True), reads=[ktl, vch], writes=pCs[dc])

                    def stE1(psm=psm, pA=pA, pCs=pCs, cf=cf, nf=nf, colv=colv, den=den, sb_=sb_, hst=hst, hfl=hfl, c0=c0, h=h, hfb=hfb):
                        P.op("act", lambda e: e.activation(out=den, in_=psm.ap[:, 384:385], func=AF.Abs), reads=psm, writes=[sb_])
                        P.op("dve", lambda e: e.scalar_tensor_tensor(out=nf.ap, in0=nf.ap, scalar=colv[3], in1=psm.ap[:, 386:388], op0=ALU.mult, op1=ALU.add),
                             reads=[psm, nf, self.tg_b], writes=nf)
                        P.op("dve", lambda e: e.tensor_scalar(out=den, in0=den, scalar1=colv[2], scalar2=None, op0=ALU.max), reads=[sb_, self.tg_b], writes=[sb_])
                        P.op("dve", lambda e: e.reciprocal(out=den, in_=den), reads=[sb_], writes=[sb_])
                        for dc in range(2):
                            cfd = Vw(cf.ap[:, dc, :], cf.bufs)
                            P.op("dve", lambda e, dc=dc, cfd=cfd: e.scalar_tensor_tensor(out=cfd.ap, in0=cfd.ap, scalar=colv[3], in1=pCs[dc].ap, op0=ALU.mult, op1=ALU.add),
                                 reads=[pCs[dc], cfd, self.tg_b], writes=cfd)
                        if not combine:
                            P.op("act", lambda e: e.activation(out=hst.ap, in_=pA.ap, func=AF.Copy, scale=den), reads=[pA, sb_], writes=hst)
                            P.dma("actq", lambda e: e.dma_start(out=self.HF[c0:c0 + 128, h * 512:(h + 1) * 512], in_=hst.ap), reads=hst, writes=[hfb])
                        else:
                            P.op("dve", lambda e: e.scalar_tensor_tensor(out=hst.ap, in0=pA.ap, scalar=den, in1=hfl.ap, op0=ALU.mult, op1=ALU.add), reads=[pA, sb_, hfl], writes=hst)

                    def stE2(cf=cf, cb=cb, nf=nf, nb=nb):
                        P.op("act", lambda e: e.activation(out=nb.ap, in_=nf.ap, func=AF.Copy), reads=nf, writes=nb)
                        for dc in range(2):
                            cfd = Vw(cf.ap[:, dc, :], cf.bufs)
                            cbd = Vw(cb.ap[:, dc, :], cb.bufs)
                            P.op("act", lambda e, cfd=cfd, cbd=cbd: e.activation(out=cbd.ap, in_=cfd.ap, func=AF.Copy), reads=cfd, writes=cbd)

                    def stG(hst=hst, st6=st6, mv=mv, rs=rs, sb_=sb_):
                        P.op("dve", lambda e: e.bn_stats(out=st6, in_=hst.ap), reads=hst, writes=[sb_])
                        P.op("dve", lambda e: e.bn_aggr(out=mv, in_=st6), reads=[sb_], writes=[sb_])
                        P.op("dve", lambda e: e.tensor_scalar(out=rs, in0=mv[:, 1:2], scalar1=LN_EPS, scalar2=None, op0=ALU.add), reads=[sb_], writes=[sb_])

                    def stH(rs=rs, sb_=sb_):
                        P.op("act", lambda e: e.activation(out=rs, in_=rs, func=AF.Sqrt), reads=[sb_], writes=[sb_])

                    def stI(rs=rs, mv=mv, nmr=nmr, sb_=sb_):
                        P.op("dve", lambda e: e.reciprocal(out=rs, in_=rs), reads=[sb_], writes=[sb_])
                        P.op("dve", lambda e: e.scalar_tensor_tensor(out=nmr, in0=mv[:, 0:1], scalar=-1.0, in1=rs, op0=ALU.mult, op1=ALU.mult), reads=[sb_], writes=[sb_])

                    def stJ(hst=hst, rs=rs, nmr=nmr, sb_=sb_, h=h):
                        P.op("act", lambda e: e.activation(out=hst.ap, in_=hst.ap, func=AF.Identity, scale=rs, bias=nmr), reads=[hst, sb_], writes=hst)

                    def stK(hst=hst, hmb=hmb, sgo=sgo):
                        P.op("dve", lambda e: e.tensor_tensor(out=hmb.ap, in0=hst.ap, in1=sgo.ap, op=ALU.mult), reads=[hst, sgo], writes=hmb)

                    def stL(hmb=hmb, psm_b=psm_b):
                        for fc in range(4):
                            P.op("pe", lambda e, fc=fc: e.transpose(out=psm_b.ap[:, fc * 128:(fc + 1) * 128], in_=hmb.ap[:, fc * 128:(fc + 1) * 128], identity=self.idb[:]),
                                 reads=[hmb, self.const_b], writes=psm_b)

                    def stM(hmt=hmt, psm_b=psm_b, c0=c0, h=h):
                        P.op("act", lambda e: e.activation(out=hmt.ap, in_=psm_b.ap[:, 0:512].rearrange("p (a b) -> p a b", a=4), func=AF.Copy), reads=psm_b, writes=hmt)
                        P.dma("actq", lambda e: e.dma_start(out=self.HMT[:, 4 * h:4 * h + 4, c0:c0 + 128], in_=hmt.ap), reads=hmt, writes=[self.hmt_b[c0 // 128]])

                    for k, f in (("p1", p1), ("A", stA), ("E1", stE1), ("E2", stE2), ("G", stG), ("H", stH), ("I", stI), ("J", stJ), ("K", stK), ("L", stL), ("M", stM)):
                        S_[k][ci_] = f
                for ci_ in range(4):
                    S_["p1"][ci_]()
                for pair in ((0, 1), (2, 3)):
                    for ci_ in pair:
                        S_["A"][ci_]()
                    for ci_ in pair:
                        S_["E1"][ci_]()
                for ci_ in range(4):
                    S_["E2"][ci_]()
                for fn in pending:
                    fn()
                pending = []
                if combine:
                    for k in ("G", "H", "I", "J", "K"):
                        for ci_ in range(4):
                            S_[k][ci_]()
                    for pair in ((0, 1), (2, 3)):
                        for ci_ in pair:
                            pending.append(S_["L"][ci_])
                        for ci_ in pair:
                            pending.append(S_["M"][ci_])
        for fn in pending:
            fn()

    def dump_rows(self, name, v, r0, ncol, dt=F32):
        if name not in self.dbg:
            return
        if name not in self.dbg_outs:
            self.dbg_outs[name] = self.nc.dram_tensor("dbg_" + name, [S, ncol], dt, kind="ExternalOutput").ap()
        d = self.dbg_outs[name]
        self.P.dma("sp", lambda e: e.dma_start(out=d[r0:r0 + 128, :], in_=v.ap), reads=v)

    def mem_kv(self, s, l):
        P = self.P
        mt = self.av_(self.O_XF, F32, (2, 1024))
        mT = self.av_(self.O_H, BF16, (8, 256))
        P.dma("sp", lambda e: e.dma_start(out=mt.ap, in_=self.mem_in[s].rearrange("(a p) d -> p a d", p=128)), writes=mt)
        sm = self.sm
        sb_ = self.sm_b[6]
        for a in range(2):
            for hh in range(2):
                P.op("dve", lambda e, a=a, hh=hh: e.bn_stats(out=sm[:, hh * 6:hh * 6 + 6], in_=mt.ap[:, a, hh * 512:(hh + 1) * 512]), reads=mt, writes=[sb_])
            P.op("dve", lambda e: e.bn_aggr(out=sm[:, 12:14], in_=sm[:, 0:12]), reads=[sb_], writes=[sb_])
            P.op("dve", lambda e: e.tensor_scalar(out=sm[:, 14:15], in0=sm[:, 13:14], scalar1=LN_EPS, scalar2=None, op0=ALU.add), reads=[sb_], writes=[sb_])
            P.op("act", lambda e: e.activation(out=sm[:, 14:15], in_=sm[:, 14:15], func=AF.Sqrt), reads=[sb_], writes=[sb_])
            P.op("dve", lambda e: e.reciprocal(out=sm[:, 14:15], in_=sm[:, 14:15]), reads=[sb_], writes=[sb_])
            P.op("dve", lambda e, a=a: e.tensor_scalar(out=mt.ap[:, a, :], in0=mt.ap[:, a, :], scalar1=sm[:, 12:13], scalar2=sm[:, 14:15], op0=ALU.subtract, op1=ALU.mult),
                 reads=[mt, sb_], writes=mt)
            for k in range(8):
                pv = self.ps(k % 6)
                P.op("pe", lambda e, pv=pv, a=a, k=k: e.transpose(out=pv.ap[:, 0:128], in_=mt.ap[:, a, k * 128:(k + 1) * 128], identity=self.idf[:]), reads=[mt, self.const_b], writes=pv)
                g = self.pc("mem_g", l, k)
                b = self.pc("mem_b", l, k)
                P.op("act", lambda e, pv=pv, a=a, k=k, g=g, b=b: e.activation(out=mT.ap[:, k, a * 128:(a + 1) * 128], in_=pv.ap[:, 0:128], func=AF.Identity, scale=g, bias=b),
                     reads=[pv, self.prm_b], writes=mT)
        mk = lambda k: Vw(mT.ap[:, k, :], mT.bufs)

        def ev_k(m, pv):
            P.op("act", lambda e: e.activation(out=self.akT[:, m, :], in_=pv.ap[:, 0:256], func=AF.Copy, scale=1.0 / 16.0), reads=pv, writes=[self.akv_b])
        self._proj_generic("xa_wkv", l, 0, 8, 8, mk, ev_k, n=256)
        for cg in range(2):
            wv = self.load_w("xa_wkv", l, 0, 8, D + cg * 512, 512)
            for a in range(2):
                pv = self.ps((cg * 2 + a) % 6)
                for k in range(8):
                    P.op("pe", lambda e, pv=pv, wv=wv, k=k, a=a: e.matmul(pv.ap, lhsT=mT.ap[:, k, a * 128:(a + 1) * 128], rhs=wv.ap[:, k, :], start=(k == 0), stop=(k == 7)),
                         reads=[wv, mT], writes=pv)
                P.op("act", lambda e, pv=pv, a=a, cg=cg: e.activation(out=self.av[:, a, cg * 512:(cg + 1) * 512], in_=pv.ap, func=AF.Copy), reads=pv, writes=[self.akv_b])

    def _proj_generic(self, key, l, c0, nchunks, kt, rhs_of, evac, n=512, r0=0):
        P = self.P
        per = max(1, min(self.WSLOT // (kt * 128), nchunks))
        m = 0
        pi = getattr(self, "_pi", 0)
        while m < nchunks:
            n_here = min(per, nchunks - m)
            wv = self.load_w(key, l, r0, kt, c0 + m * 128, n_here * 128)
            for j in range(n_here):
                pv = self.ps(pi % 6)
                pi += 1
                for k in range(kt):
                    rk = rhs_of(k)
                    P.op("pe", lambda e, pv=pv, wv=wv, rk=rk, k=k, j=j: e.matmul(pv.ap[:, 0:n], lhsT=wv.ap[:, k, j * 128:(j + 1) * 128], rhs=rk.ap,
                                                                                 start=(k == 0), stop=(k == kt - 1)), reads=[wv, rk], writes=pv)
                evac(m + j, pv)
            m += n_here
        self._pi = pi

    def phase_c(self, s, l, last):
        P = self.P
        self.mem_kv(s, l)
        for tt in range(NT):
            t0 = tt * T
            xf = self.av_(self.O_XF, F32, (8, 512))
            P.dma("sp", lambda e, xf=xf, t0=t0: e.dma_start(out=xf.ap, in_=self.X1[:, :, t0:t0 + T]), reads=[self.x1_b[tt]], writes=xf)
            hm = self.av_(self.O_H, BF16, (16, 512))
            hr = self.av_(self.O_ST, BF16, (8, 512))
            P.dma("sp", lambda e, hm=hm, t0=t0: e.dma_start(out=hm.ap, in_=self.HMT[:, :, t0:t0 + T]), reads=self.hmt_b[4 * tt:4 * tt + 4], writes=hm)
            P.dma("sp", lambda e, hr=hr, t0=t0: e.dma_start(out=hr.ap, in_=self.HRT[:, :, t0:t0 + T]), reads=[self.dbuf["HRT"]], writes=hr)
            x1bv = self.av_(self.O_X1B, BF16, (8, 512))
            P.dma("sp", lambda e, x1bv=x1bv, t0=t0: e.dma_start(out=x1bv.ap, in_=self.X1B[:, :, t0:t0 + T]), reads=[self.x1_b[tt]], writes=x1bv)
            x1k = lambda k: self.avk(self.O_X1B, BF16, (8, 512), k)

            def ev_yr(m, pv):
                tmp = self.av_(self.O_TMP if m % 2 else self.O_TMP2, F32, (512,))
                b = self.pc("b_yr", l, m)
                hk = self.avk(self.O_ST, BF16, (8, 512), m)
                P.op("act", lambda e: e.activation(out=tmp.ap, in_=pv.ap, func=AF.Gelu_apprx_tanh, bias=b), reads=[pv, self.prm_b], writes=tmp)
                P.op("dve", lambda e: e.tensor_tensor(out=hk.ap, in0=hk.ap, in1=tmp.ap, op=ALU.mult), reads=[hk, tmp], writes=hk)
            self.proj_fm("w_in", l, OFF_YR, 8, 8, x1k, ev_yr)

            def ev_mg(m, pv):
                gk = self.avk(self.O_G, BF16, (16, 512), m)
                b = self.pc("b_mg", l, m)
                P.op("act", lambda e: e.activation(out=gk.ap, in_=pv.ap, func=AF.Sigmoid, bias=b), reads=[pv, self.prm_b], writes=gk)
            self.proj_fm("w_in", l, OFF_MG, 16, 8, x1k, ev_mg)
            hmk = lambda k: self.avk(self.O_H, BF16, (16, 512), k)
            hrk = lambda k: self.avk(self.O_ST, BF16, (8, 512), k)
            mtmp = {}

            def ev_pm(m, pv):
                tmp = self.avk(self.O_ZB, F32, (8, 512), m)
                gk = self.avk(self.O_G, BF16, (16, 512), m)
                P.op("dve", lambda e: e.tensor_tensor(out=tmp.ap, in0=pv.ap, in1=gk.ap, op=ALU.mult), reads=[pv, gk], writes=tmp)

            def ev_pr(m, pv):
                tmp = self.avk(self.O_ZB, F32, (8, 512), m)
                gk = self.avk(self.O_G, BF16, (16, 512), 8 + m)
                t2 = self.av_(self.O_TMP if m % 2 else self.O_TMP2, F32, (512,))
                mk_ = self.avk(self.O_Q, BF16, (8, 512), m)
                P.op("dve", lambda e: e.tensor_tensor(out=t2.ap, in0=pv.ap, in1=gk.ap, op=ALU.mult), reads=[pv, gk], writes=t2)
                P.op("dve", lambda e: e.tensor_tensor(out=mk_.ap, in0=tmp.ap, in1=t2.ap, op=ALU.add), reads=[tmp, t2], writes=mk_)
            self.proj_fm("w_pm", l, 0, 8, 16, hmk, ev_pm)
            self.proj_fm("w_pr", l, 0, 8, 8, hrk, ev_pr)
            if "merged" in self.dbg and tt == 0:
                self.dump("merged", self.av_(self.O_Q, BF16, (8, 512)).ap, (128, 8, 512), BF16, self.av_(self.O_Q, BF16, (8, 512)))
            mgk = lambda k: self.avk(self.O_Q, BF16, (8, 512), k)

            def ev_res(c):
                def ev(m, pv):
                    xk = self.avk(self.O_XF, F32, (8, 512), m)
                    P.op("dve", lambda e: e.scalar_tensor_tensor(out=xk.ap, in0=pv.ap, scalar=c, in1=xk.ap, op0=ALU.mult, op1=ALU.add), reads=[pv, xk], writes=xk)
                return ev
            self.proj_fm("w_out", l, 0, 8, 8, mgk, ev_res(C_MIX))
            self.layer_norm_fm(l, 1)
            if "x2" in self.dbg and tt == 0:
                self.dump("x2", xf.ap, (128, 8, 512), F32, xf)
            if self.stop == "C1":
                continue
            xbk = lambda k: self.avk(self.O_XB, BF16, (8, 512), k)

            def ev_q(m, pv):
                qk_ = self.avk(self.O_Q, BF16, (8, 512), m)
                P.op("act", lambda e: e.activation(out=qk_.ap, in_=pv.ap, func=AF.Copy), reads=pv, writes=qk_)
            self.proj_fm("xa_wq", l, 0, 8, 8, xbk, ev_q)
            ao = lambda m: self.avk(self.O_X1B, BF16, (8, 512), m)
            for h in range(4):
                eT = self.av_(self.O_E, BF16, (2, 512))
                for mc in range(2):
                    pv = self.ps((h * 2 + mc) % 4)
                    for dc in range(2):
                        qv = self.avk(self.O_Q, BF16, (8, 512), h * 2 + dc)
                        P.op("pe", lambda e, pv=pv, qv=qv, h=h, dc=dc, mc=mc: e.matmul(pv.ap, lhsT=self.akT[:, h * 2 + dc, mc * 128:(mc + 1) * 128], rhs=qv.ap, start=(dc == 0), stop=(dc == 1)),
                             reads=[qv, self.akv_b], writes=pv)
                    P.op("act", lambda e, pv=pv, eT=eT, mc=mc: e.activation(out=eT.ap[:, mc, :], in_=pv.ap, func=AF.Exp), reads=pv, writes=eT)
                pd = self.ps(4)
                for mc in range(2):
                    P.op("pe", lambda e, eT=eT, mc=mc: e.matmul(pd.ap, lhsT=self.ones1[:], rhs=eT.ap[:, mc, :], start=(mc == 0), stop=(mc == 1)), reads=[eT, self.const_b], writes=pd)
                rden = self.av_(self.O_E + 2048, F32, (512,))
                P.op("dve", lambda e, rden=rden: e.reciprocal(out=rden.ap, in_=pd.ap), reads=pd, writes=rden)
                for dc in range(2):
                    po = self.ps(5 + dc)
                    for mc in range(2):
                        P.op("pe", lambda e, po=po, eT=eT, mc=mc, h=h, dc=dc: e.matmul(po.ap, lhsT=self.av[:, mc, h * 256 + dc * 128:h * 256 + (dc + 1) * 128], rhs=eT.ap[:, mc, :],
                                                                                        start=(mc == 0), stop=(mc == 1)), reads=[eT, self.akv_b], writes=po)
                    aok = ao(h * 2 + dc)
                    P.op("dve", lambda e, po=po, aok=aok, rden=rden: e.tensor_tensor(out=aok.ap, in0=po.ap, in1=rden.ap, op=ALU.mult), reads=[po, rden], writes=aok)
            self.proj_fm("xa_wo", l, 0, 8, 8, ao, ev_res(C_MIX))
            self.layer_norm_fm(l, 2)
            if "x3" in self.dbg and tt == 0:
                self.dump("x3", xf.ap, (128, 8, 512), F32, xf)
            self.ffn_fm(l, "ff2")
            self.layer_norm_fm(l, 3)
            if not last:
                P.dma("actq", lambda e, xf=xf, t0=t0: e.dma_start(out=self.XN[:, :, t0:t0 + T], in_=xf.ap), reads=xf, writes=[self.xn_b[tt]])
                xbv2 = self.av_(self.O_XB, BF16, (8, 512))
                P.dma("actq", lambda e, xbv2=xbv2, t0=t0: e.dma_start(out=self.XNB[:, :, t0:t0 + T], in_=xbv2.ap), reads=xbv2, writes=[self.xn_b[tt]])
            else:
                yt = self.av_(self.O_G, F32, (4, 1024))
                for a in range(4):
                    for kk in range(2):
                        pv = self.ps((a * 2 + kk) % 6)
                        for k4 in range(4):
                            k = kk * 4 + k4
                            P.op("pe", lambda e, pv=pv, xf=xf, a=a, k=k, k4=k4: e.transpose(out=pv.ap[:, k4 * 128:(k4 + 1) * 128], in_=xf.ap[:, k, a * 128:(a + 1) * 128], identity=self.idf[:]),
                                 reads=[xf, self.const_b], writes=pv)
                        P.op("act" if kk else "dve", (lambda e, pv=pv, yt=yt, a=a, kk=kk: e.activation(out=yt.ap[:, a, kk * 512:(kk + 1) * 512], in_=pv.ap, func=AF.Copy)) if kk else
                             (lambda e, pv=pv, yt=yt, a=a, kk=kk: e.tensor_copy(out=yt.ap[:, a, kk * 512:(kk + 1) * 512], in_=pv.ap)), reads=pv, writes=yt)
                P.dma("actq", lambda e, yt=yt, t0=t0: e.dma_start(out=self.y_out[s, t0:t0 + T, :].rearrange("(a p) d -> p a d", p=128), in_=yt.ap), reads=yt)

    def build(self):
        self.setup()
        stop = self.stop
        if stop == "S":
            self.P.final_wait_all("sp")
            self.P.emit()
            return self.nc
        for s in range(self.nseq):
            for l in range(self.nlayers):
                self.layer_setup(l)
                self.phase_a(s, l)
                if stop in ("A1", "A"):
                    break
                self.phase_b(s, l)
                if stop in ("B0", "B1", "B2g", "B"):
                    break
                self.phase_c(s, l, last=(l == self.nlayers - 1))
                if stop in ("C1", "C"):
                    break
            if stop:
                break
        for name, ap, shape, dt in (("X1", self.X1, (128, 8, S), F32), ("XN", self.XN, (128, 8, S), F32), ("QKP", self.QKP, (128, 16, S + 4), BF16),
                                    ("XRP", self.XRP, (128, 8, S + 4), BF16), ("QKC", self.QKC, (128, 16, S), BF16), ("VS", self.VS, (S, 2048), BF16),
                                    ("SIGO", self.SIGO, (S, 2048), BF16), ("HF", self.HF, (S, 2048), F32), ("HMT", self.HMT, (128, 16, S), BF16),
                                    ("HRT", self.HRT, (128, 8, S), BF16), ("GSC", self.GSC, (2, 64, S), F32)):
            bufs = [self.dbuf[name]] + (self.x1_b if name == "X1" else []) + (self.xn_b if name == "XN" else []) + (self.hmt_b if name == "HMT" else [])
            self.dump(name, ap, shape, dt, bufs)
        if "TG" in self.dbg:
            self.dump("TG", self.tg[:], (128, 32, 32), F32, [self.tg_b])
        self.P.final_wait_all("sp")
        self.P.emit()
        return self.nc


_W_KEYS = ("w_in", "b_in", "w_conv_qk", "b_conv_qk", "w_conv_r", "b_conv_r", "mh_gain", "lru_wa", "lru_ba", "lru_wx",
           "lru_bx", "lru_lam", "w_pm", "w_pr", "w_out", "xa_wq", "xa_wkv", "xa_wo", "mem_ln_g", "mem_ln_b",
           "ff1_in", "ff1_out", "ff2_in", "ff2_out", "ln_g", "ln_b")


def kernel(x_prompt, x_sample, mem_prompt, mem_sample, **weights):
    ncores = 8
    xs = np.concatenate([np.asarray(x_prompt, np.float32), np.asarray(x_sample, np.float32)], axis=0)
    ms = np.concatenate([np.asarray(mem_prompt, np.float32), np.asarray(mem_sample, np.float32)], axis=0)
    nseq = xs.shape[0] // ncores
    b = Builder(nseq=nseq)
    nc = b.build()
    wts = {k: np.ascontiguousarray(np.asarray(weights[k], np.float32)) for k in _W_KEYS}
    in_maps = []
    for c in range(ncores):
        m = {"x": np.ascontiguousarray(xs[c * nseq:(c + 1) * nseq]), "mem": np.ascontiguousarray(ms[c * nseq:(c + 1) * nseq])}
        m.update(wts)
        in_maps.append(m)
    res = run_bass_kernel_spmd(nc, in_maps, core_ids=list(range(ncores)))
    y = np.concatenate([np.asarray(r["y"], np.float32) for r in res.results], axis=0)
    nb = np.asarray(x_prompt).shape[0]
    return (np.ascontiguousarray(y[:nb]), np.ascontiguousarray(y[nb:]))
```

```python
import math
import os
import numpy as np
from contextlib import ExitStack
import concourse.bass as bass
import concourse.mybir as mybir
from concourse.bass_utils import run_bass_kernel_spmd

F32 = mybir.dt.float32
BF16 = mybir.dt.bfloat16
AF = mybir.ActivationFunctionType
ALU = mybir.AluOpType

D = 1024
S = 4096
T = 512
NT = S // T
DFF = 2816
DIN = 10256
OFF_V, OFF_O, OFF_GATE, OFF_XR, OFF_YR, OFF_MG = 2048, 4096, 6144, 6160, 7184, 8208
NMEM = 256
ALPHA = 4.0 ** 0.25
LN_EPS = 1e-5
EPS_S = LN_EPS / (ALPHA * ALPHA)
C_FF = 0.5 / ALPHA
C_MIX = 1.0 / ALPHA
LN16 = math.log(16.0)


class Buf:
    __slots__ = ("name", "lw", "rd", "excl")

    def __init__(self, name="", excl=False):
        self.name = name
        self.lw = None
        self.rd = {}
        self.excl = excl


def _flat(xs):
    out = []
    for x in xs:
        if x is None:
            continue
        if isinstance(x, Buf):
            out.append(x)
        elif isinstance(x, (list, tuple)):
            out.extend(_flat(x))
        else:
            out.extend(x.bufs)
    return out


class Prog:
    ENG = ("pe", "act", "dve", "pool")
    DMAQ = ("sp", "actq", "poolq")

    def __init__(self, nc, n_dma_sems=16):
        self.nc = nc
        self.es = ExitStack()
        self.ops = {e: [] for e in ("pe", "act", "dve", "pool", "sp")}
        self.sem = {}
        self.cnt = {}
        for e in self.ENG:
            self.sem[e] = self.es.enter_context(nc.semaphore("s_" + e))
            self.cnt[e] = 0
        self.dsem = {}
        self.dcnt = {}
        self.dnext = {}
        for q in self.DMAQ:
            self.dsem[q] = [self.es.enter_context(nc.semaphore(f"d_{q}_{i}")) for i in range(n_dma_sems)]
            self.dcnt[q] = [0] * n_dma_sems
            self.dnext[q] = 0
        self.waited = {}
        self.nops = 0
        self.nwaits = 0

    @staticmethod
    def _issuer(stream):
        return {"sp": "sp", "actq": "act", "poolq": "pool"}.get(stream, stream)

    def _need(self, issuer, waits, dep):
        if dep is None:
            return
        key, val = dep
        if issuer == "pe" and key == ("e", "pe"):
            return
        if self.waited.get((issuer, key), 0) >= val:
            return
        if waits.get(key, 0) < val:
            waits[key] = val

    def _deps(self, issuer, waits, reads, writes):
        for b in reads:
            self._need(issuer, waits, b.lw)
        for b in writes:
            self._need(issuer, waits, b.lw)
            for v in b.rd.values():
                self._need(issuer, waits, v)

    def op(self, eng, fn, reads=(), writes=()):
        reads = _flat(reads)
        writes = _flat(writes)
        ex = [b for b in reads if b.excl]
        if ex:
            reads = [b for b in reads if not b.excl]
            writes = writes + ex
        waits = {}
        self._deps(eng, waits, reads, writes)
        for key, val in waits.items():
            self.waited[(eng, key)] = val
        self.cnt[eng] += 1
        key = ("e", eng)
        me = (key, self.cnt[eng])
        self.ops[eng].append((fn, list(waits.items()), key, 1))
        for b in reads:
            b.rd[key] = me
        for b in writes:
            b.lw = me
            b.rd = {}
        self.nops += 1
        self.nwaits += len(waits)

    def dma(self, q, fn, reads=(), writes=()):
        reads = _flat(reads)
        writes = _flat(writes)
        issuer = self._issuer(q)
        waits = {}
        i = self.dnext[q]
        self.dnext[q] = (i + 1) % len(self.dsem[q])
        key = ("d", q, i)
        if self.dcnt[q][i] > 0:
            self._need(issuer, waits, (key, self.dcnt[q][i]))
        self._deps(issuer, waits, reads, writes)
        for k2, val in waits.items():
            self.waited[(issuer, k2)] = val
        self.dcnt[q][i] += 16
        me = (key, self.dcnt[q][i])
        self.ops[issuer].append((fn, list(waits.items()), key, 16))
        for b in reads:
            b.rd[key] = me
        for b in writes:
            b.lw = me
            b.rd = {}
        self.nops += 1
        self.nwaits += len(waits)

    def _semof(self, key):
        if key[0] == "e":
            return self.sem[key[1]]
        return self.dsem[key[1]][key[2]]

    def final_wait_all(self, eng="sp"):
        waits = []
        for e in self.ENG:
            if self.cnt[e]:
                waits.append((("e", e), self.cnt[e]))
        for q in self.DMAQ:
            for i, c in enumerate(self.dcnt[q]):
                if c:
                    waits.append((("d", q, i), c))
        self.ops[eng].append((None, waits, None, 0))

    def emit(self):
        nc = self.nc
        engobj = {"pe": "tensor", "act": "scalar", "dve": "vector", "pool": "gpsimd", "sp": "sync"}
        with nc.allow_non_contiguous_dma(reason="small strided parameter / halo transfers"), nc.Block() as block:
            for e, attr in engobj.items():
                lst = self.ops[e]
                if not lst:
                    continue

                def body(eng, lst=lst):
                    for fn, waits, key, inc in lst:
                        for k2, val in waits:
                            eng.wait_ge(self._semof(k2), val)
                        if fn is not None:
                            fn(eng).then_inc(self._semof(key), inc)

                getattr(block, attr)(body)


class Vw:
    __slots__ = ("ap", "bufs")

    def __init__(self, ap, bufs):
        self.ap = ap
        self.bufs = bufs if isinstance(bufs, list) else [bufs]

    def __getitem__(self, idx):
        return Vw(self.ap[idx], self.bufs)


class Builder:
    def __init__(self, nseq=3, nlayers=2, stop=None, dbg=()):
        self.nseq = nseq
        self.nlayers = nlayers
        self.stop = stop
        self.dbg = set(dbg)
        nc = self.nc = bass.Bass("TRN2", target_bir_lowering=False)
        self.P = Prog(nc)
        self.es = self.P.es
        self.dbg_outs = {}
        self._decl_dram()
        self._decl_sbuf()

    def _decl_dram(self):
        nc = self.nc
        n = self.nseq
        L = 2
        di = lambda name, shape: nc.dram_tensor(name, list(shape), F32, kind="ExternalInput").ap()
        self.x_in = di("x", (n, S, D))
        self.mem_in = di("mem", (n, NMEM, D))
        self.win = {}
        shapes = dict(w_in=(L, D, DIN), b_in=(L, DIN), w_conv_qk=(L, 4, 2048), b_conv_qk=(L, 2048),
                      w_conv_r=(L, 4, 1024), b_conv_r=(L, 1024), mh_gain=(L, 2048),
                      lru_wa=(L, 2, 8, 128, 128), lru_ba=(L, 2, 1024), lru_wx=(L, 2, 8, 128, 128),
                      lru_bx=(L, 2, 1024), lru_lam=(L, 2, 1024), w_pm=(L, 2048, D), w_pr=(L, D, D),
                      w_out=(L, D, D), xa_wq=(L, D, D), xa_wkv=(L, D, 2 * D), xa_wo=(L, D, D),
                      mem_ln_g=(L, D), mem_ln_b=(L, D), ff1_in=(L, D, 2 * DFF), ff1_out=(L, DFF, D),
                      ff2_in=(L, D, 2 * DFF), ff2_out=(L, DFF, D), ln_g=(L, 4, D), ln_b=(L, 4, D))
        for k, shp in shapes.items():
            self.win[k] = di(k, shp)
        self.y_out = nc.dram_tensor("y", [n, S, D], F32, kind="ExternalOutput").ap()
        self.wb = {}
        self.wb_buf = {}
        for k in ("w_in", "w_pm", "w_pr", "w_out", "xa_wq", "xa_wkv", "xa_wo", "ff1_in", "ff1_out", "ff2_in", "ff2_out"):
            shp = shapes[k]
            self.wb[k] = nc.dram_tensor("wb_" + k, list(shp), BF16, kind="Internal").ap()
            self.wb_buf[k] = []
        for k in ("lru_wa", "lru_wx"):
            self.wb[k] = nc.dram_tensor("wb_" + k, [L, 2, 8, 128, 128], BF16, kind="Internal").ap()
            self.wb_buf[k] = []
        sc = lambda name, shape, dt: nc.dram_tensor(name, list(shape), dt, kind="Internal").ap()
        self.X1 = sc("X1", (128, 8, S), F32)
        self.XN = sc("XN", (128, 8, S), F32)
        self.X1B = sc("X1B", (128, 8, S), BF16)
        self.XNB = sc("XNB", (128, 8, S), BF16)
        self.QKP = sc("QKP", (128, 16, S + 4), BF16)
        self.XRP = sc("XRP", (128, 8, S + 4), BF16)
        self.QKC = sc("QKC", (128, 16, S), BF16)
        self.VS = sc("VS", (S, 2048), BF16)
        self.SIGO = sc("SIGO", (S, 2048), BF16)
        self.HF = sc("HF", (S, 2048), F32)
        self.HMT = sc("HMT", (128, 16, S), BF16)
        self.HRT = sc("HRT", (128, 8, S), BF16)
        self.GSC = sc("GSC", (2, 64, S), F32)
        self.dbuf = {k: Buf(k) for k in ("X1", "XN", "QKP", "XRP", "QKC", "VS", "SIGO", "HF", "HMT", "HRT", "GSC")}
        self.x1_b = [Buf(f"X1_{i}") for i in range(NT)]
        self.hmt_b = [Buf(f"HMT_{i}") for i in range(32)]
        self.xn_b = [Buf(f"XN_{i}") for i in range(NT)]

    def dump(self, name, src_ap, shape, dt, bufs, q="sp"):
        if name not in self.dbg:
            return
        d = self.nc.dram_tensor("dbg_" + name, list(shape), dt, kind="ExternalOutput").ap()
        self.dbg_outs[name] = d
        self.P.dma(q, lambda e: e.dma_start(out=d, in_=src_ap), reads=bufs)

    def _decl_sbuf(self):
        nc = self.nc
        es = self.es
        sb = lambda name, shape, dt=F32: es.enter_context(nc.sbuf_tensor(name, list(shape), dt))
        self.AR_BYTES = 116736 + 16384
        self.arena = sb("arena", (128, self.AR_BYTES // 4))
        self.abuf = [Buf(f"ar{i}") for i in range(self.AR_BYTES // 2048)]
        self.NW = 3
        self.WSLOT = 8192
        self.wpool = sb("wpool", (128, self.NW * self.WSLOT), BF16)
        self.wbufs = [Buf(f"w{i}") for i in range(self.NW)]
        self.wnext = 0
        self.prm = sb("prm", (128, 2 * 320))
        self.prm_b = Buf("prm")
        self.prm_off = {}
        self.idb = sb("idb", (128, 128), BF16)
        self.idf = sb("idf", (128, 128))
        self.onesf = sb("onesf", (128, 128))
        self.onesm = sb("onesm", (128, 128), BF16)
        self.ones1 = sb("ones1", (128, 128), BF16)
        self.maskf = sb("maskf", (128, 128), BF16)
        self.maskb = sb("maskb", (128, 128), BF16)
        self.const_b = Buf("const")
        self.stg = sb("stg", (128, 128))
        self.stg_b = Buf("stg")
        self.gwi = sb("gwi", (128, 8, 64), BF16)
        self.gwf = sb("gwf", (128, 8, 64), BF16)
        self.gw_b = Buf("gw")
        self.gbias = sb("gbias", (64, 2))
        self.gb_b = Buf("gb")
        self.tg = sb("tg", (128, 32, 32))
        self.tg_b = Buf("tg")
        self.rr = sb("rr", (64, 2, 32))
        self.rr_b = Buf("rr")
        self.sm = sb("sm", (128, 128))
        self.sm_b = [Buf(f"sm{i}") for i in range(8)]
        self.dg = sb("dg", (128, 4, 128), BF16)
        self.dg_b = Buf("dg")
        self.lw = sb("lw", (128, 4, 128), BF16)
        self.lw_b = Buf("lw")
        self.lsc = sb("lsc", (128, 3, 16))
        self.lsc_b = Buf("lsc")
        self.lt = sb("lt", (128, 3, 16))
        self.lt_b = Buf("lt")
        self.akT = sb("akT", (128, 8, 256), BF16)
        self.av = sb("av", (128, 2, 1024), BF16)
        self.akv_b = Buf("akv")
        self.psum = [es.enter_context(nc.psum_tensor(f"ps{i}", [128, 512], F32)) for i in range(8)]
        self.ps_b = [Buf(f"ps{i}", excl=True) for i in range(8)]

    def av_(self, off, dt, shape):
        esz = 2 if dt == BF16 else 4
        nel = int(np.prod(shape))
        nb = nel * esz
        assert off % 4 == 0 and nb % 4 == 0 and off + nb <= self.AR_BYTES, (off, nb)
        ap = self.arena[:, off // 4:(off + nb) // 4]
        if dt == BF16:
            ap = ap.bitcast(BF16)
        if len(shape) == 2:
            ap = ap.rearrange("p (a b) -> p a b", a=shape[0])
        elif len(shape) == 3:
            ap = ap.rearrange("p (a b c) -> p a b c", a=shape[0], b=shape[1])
        bufs = self.abuf[off // 2048:(off + nb + 2047) // 2048]
        return Vw(ap, list(bufs))

    def avk(self, off, dt, shape, k):
        esz = 2 if dt == BF16 else 4
        inner = int(np.prod(shape[1:]))
        v = self.av_(off + k * inner * esz, dt, shape[1:])
        return v

    def ps(self, i, dt=F32):
        ap = self.psum[i][:]
        if dt == BF16:
            ap = ap.bitcast(BF16)
        return Vw(ap, [self.ps_b[i]])

    def wslot(self):
        i = self.wnext
        self.wnext = (i + 1) % self.NW
        return i

    def wview(self, i, shape):
        nel = int(np.prod(shape))
        assert nel <= self.WSLOT
        ap = self.wpool[:, i * self.WSLOT:i * self.WSLOT + nel]
        if len(shape) == 2:
            ap = ap.rearrange("p (a b) -> p a b", a=shape[0])
        elif len(shape) == 3:
            ap = ap.rearrange("p (a b c) -> p a b c", a=shape[0], b=shape[1])
        return Vw(ap, [self.wbufs[i]])

    def load_w(self, key, l, r0, kt, c0, nc_, q="sp"):
        i = self.wslot()
        v = self.wview(i, (kt, nc_))
        src = self.wb[key][l, r0:r0 + kt * 128, c0:c0 + nc_].rearrange("(k p) n -> p k n", p=128)
        self.P.dma(q, lambda e: e.dma_start(out=v.ap, in_=src), reads=[self.wb_buf[key]], writes=v)
        return v

    def setup(self):
        P = self.P
        cb = self.const_b
        P.op("pool", lambda e: e.memset(self.onesf[:], 1.0), writes=[cb])
        P.op("pool", lambda e: e.memset(self.ones1[:], 1.0), writes=[cb])
        P.op("pool", lambda e: e.memset(self.onesm[:], 1.0 / 1024.0), writes=[cb])
        sel = lambda out, op, cm, pat: (lambda e: e.affine_select(out=out[:], in_=self.onesf[:], pattern=[[pat, 128]],
                                                                    compare_op=op, fill=0.0, base=0, channel_multiplier=cm))
        P.op("pool", sel(self.idf, ALU.is_equal, -1, 1), reads=[cb], writes=[cb])
        P.op("pool", sel(self.idb, ALU.is_equal, -1, 1), reads=[cb], writes=[cb])
        P.op("pool", sel(self.maskf, ALU.is_ge, -1, 1), reads=[cb], writes=[cb])
        P.op("pool", sel(self.maskb, ALU.is_ge, 1, -1), reads=[cb], writes=[cb])
        z = self.av_(0, BF16, (16, 4))
        P.op("pool", lambda e: e.memset(z.ap, 0.0), writes=z)
        for dst, nj, bk in ((self.QKP, 16, "QKP"), (self.XRP, 8, "XRP")):
            P.dma("sp", lambda e, dst=dst, nj=nj: e.dma_start(out=dst[:, :, 0:1], in_=z.ap[:, 0:nj, 0:1]), reads=z, writes=[self.dbuf[bk]])
            P.dma("sp", lambda e, dst=dst, nj=nj: e.dma_start(out=dst[:, :, S + 1:S + 4], in_=z.ap[:, 0:nj, 0:3]), reads=z, writes=[self.dbuf[bk]])
        for key in self.wb:
            src = self.win[key]
            dst = self.wb[key]
            if key in ("lru_wa", "lru_wx"):
                s2 = src.rearrange("l d n i j -> (l d n i) j")
                d2 = dst.rearrange("l d n i j -> (l d n i) j")
            else:
                s2 = src.rearrange("l r c -> (l r) c")
                d2 = dst.rearrange("l r c -> (l r) c")
            R, C = s2.shape
            for r0 in range(0, R, 512):
                r1 = min(R, r0 + 512)
                for c0 in range(0, C, 2048):
                    c1 = min(C, c0 + 2048)
                    bb = Buf("wbc")
                    self.wb_buf[key].append(bb)
                    P.dma("poolq", lambda e, s2=s2, d2=d2, r0=r0, r1=r1, c0=c0, c1=c1:
                          e.dma_start(out=d2[r0:r1, c0:c1], in_=s2[r0:r1, c0:c1]), writes=[bb])
        segs = []
        for l in range(self.nlayers):
            w = self.win
            segs += [(("ln_g", l), w["ln_g"][l].rearrange("g (k p) -> (g k) p", p=128)),
                     (("ln_b", l), w["ln_b"][l].rearrange("g (k p) -> (g k) p", p=128)),
                     (("b_qk", l), w["b_in"][l, 0:2048].rearrange("(k p) -> k p", p=128)),
                     (("b_xr", l), w["b_in"][l, OFF_XR:OFF_YR].rearrange("(k p) -> k p", p=128)),
                     (("b_yr", l), w["b_in"][l, OFF_YR:OFF_MG].rearrange("(k p) -> k p", p=128)),
                     (("b_mg", l), w["b_in"][l, OFF_MG:DIN].rearrange("(k p) -> k p", p=128)),
                     (("wc_qk", l), w["w_conv_qk"][l].rearrange("t (k p) -> (t k) p", p=128)),
                     (("bc_qk", l), w["b_conv_qk"][l].rearrange("(k p) -> k p", p=128)),
                     (("wc_r", l), w["w_conv_r"][l].rearrange("t (k p) -> (t k) p", p=128)),
                     (("bc_r", l), w["b_conv_r"][l].rearrange("(k p) -> k p", p=128)),
                     (("lru_ba", l), w["lru_ba"][l].rearrange("d (k p) -> (d k) p", p=128)),
                     (("lru_bx", l), w["lru_bx"][l].rearrange("d (k p) -> (d k) p", p=128)),
                     (("lru_lam", l), w["lru_lam"][l].rearrange("d (k p) -> (d k) p", p=128)),
                     (("mem_g", l), w["mem_ln_g"][l].rearrange("(k p) -> k p", p=128)),
                     (("mem_b", l), w["mem_ln_b"][l].rearrange("(k p) -> k p", p=128))]
        off = 0
        for i, (key, src) in enumerate(segs):
            J = src.shape[0]
            self.prm_off[key] = off
            pi = i % 2
            P.dma("sp", lambda e, src=src, J=J: e.dma_start(out=self.stg[0:J, :], in_=src), writes=[self.stg_b])
            pv = self.ps(pi)
            P.op("pe", lambda e, J=J, pv=pv: e.transpose(out=pv.ap[:, 0:J], in_=self.stg[0:J, :], identity=self.idf[0:J, 0:J]),
                 reads=[self.stg_b, cb], writes=pv)
            P.op("dve", lambda e, J=J, pv=pv, off=off: e.tensor_copy(out=self.prm[:, off:off + J], in_=pv.ap[:, 0:J]),
                 reads=pv, writes=[self.prm_b])
            off += J
        assert off <= 640, off

    def pc(self, key, l, j=0, n=1):
        o = self.prm_off[(key, l)] + j
        return self.prm[:, o:o + n]

    def layer_setup(self, l):
        P = self.P
        w = self.win
        lam = self.pc("lru_lam", l, 0, 16)
        lt = self.lt
        P.op("act", lambda e: e.activation(out=lt[:, 0, :], in_=lam, func=AF.Exp, scale=-1.0), reads=[self.prm_b], writes=[self.lt_b])
        P.op("dve", lambda e: e.tensor_scalar(out=lt[:, 1, :], in0=lt[:, 0, :], scalar1=-0.25, scalar2=1.0 / 3.0, op0=ALU.mult, op1=ALU.add), reads=[self.lt_b], writes=[self.lt_b])
        P.op("dve", lambda e: e.tensor_tensor(out=lt[:, 1, :], in0=lt[:, 1, :], in1=lt[:, 0, :], op=ALU.mult), reads=[self.lt_b], writes=[self.lt_b])
        P.op("dve", lambda e: e.tensor_scalar(out=lt[:, 1, :], in0=lt[:, 1, :], scalar1=-0.5, scalar2=None, op0=ALU.add), reads=[self.lt_b], writes=[self.lt_b])
        P.op("dve", lambda e: e.tensor_tensor(out=lt[:, 1, :], in0=lt[:, 1, :], in1=lt[:, 0, :], op=ALU.mult), reads=[self.lt_b], writes=[self.lt_b])
        P.op("dve", lambda e: e.tensor_scalar(out=lt[:, 1, :], in0=lt[:, 1, :], scalar1=1.0, scalar2=None, op0=ALU.add), reads=[self.lt_b], writes=[self.lt_b])
        P.op("dve", lambda e: e.tensor_tensor(out=lt[:, 2, :], in0=lt[:, 1, :], in1=lt[:, 0, :], op=ALU.mult), reads=[self.lt_b], writes=[self.lt_b])
        P.op("dve", lambda e: e.tensor_scalar(out=self.lsc[:, 0, :], in0=lt[:, 2, :], scalar1=-8.0, scalar2=None, op0=ALU.mult), reads=[self.lt_b], writes=[self.lsc_b])
        P.op("dve", lambda e: e.tensor_scalar(out=self.lsc[:, 1, :], in0=lt[:, 2, :], scalar1=8.0, scalar2=None, op0=ALU.mult), reads=[self.lt_b], writes=[self.lsc_b])
        P.op("dve", lambda e: e.tensor_scalar(out=self.lsc[:, 2, :], in0=lt[:, 2, :], scalar1=-16.0, scalar2=None, op0=ALU.mult), reads=[self.lt_b], writes=[self.lsc_b])
        P.op("pool", lambda e: e.memset(self.gwi[:], 0.0), writes=[self.gw_b])
        P.op("pool", lambda e: e.memset(self.gwf[:], 0.0), writes=[self.gw_b])
        P.op("pool", lambda e: e.memset(self.gbias[:], 0.0), writes=[self.gb_b])
        gsrc = lambda c: self.wb["w_in"][l, :, OFF_GATE + c:OFF_GATE + c + 4].rearrange("(k p) n -> p k n", p=128)
        nc = self.nc
        with nc.allow_non_contiguous_dma(reason="tiny gate weight columns"):
            for (dst, c, col) in ((self.gwi, 0, 0), (self.gwf, 4, 0), (self.gwi, 8, 32), (self.gwf, 12, 32)):
                P.dma("sp", lambda e, dst=dst, c=c, col=col: e.dma_start(out=dst[:, :, col:col + 4], in_=gsrc(c)),
                      reads=[self.wb_buf["w_in"]], writes=[self.gw_b])
            for (bcol, c, row) in ((0, 0, 0), (1, 4, 0), (0, 8, 32), (1, 12, 32)):
                P.dma("sp", lambda e, bcol=bcol, c=c, row=row: e.dma_start(
                    out=self.gbias[row:row + 4, bcol:bcol + 1],
                    in_=w["b_in"][l, OFF_GATE + c:OFF_GATE + c + 4].rearrange("(p o) -> p o", o=1)), writes=[self.gb_b])

    O_XF = 0
    O_XB = 16384
    O_ZB = 24576
    O_ZQ = 32768
    O_MEAN = 40960
    O_RSTD = 43008
    O_TMP = 45056
    O_TMP2 = 47104
    O_H = 49152
    O_ST = 71680
    O_X1B = 79872
    O_G = 88064
    O_Q = 104448
    O_E = 112640
    O_XB2 = 116736
    O_X1B2 = 124928

    def layer_norm_fm(self, l, gi):
        P = self.P
        ps_s, ps_q = self.ps(6), self.ps(7)
        for k in range(8):
            zk = self.avk(self.O_XF, F32, (8, 512), k)
            zb = self.avk(self.O_ZB, BF16, (8, 512), k)
            zq = self.avk(self.O_ZQ, BF16, (8, 512), k)
            P.op("dve", lambda e, zb=zb, zk=zk: e.tensor_copy(out=zb.ap, in_=zk.ap), reads=zk, writes=zb)
            P.op("act", lambda e, zq=zq, zk=zk: e.activation(out=zq.ap, in_=zk.ap, func=AF.Square), reads=zk, writes=zq)
        for k in range(8):
            zb = self.avk(self.O_ZB, BF16, (8, 512), k)
            P.op("pe", lambda e, zb=zb, k=k: e.matmul(ps_s.ap, lhsT=self.onesm[:], rhs=zb.ap, start=(k == 0), stop=(k == 7)),
                 reads=[zb, self.const_b], writes=ps_s)
        for k in range(8):
            zq = self.avk(self.O_ZQ, BF16, (8, 512), k)
            P.op("pe", lambda e, zq=zq, k=k: e.matmul(ps_q.ap, lhsT=self.onesm[:], rhs=zq.ap, start=(k == 0), stop=(k == 7)),
                 reads=[zq, self.const_b], writes=ps_q)
        mean = self.av_(self.O_MEAN, F32, (512,))
        rstd = self.av_(self.O_RSTD, F32, (512,))
        tmp = self.av_(self.O_TMP, F32, (512,))
        P.op("act", lambda e: e.activation(out=mean.ap, in_=ps_s.ap, func=AF.Copy), reads=ps_s, writes=mean)
        P.op("act", lambda e: e.activation(out=tmp.ap, in_=ps_s.ap, func=AF.Square), reads=ps_s, writes=tmp)
        P.op("dve", lambda e: e.tensor_tensor(out=tmp.ap, in0=ps_q.ap, in1=tmp.ap, op=ALU.subtract), reads=[ps_q, tmp], writes=tmp)
        P.op("dve", lambda e: e.tensor_scalar(out=tmp.ap, in0=tmp.ap, scalar1=0.0, scalar2=EPS_S, op0=ALU.max, op1=ALU.add), reads=tmp, writes=tmp)
        P.op("act", lambda e: e.activation(out=tmp.ap, in_=tmp.ap, func=AF.Ln), reads=tmp, writes=tmp)
        P.op("act", lambda e: e.activation(out=rstd.ap, in_=tmp.ap, func=AF.Exp, scale=-0.5), reads=tmp, writes=rstd)
        for k in range(8):
            zk = self.avk(self.O_XF, F32, (8, 512), k)
            xb = self.avk(self.xb_off, BF16, (8, 512), k)
            P.op("dve", lambda e, zk=zk: e.tensor_tensor(out=zk.ap, in0=zk.ap, in1=mean.ap, op=ALU.subtract), reads=[zk, mean], writes=zk)
            P.op("dve", lambda e, zk=zk: e.tensor_tensor(out=zk.ap, in0=zk.ap, in1=rstd.ap, op=ALU.mult), reads=[zk, rstd], writes=zk)
            g = self.pc("ln_g", l, gi * 8 + k)
            b = self.pc("ln_b", l, gi * 8 + k)
            P.op("act", lambda e, zk=zk, xb=xb, g=g, b=b: e.activation(out=xb.ap, in_=zk.ap, func=AF.Identity, scale=g, bias=b),
                 reads=[zk, self.prm_b], writes=xb)
        for k in range(8):
            zk = self.avk(self.O_XF, F32, (8, 512), k)
            g = self.pc("ln_g", l, gi * 8 + k)
            b = self.pc("ln_b", l, gi * 8 + k)
            P.op("act", lambda e, zk=zk, g=g, b=b: e.activation(out=zk.ap, in_=zk.ap, func=AF.Identity, scale=g, bias=b),
                 reads=[zk, self.prm_b], writes=zk)

    def ffn_fm(self, l, which):
        P = self.P
        kin, kout = which + "_in", which + "_out"
        pi = 0
        for g in range(11):
            wg = self.load_w(kin, l, 0, 8, g * 256, 256, q="sp")
            wu = self.load_w(kin, l, 0, 8, DFF + g * 256, 256, q="sp")
            for j in range(2):
                m = g * 2 + j
                pg, pu = self.ps(pi % 6), self.ps((pi + 1) % 6)
                pi += 2
                for k in range(8):
                    xb = self.avk(self.xb_off, BF16, (8, 512), k)
                    P.op("pe", lambda e, pg=pg, wg=wg, xb=xb, k=k, j=j: e.matmul(pg.ap, lhsT=wg.ap[:, k, j * 128:(j + 1) * 128], rhs=xb.ap,
                                                                                 start=(k == 0), stop=(k == 7)), reads=[wg, xb], writes=pg)
                for k in range(8):
                    xb = self.avk(self.xb_off, BF16, (8, 512), k)
                    P.op("pe", lambda e, pu=pu, wu=wu, xb=xb, k=k, j=j: e.matmul(pu.ap, lhsT=wu.ap[:, k, j * 128:(j + 1) * 128], rhs=xb.ap,
                                                                                 start=(k == 0), stop=(k == 7)), reads=[wu, xb], writes=pu)
                tmp = self.av_(self.O_TMP2 if m % 2 else self.O_TMP, F32, (512,))
                hm = self.avk(self.O_H, BF16, (22, 512), m)
                P.op("act", lambda e, tmp=tmp, pg=pg: e.activation(out=tmp.ap, in_=pg.ap, func=AF.Silu), reads=pg, writes=tmp)
                P.op("dve", lambda e, hm=hm, tmp=tmp, pu=pu: e.tensor_tensor(out=hm.ap, in0=pu.ap, in1=tmp.ap, op=ALU.mult), reads=[pu, tmp], writes=hm)
        for n4 in range(4):
            wo = self.load_w(kout, l, 0, 22, n4 * 256, 256, q="sp")
            for j in range(2):
                m = n4 * 2 + j
                py = self.ps(pi % 6)
                pi += 1
                for k in range(22):
                    hk = self.avk(self.O_H, BF16, (22, 512), k)
                    P.op("pe", lambda e, py=py, wo=wo, hk=hk, k=k, j=j: e.matmul(py.ap, lhsT=wo.ap[:, k, j * 128:(j + 1) * 128], rhs=hk.ap,
                                                                                 start=(k == 0), stop=(k == 21)), reads=[wo, hk], writes=py)
                xk = self.avk(self.O_XF, F32, (8, 512), m)
                P.op("dve", lambda e, xk=xk, py=py: e.scalar_tensor_tensor(out=xk.ap, in0=py.ap, scalar=C_FF, in1=xk.ap, op0=ALU.mult, op1=ALU.add),
                     reads=[py, xk], writes=xk)

    def proj_fm(self, key, l, c0, nchunks, kt, rhs_of, evac, r0=0, ps_cycle=6):
        P = self.P
        per = min(self.WSLOT // (kt * 128), nchunks)
        per = max(1, per)
        m = 0
        pi = getattr(self, "_pi", 0)
        while m < nchunks:
            n_here = min(per, nchunks - m)
            wv = self.load_w(key, l, r0, kt, c0 + m * 128, n_here * 128)
            for j in range(n_here):
                pv = self.ps(pi % ps_cycle)
                pi += 1
                for k in range(kt):
                    rk = rhs_of(k)
                    P.op("pe", lambda e, pv=pv, wv=wv, rk=rk, k=k, j=j: e.matmul(pv.ap, lhsT=wv.ap[:, k, j * 128:(j + 1) * 128], rhs=rk.ap,
                                                                                 start=(k == 0), stop=(k == kt - 1)), reads=[wv, rk], writes=pv)
                evac(m + j, pv)
            m += n_here
        self._pi = pi

    def phase_a(self, s, l):
        P = self.P
        w = self.win
        bcv = self.av_(self.O_X1B, BF16, (2048,))
        bco = self.av_(self.O_X1B + 4096, BF16, (2048,))
        P.dma("poolq", lambda e: e.dma_start(out=bcv.ap, in_=w["b_in"][l, OFF_V:OFF_O].partition_broadcast(128)), writes=bcv)
        P.dma("poolq", lambda e: e.dma_start(out=bco.ap, in_=w["b_in"][l, OFF_O:OFF_GATE].partition_broadcast(128)), writes=bco)
        bcgA = self.av_(self.O_Q, F32, (2048,))
        P.dma("sp", lambda e: e.dma_start(out=bcgA.ap, in_=w["mh_gain"][l].partition_broadcast(128)), writes=bcgA)
        for tt in range(NT):
            t0 = tt * T
            xf = self.av_(self.O_XF, F32, (8, 512))
            self.xb_off = self.O_XB if tt % 2 == 0 else self.O_XB2
            if l == 0:
                xt = self.av_(self.O_G, F32, (4, 1024))
                P.dma("actq", lambda e, xt=xt, t0=t0: e.dma_start(out=xt.ap, in_=self.x_in[s, t0:t0 + T, :].rearrange("(a p) d -> p a d", p=128)), writes=xt)
                for k in range(8):
                    pv = self.ps(k % 6)
                    for a in range(4):
                        P.op("pe", lambda e, pv=pv, xt=xt, a=a, k=k: e.transpose(out=pv.ap[:, a * 128:(a + 1) * 128], in_=xt.ap[:, a, k * 128:(k + 1) * 128], identity=self.idf[:]),
                             reads=[xt, self.const_b], writes=pv)
                    xk = self.avk(self.O_XF, F32, (8, 512), k)
                    xb = self.avk(self.xb_off, BF16, (8, 512), k)
                    P.op("act", lambda e, xk=xk, pv=pv: e.activation(out=xk.ap, in_=pv.ap, func=AF.Copy), reads=pv, writes=xk)
                    P.op("dve", lambda e, xb=xb, pv=pv: e.tensor_copy(out=xb.ap, in_=pv.ap), reads=pv, writes=xb)
            else:
                xbv = self.av_(self.xb_off, BF16, (8, 512))
                P.dma("sp", lambda e, xbv=xbv, t0=t0: e.dma_start(out=xbv.ap, in_=self.XNB[:, :, t0:t0 + T]), reads=[self.xn_b[tt]], writes=xbv)
                P.dma("sp", lambda e, xf=xf, t0=t0: e.dma_start(out=xf.ap, in_=self.XN[:, :, t0:t0 + T]), reads=[self.xn_b[tt]], writes=xf)
            self.ffn_fm(l, "ff1")
            self.layer_norm_fm(l, 0)
            P.dma("actq", lambda e, xf=xf, t0=t0: e.dma_start(out=self.X1[:, :, t0:t0 + T], in_=xf.ap), reads=xf, writes=[self.x1_b[tt]])
            xbv_ = self.av_(self.xb_off, BF16, (8, 512))
            P.dma("actq", lambda e, xbv_=xbv_, t0=t0: e.dma_start(out=self.X1B[:, :, t0:t0 + T], in_=xbv_.ap), reads=xbv_, writes=[self.x1_b[tt]])
            if self.stop == "A1":
                continue
            xbk = lambda k, o_=self.xb_off: self.avk(o_, BF16, (8, 512), k)
            st = self.av_(self.O_H, BF16, (16, 512))

            def ev_qk(m, pv, st=st):
                o = self.avk(self.O_H, BF16, (16, 512), m)
                b = self.pc("b_qk", l, m)
                P.op("act", lambda e: e.activation(out=o.ap, in_=pv.ap, func=AF.Identity, bias=b), reads=[pv, self.prm_b], writes=o)
            self.proj_fm("w_in", l, 0, 16, 8, xbk, ev_qk)
            P.dma("actq", lambda e, st=st, t0=t0: e.dma_start(out=self.QKP[:, :, 1 + t0:1 + t0 + T], in_=st.ap), reads=st, writes=[self.dbuf["QKP"]])
            st2 = self.av_(self.O_ST, BF16, (8, 512))

            def ev_xr(m, pv):
                o = self.avk(self.O_ST, BF16, (8, 512), m)
                b = self.pc("b_xr", l, m)
                P.op("act", lambda e: e.activation(out=o.ap, in_=pv.ap, func=AF.Identity, bias=b), reads=[pv, self.prm_b], writes=o)
            self.proj_fm("w_in", l, OFF_XR, 8, 8, xbk, ev_xr)
            P.dma("actq", lambda e, st2=st2, t0=t0: e.dma_start(out=self.XRP[:, :, 1 + t0:1 + t0 + T], in_=st2.ap), reads=st2, writes=[self.dbuf["XRP"]])
            for (gw, col) in ((self.gwi, 0), (self.gwf, 1)):
                pv = self.ps(6)
                gst = self.av_(self.O_ZB + col * 2048, F32, (512,))
                for k in range(8):
                    xb = xbk(k)
                    P.op("pe", lambda e, pv=pv, gw=gw, xb=xb, k=k: e.matmul(pv.ap[0:64, :], lhsT=gw[:, k, :], rhs=xb.ap, start=(k == 0), stop=(k == 7)),
                         reads=[xb, self.gw_b], writes=pv)
                P.op("act", lambda e, pv=pv, gst=gst, col=col: e.activation(out=gst.ap[0:64, :], in_=pv.ap[0:64, :], func=AF.Identity,
                                                                              bias=self.gbias[:, col:col + 1]), reads=[pv, self.gb_b], writes=gst)
                P.dma("actq", lambda e, gst=gst, col=col, t0=t0: e.dma_start(out=self.GSC[col, :, t0:t0 + T], in_=gst.ap[0:64, :]), reads=gst, writes=[self.dbuf["GSC"]])
            for (c0, bc, dstd, bk, sig) in ((OFF_V, bcv.ap, self.VS, "VS", False), (OFF_O, bco.ap, self.SIGO, "SIGO", True)):
                stv = self.av_(self.O_H if not sig else self.O_G, BF16, (4, 2048))
                for cg in range(4):
                    wv = self.load_w("w_in", l, 0, 8, c0 + cg * 512, 512)
                    for a in range(4):
                        pv = self.ps((cg * 4 + a) % 6)
                        for k in range(8):
                            xb = xbk(k)
                            P.op("pe", lambda e, pv=pv, wv=wv, xb=xb, k=k, a=a: e.matmul(pv.ap, lhsT=xb.ap[:, a * 128:(a + 1) * 128], rhs=wv.ap[:, k, :],
                                                                                         start=(k == 0), stop=(k == 7)), reads=[wv, xb], writes=pv)
                        o = Vw(stv.ap[:, a, cg * 512:(cg + 1) * 512], stv.bufs)
                        if not sig:
                            P.op("dve", lambda e, o=o, pv=pv, bc=bc, cg=cg: e.tensor_tensor(out=o.ap, in0=pv.ap, in1=bc[:, cg * 512:(cg + 1) * 512], op=ALU.add),
                                 reads=[pv, bcv, bco], writes=o)
                        else:
                            tmp = self.av_(self.O_TMP if a % 2 else self.O_TMP2, F32, (512,))
                            P.op("dve", lambda e, tmp=tmp, pv=pv, bc=bc, cg=cg: e.tensor_tensor(out=tmp.ap, in0=pv.ap, in1=bc[:, cg * 512:(cg + 1) * 512], op=ALU.add),
                                 reads=[pv, bcv, bco], writes=tmp)
                            P.op("act", lambda e, tmp=tmp: e.activation(out=tmp.ap, in_=tmp.ap, func=AF.Sigmoid), reads=tmp, writes=tmp)
                            P.op("dve", lambda e, o=o, tmp=tmp, cg=cg: e.tensor_tensor(out=o.ap, in0=tmp.ap, in1=bcgA.ap[:, cg * 512:(cg + 1) * 512], op=ALU.mult), reads=[tmp, bcgA], writes=o)
                P.dma("actq", lambda e, stv=stv, dstd=dstd, t0=t0: e.dma_start(out=dstd[t0:t0 + T, :].rearrange("(a p) c -> p a c", p=128), in_=stv.ap),
                      reads=stv, writes=[self.dbuf[bk]])

    def phase_b(self, s, l):
        self.conv_qk(l)
        if self.stop == "B0":
            return
        self.rglru(l)
        if self.stop == "B1":
            return
        self.gate_prep()
        if self.stop == "B2g":
            return
        self.mlstm(l)

    def build_diag(self, wkey, l, j, nj):
        P = self.P
        for tap in range(4):
            wcol = self.pc(wkey, l, tap * nj + j)
            P.op("dve", lambda e, tap=tap, wcol=wcol: e.tensor_scalar(out=self.dg[:, tap, :], in0=self.idf[:], scalar1=wcol, scalar2=None, op0=ALU.mult),
                 reads=[self.prm_b, self.const_b], writes=[self.dg_b])

    def conv_qk(self, l):
        P = self.P
        for j in range(16):
            par = j % 2
            row = self.av_(par * 8208, BF16, (S + 4,))
            out = self.av_(16416 + par * 8192, BF16, (S,))
            P.dma("sp", lambda e, row=row, j=j: e.dma_start(out=row.ap, in_=self.QKP[:, j, :]), reads=[self.dbuf["QKP"]], writes=row)
            self.build_diag("wc_qk", l, j, 16)
            b = self.pc("bc_qk", l, j)
            for tt in range(NT):
                pv = self.ps(tt % 6)
                for tap in range(4):
                    P.op("pe", lambda e, pv=pv, row=row, tap=tap, tt=tt: e.matmul(pv.ap, lhsT=self.dg[:, tap, :], rhs=row.ap[:, tt * T + tap:tt * T + tap + T],
                                                                                  start=(tap == 0), stop=(tap == 3)), reads=[row, self.dg_b], writes=pv)
                P.op("act", lambda e, pv=pv, out=out, tt=tt, b=b: e.activation(out=out.ap[:, tt * T:(tt + 1) * T], in_=pv.ap, func=AF.Silu, bias=b),
                     reads=[pv, self.prm_b], writes=out)
            P.dma("actq", lambda e, out=out, j=j: e.dma_start(out=self.QKC[:, j, :], in_=out.ap), reads=out, writes=[self.dbuf["QKC"]])

    def rglru(self, l):
        P = self.P
        A = self.av_
        O_XCF, O_XCB, O_ROW, O_R, O_I, O_A, O_T1, O_HF = 0, 16384, 24576, 34816, 51200, 67584, 83968, 100352
        NQ = 4
        QN = S // NQ
        xcf = A(O_XCF, F32, (S,))
        xcb = A(O_XCB, BF16, (S,))
        row = A(O_ROW, BF16, (S + 4,))
        rT = A(O_R, F32, (S,))
        hf = A(O_HF, F32, (S,))
        f32q = lambda off, q: A(off + q * QN * 4, F32, (QN,))
        f32t = lambda off, tt: A(off + tt * T * 4, F32, (T,))
        for n in range(8):
            P.dma("sp", lambda e, n=n: e.dma_start(out=row.ap, in_=self.XRP[:, n, :]), reads=[self.dbuf["XRP"]], writes=row)
            self.build_diag("wc_r", l, n, 8)
            b = self.pc("bc_r", l, n)
            for tt in range(NT):
                pv = self.ps(tt % 6)
                for tap in range(4):
                    P.op("pe", lambda e, pv=pv, tap=tap, tt=tt: e.matmul(pv.ap, lhsT=self.dg[:, tap, :], rhs=row.ap[:, tt * T + tap:tt * T + tap + T],
                                                                         start=(tap == 0), stop=(tap == 3)), reads=[row, self.dg_b], writes=pv)
                xf_t = f32t(O_XCF, tt)
                xb_t = A(O_XCB + tt * T * 2, BF16, (T,))
                P.op("act", lambda e, pv=pv, xf_t=xf_t, b=b: e.activation(out=xf_t.ap, in_=pv.ap, func=AF.Identity, bias=b), reads=[pv, self.prm_b], writes=xf_t)
                P.op("dve", lambda e, pv=pv, xb_t=xb_t, b=b: e.tensor_scalar(out=xb_t.ap, in0=pv.ap, scalar1=b, scalar2=None, op0=ALU.add), reads=[pv, self.prm_b], writes=xb_t)
            if "xrc" in self.dbg and n == 0:
                self.dump("xrc", xcf.ap, (128, S), F32, xcf)
            for d in range(2):
                for wi, key in enumerate(("lru_wa", "lru_wx")):
                    P.dma("sp", lambda e, d=d, wi=wi, key=key, n=n: e.dma_start(out=self.lw[:, d * 2 + wi, :], in_=self.wb[key][l, d, n]),
                          reads=[self.wb_buf[key]], writes=[self.lw_b])
            for d in range(2):
                for wi, (doff, bkey) in enumerate(((O_R, "lru_ba"), (O_I, "lru_bx"))):
                    bb = self.pc(bkey, l, d * 8 + n)
                    for tt in range(NT):
                        pv = self.ps((tt + wi * 3) % 6)
                        xb_t = A(O_XCB + tt * T * 2, BF16, (T,))
                        dst = f32t(doff, tt)
                        P.op("pe", lambda e, pv=pv, d=d, wi=wi, xb_t=xb_t: e.matmul(pv.ap, lhsT=self.lw[:, d * 2 + wi, :], rhs=xb_t.ap, start=True, stop=True),
                             reads=[xb_t, self.lw_b], writes=pv)
                        P.op("act", lambda e, pv=pv, dst=dst, bb=bb: e.activation(out=dst.ap, in_=pv.ap, func=AF.Sigmoid, bias=bb), reads=[pv, self.prm_b], writes=dst)
                s1 = self.lsc[:, 0, d * 8 + n:d * 8 + n + 1]
                s1n = self.lsc[:, 1, d * 8 + n:d * 8 + n + 1]
                s2 = self.lsc[:, 2, d * 8 + n:d * 8 + n + 1]
                qs = list(range(NQ)) if d == 0 else list(range(NQ - 1, -1, -1))
                R = lambda q: f32q(O_R, q)
                I = lambda q: f32q(O_I, q)
                Aq = lambda q: f32q(O_A, q)
                T1 = lambda q: f32q(O_T1, q)
                X = lambda q: f32q(O_XCF, q)
                for q in qs:
                    P.op("act", lambda e, s1=s1, a=Aq(q), r=R(q): e.activation(out=a.ap, in_=r.ap, func=AF.Exp, scale=s1), reads=[R(q), self.lsc_b], writes=Aq(q))
                for q in qs:
                    P.op("act", lambda e, s1n=s1n, t=T1(q), r=R(q): e.activation(out=t.ap, in_=r.ap, func=AF.Tanh, scale=s1n), reads=[R(q), self.lsc_b], writes=T1(q))
                for q in qs:
                    P.op("act", lambda e, s2=s2, r=R(q): e.activation(out=r.ap, in_=r.ap, func=AF.Exp, scale=s2), reads=[R(q), self.lsc_b], writes=R(q))
                for q in qs:
                    P.op("dve", lambda e, t=T1(q), r=R(q): e.scalar_tensor_tensor(out=t.ap, in0=r.ap, scalar=1.0, in1=t.ap, op0=ALU.add, op1=ALU.mult), reads=[R(q), T1(q)], writes=T1(q))
                for q in qs:
                    P.op("dve", lambda e, i_=I(q), x=X(q): e.tensor_tensor(out=i_.ap, in0=i_.ap, in1=x.ap, op=ALU.mult), reads=[I(q), X(q)], writes=I(q))
                for q in qs:
                    P.op("act", lambda e, t=T1(q): e.activation(out=t.ap, in_=t.ap, func=AF.Sqrt), reads=T1(q), writes=T1(q))
                for q in qs:
                    P.op("dve", lambda e, t=T1(q), i_=I(q): e.tensor_tensor(out=t.ap, in0=t.ap, in1=i_.ap, op=ALU.mult), reads=[T1(q), I(q)], writes=T1(q))
                prev = None
                for q in qs:
                    if d == 0:
                        o = f32q(O_HF, q)
                        init = 0.0 if prev is None else prev.ap[:, QN - 1:QN]
                        P.op("dve", lambda e, o=o, a=Aq(q), t=T1(q), init=init: e.tensor_tensor_scan(out=o.ap, data0=a.ap, data1=t.ap, initial=init, op0=ALU.mult, op1=ALU.add),
                             reads=[Aq(q), T1(q), prev], writes=o)
                    else:
                        o = R(q)
                        init = 0.0 if prev is None else prev.ap[:, 0:1]
                        P.op("dve", lambda e, o=o, a=Aq(q), t=T1(q), init=init: e.tensor_tensor_scan(out=o.ap[:, ::-1], data0=a.ap[:, ::-1], data1=t.ap[:, ::-1], initial=init, op0=ALU.mult, op1=ALU.add),
                             reads=[Aq(q), T1(q), prev], writes=o)
                    prev = o
            hsum = A(O_ROW, BF16, (S,))
            for q in range(NQ):
                hq = A(O_ROW + q * QN * 2, BF16, (QN,))
                P.op("dve", lambda e, hq=hq, a=f32q(O_HF, q), r=f32q(O_R, q): e.tensor_tensor(out=hq.ap, in0=a.ap, in1=r.ap, op=ALU.add), reads=[f32q(O_HF, q), f32q(O_R, q)], writes=[hq])
            P.dma("actq", lambda e, n=n: e.dma_start(out=self.HRT[:, n, :], in_=hsum.ap), reads=hsum, writes=[self.dbuf["HRT"], row])

    def gate_prep(self):
        P = self.P
        K16 = 16384
        GI = self.av_(2 * K16, F32, (S,))
        GF = self.av_(3 * K16, F32, (S,))
        P.dma("sp", lambda e: e.dma_start(out=GI.ap[0:64, :], in_=self.GSC[0]), reads=[self.dbuf["GSC"]], writes=GI)
        P.dma("sp", lambda e: e.dma_start(out=GF.ap[0:64, :], in_=self.GSC[1]), reads=[self.dbuf["GSC"]], writes=GF)
        MG = self.av_(0, F32, (S,))
        TQ = self.av_(K16, F32, (S,))
        R2 = lambda v: v.ap[0:64, :]
        P.op("act", lambda e: e.activation(out=R2(GF), in_=R2(GF), func=AF.Exp, scale=-1.0), reads=GF, writes=GF)
        P.op("act", lambda e: e.activation(out=R2(GF), in_=R2(GF), func=AF.Ln, bias=1.0), reads=GF, writes=GF)
        P.op("dve", lambda e: e.memset(R2(MG), 1.0), reads=[], writes=MG)
        P.op("dve", lambda e: e.tensor_tensor_scan(out=TQ.ap[0:4, :], data0=MG.ap[0:4, :], data1=GF.ap[0:4, :], initial=0.0, op0=ALU.mult, op1=ALU.add),
             reads=[GF, MG], writes=TQ)
        P.op("dve", lambda e: e.tensor_tensor_scan(out=TQ.ap[32:36, ::-1], data0=MG.ap[32:36, ::-1], data1=GF.ap[32:36, ::-1], initial=0.0, op0=ALU.mult, op1=ALU.add),
             reads=[GF, MG], writes=TQ)
        P.op("dve", lambda e: e.tensor_tensor(out=R2(GI), in0=R2(GI), in1=R2(TQ), op=ALU.add), reads=[GI, TQ], writes=GI)
        P.op("pool", lambda e: e.tensor_copy(out=R2(GF), in_=R2(TQ)), reads=TQ, writes=GF)
        P.op("dve", lambda e: e.tensor_tensor_scan(out=MG.ap[0:4, :], data0=GI.ap[0:4, :], data1=GI.ap[0:4, :], initial=0.0, op0=ALU.max, op1=ALU.max), reads=GI, writes=MG)
        P.op("dve", lambda e: e.tensor_tensor_scan(out=MG.ap[32:36, ::-1], data0=GI.ap[32:36, ::-1], data1=GI.ap[32:36, ::-1], initial=0.0, op0=ALU.max, op1=ALU.max), reads=GI, writes=MG)
        rr = self.rr
        mg3 = MG.ap.rearrange("p (c t) -> p c t", t=128)
        P.op("dve", lambda e: e.memset(rr[:], 0.0), writes=[self.rr_b])
        P.op("dve", lambda e: e.tensor_copy(out=rr[0:4, 1, :], in_=mg3[0:4, :, 127]), reads=MG, writes=[self.rr_b])
        P.op("dve", lambda e: e.tensor_copy(out=rr[32:36, 1, :], in_=mg3[32:36, :, 0]), reads=MG, writes=[self.rr_b])
        P.op("dve", lambda e: e.tensor_copy(out=rr[0:4, 0, 1:32], in_=rr[0:4, 1, 0:31]), reads=[self.rr_b], writes=[self.rr_b])
        P.op("dve", lambda e: e.tensor_copy(out=rr[32:36, 0, 0:31], in_=rr[32:36, 1, 1:32]), reads=[self.rr_b], writes=[self.rr_b])
        v3 = lambda v: v.ap[0:64, :].rearrange("p (c t) -> p c t", t=128)
        bc3 = lambda q: rr[0:64, q, :].rearrange("p (c o) -> p c o", o=1).to_broadcast([64, 32, 128])
        for q in range(4):
            if q == 0:
                P.op("dve", lambda e: e.tensor_tensor(out=v3(TQ), in0=v3(GI), in1=bc3(0), op=ALU.subtract), reads=[GI, self.rr_b], writes=TQ)
                P.op("dve", lambda e: e.tensor_scalar(out=R2(TQ), in0=R2(TQ), scalar1=-LN16, scalar2=None, op0=ALU.add), reads=TQ, writes=TQ)
                P.op("act", lambda e: e.activation(out=R2(TQ), in_=R2(TQ), func=AF.Exp), reads=TQ, writes=TQ)
            elif q == 1:
                P.op("dve", lambda e: e.tensor_tensor(out=v3(TQ), in0=v3(GI), in1=bc3(1), op=ALU.subtract), reads=[GI, self.rr_b], writes=TQ)
                P.op("dve", lambda e: e.tensor_scalar(out=R2(TQ), in0=R2(TQ), scalar1=-LN16, scalar2=None, op0=ALU.add), reads=TQ, writes=TQ)
                P.op("act", lambda e: e.activation(out=R2(TQ), in_=R2(TQ), func=AF.Exp), reads=TQ, writes=TQ)
            elif q == 2:
                P.op("dve", lambda e: e.tensor_tensor(out=v3(TQ), in0=v3(GF), in1=bc3(0), op=ALU.subtract), reads=[GF, self.rr_b], writes=TQ)
                P.op("act", lambda e: e.activation(out=R2(TQ), in_=R2(TQ), func=AF.Exp), reads=TQ, writes=TQ)
            else:
                P.op("dve", lambda e: e.memset(R2(TQ), 0.0), writes=TQ)
                P.op("dve", lambda e: e.tensor_tensor(out=v3(TQ), in0=v3(TQ), in1=bc3(0), op=ALU.add), reads=[TQ, self.rr_b], writes=TQ)
                P.op("dve", lambda e: e.tensor_tensor(out=v3(TQ), in0=v3(TQ), in1=bc3(1), op=ALU.subtract), reads=[TQ, self.rr_b], writes=TQ)
                P.op("act", lambda e: e.activation(out=R2(TQ), in_=R2(TQ), func=AF.Exp), reads=TQ, writes=TQ)
            for c in range(32):
                pv = self.ps(c % 6)
                P.op("pe", lambda e, pv=pv, c=c: e.transpose(out=pv.ap[:, 0:64], in_=TQ.ap[0:64, c * 128:(c + 1) * 128], identity=self.idf[0:64, 0:64]),
                     reads=[TQ, self.const_b], writes=pv)
                P.op("dve", lambda e, pv=pv, c=c, q=q: e.tensor_copy(out=self.tg[:, c, q * 8:q * 8 + 4], in_=pv.ap[:, 0:4]), reads=pv, writes=[self.tg_b])
                P.op("act", lambda e, pv=pv, c=c, q=q: e.activation(out=self.tg[:, c, q * 8 + 4:q * 8 + 8], in_=pv.ap[:, 32:36], func=AF.Copy), reads=pv, writes=[self.tg_b])

    def mlstm(self, l):
        P = self.P
        A = self.av_
        CH = 28672
        hf_b = [[Buf(f"hf{h}_{c}") for c in range(32)] for h in range(4)]
        sm = self.sm
        pending = []
        for hp in range(2):
            chains = [(hh, d) for hh in range(2) for d in range(2)]
            st = {}
            for ci_, (hh, d) in enumerate(chains):
                o = ci_ * CH
                cf = A(o, F32, (2, 512))
                cb = A(o + 4096, BF16, (2, 512))
                nf = A(o + 6144, F32, (2,))
                nb = A(o + 6152, BF16, (2,))
                st[ci_] = (cf, cb, nf, nb, o + 8192)
                P.op("pool", lambda e, cf=cf: e.memset(cf.ap, 0.0), writes=cf)
                P.op("pool", lambda e, cb=cb: e.memset(cb.ap, 0.0), writes=cb)
                P.op("pool", lambda e, nf=nf: e.memset(nf.ap, 0.0), writes=[nf, nb])
                P.op("pool", lambda e, nb=nb: e.memset(nb.ap, 0.0), writes=[nf, nb])
            for i in range(32):
                par = i % 2
                combine = i >= 16
                S_ = {k: {} for k in ("p1", "A", "E1", "E2", "G", "H", "I", "J", "K", "L", "M")}
                for ci_, (hh, d) in enumerate(chains):
                    h = hp * 2 + hh
                    c = i if d == 0 else 31 - i
                    c0 = c * 128
                    cf, cb, nf, nb, ot = st[ci_]
                    ot = ot + par * 10240
                    colv = [self.tg[:, c, q * 8 + d * 4 + h:q * 8 + d * 4 + h + 1] for q in range(4)]
                    qc = A(ot, BF16, (2, 128))
                    kc = A(ot + 512, BF16, (2, 128))
                    vch = A(ot + 1024, BF16, (512,))
                    ktl = A(ot + 5120, BF16, (256,))
                    pT = A(ot + 5632, BF16, (128,))
                    hst = A(ot + 6144, F32, (512,))
                    hfl = A(ot + 2048, F32, (512,))
                    sgo = A(ot + 4096, BF16, (512,))
                    hmb = A(ot + 8192, BF16, (512,))
                    hmt = A(ot + 9216, BF16, (4, 128))
                    pb = (ci_ % 2) * 4
                    psm_b = self.ps(pb, BF16)
                    psm = self.ps(pb)
                    pA = self.ps(pb + 1)
                    pCs = [self.ps(pb + 2), self.ps(pb + 3)]
                    mask = self.maskf if d == 0 else self.maskb
                    sbase = ci_ * 32 + par * 16
                    sb_ = self.sm_b[ci_ * 2 + par]
                    den = sm[:, sbase:sbase + 1]
                    st6 = sm[:, sbase + 2:sbase + 8]
                    mv = sm[:, sbase + 8:sbase + 10]
                    rs = sm[:, sbase + 10:sbase + 11]
                    nmr = sm[:, sbase + 11:sbase + 12]
                    hfb = hf_b[h][c]

                    def p1(qc=qc, kc=kc, vch=vch, ktl=ktl, pT=pT, psm=psm, psm_b=psm_b, colv=colv, mask=mask, c0=c0, h=h, hfl=hfl, sgo=sgo, hfb=hfb):
                        P.dma("sp", lambda e: e.dma_start(out=qc.ap, in_=self.QKC[:, 2 * h:2 * h + 2, c0:c0 + 128]), reads=[self.dbuf["QKC"]], writes=qc)
                        P.dma("sp", lambda e: e.dma_start(out=kc.ap, in_=self.QKC[:, 8 + 2 * h:8 + 2 * h + 2, c0:c0 + 128]), reads=[self.dbuf["QKC"]], writes=kc)
                        P.dma("sp", lambda e: e.dma_start(out=vch.ap, in_=self.VS[c0:c0 + 128, h * 512:(h + 1) * 512]), reads=[self.dbuf["VS"]], writes=vch)
                        if combine:
                            P.dma("sp", lambda e: e.dma_start(out=hfl.ap, in_=self.HF[c0:c0 + 128, h * 512:(h + 1) * 512]), reads=[hfb], writes=hfl)
                            P.dma("sp", lambda e: e.dma_start(out=sgo.ap, in_=self.SIGO[c0:c0 + 128, h * 512:(h + 1) * 512]), reads=[self.dbuf["SIGO"]], writes=sgo)
                        for dc in range(2):
                            P.op("pe", lambda e, dc=dc: e.transpose(out=psm_b.ap[:, dc * 128:(dc + 1) * 128], in_=kc.ap[:, dc, :], identity=self.idb[:]),
                                 reads=[kc, self.const_b], writes=psm_b)
                        for dc in range(2):
                            P.op("pe", lambda e, dc=dc: e.matmul(psm.ap[:, 256:384], lhsT=kc.ap[:, dc, :], rhs=qc.ap[:, dc, :], start=(dc == 0), stop=(dc == 1)),
                                 reads=[kc, qc], writes=psm)
                        P.op("act", lambda e: e.activation(out=ktl.ap, in_=psm_b.ap[:, 0:256], func=AF.Copy, scale=colv[1]), reads=[psm_b, self.tg_b], writes=ktl)
                        P.op("dve", lambda e: e.scalar_tensor_tensor(out=pT.ap, in0=psm.ap[:, 256:384], scalar=colv[0], in1=mask[:], op0=ALU.mult, op1=ALU.mult),
                             reads=[psm, self.tg_b, self.const_b], writes=pT)

                    def stA(qc=qc, vch=vch, ktl=ktl, pT=pT, psm=psm, pA=pA, pCs=pCs, cb=cb, nb=nb):
                        P.op("pe", lambda e: e.matmul(pA.ap, lhsT=pT.ap, rhs=vch.ap, start=True, stop=False), reads=[pT, vch], writes=pA)
                        for dc in range(2):
                            P.op("pe", lambda e, dc=dc: e.matmul(pA.ap, lhsT=qc.ap[:, dc, :], rhs=cb.ap[:, dc, :], start=False, stop=(dc == 1)), reads=[qc, cb], writes=pA)
                        P.op("pe", lambda e: e.matmul(psm.ap[:, 384:385], lhsT=pT.ap, rhs=self.ones1[:, 0:1], start=True, stop=False), reads=[pT, self.const_b], writes=psm)
                        for dc in range(2):
                            P.op("pe", lambda e, dc=dc: e.matmul(psm.ap[:, 384:385], lhsT=qc.ap[:, dc, :], rhs=nb.ap[:, dc:dc + 1], start=False, stop=(dc == 1)), reads=[qc, nb], writes=psm)
                        for dc in range(2):
                            P.op("pe", lambda e, dc=dc: e.matmul(psm.ap[:, 386 + dc:387 + dc], lhsT=ktl.ap[:, dc * 128:(dc + 1) * 128], rhs=self.ones1[:, 0:1], start=True, stop=True),
                                 reads=[ktl, self.const_b], writes=psm)
                        for dc in range(2):
                            P.op("pe", lambda e, dc=dc: e.matmul(pCs[dc].ap, lhsT=ktl.ap[:, dc * 128:(dc + 1) * 128], rhs=vch.ap, start=True, stop=True), reads=[ktl, vch], writes=pCs[dc])

                    def stE1(psm=psm, pA=pA, pCs=pCs, cf=cf, nf=nf, colv=colv, den=den, sb_=sb_, hst=hst, hfl=hfl, c0=c0, h=h, hfb=hfb):
                        P.op("act", lambda e: e.activation(out=den, in_=psm.ap[:, 384:385], func=AF.Abs), reads=psm, writes=[sb_])
                        P.op("dve", lambda e: e.scalar_tensor_tensor(out=nf.ap, in0=nf.ap, scalar=colv[3], in1=psm.ap[:, 386:388], op0=ALU.mult, op1=ALU.add),
                             reads=[psm, nf, self.tg_b], writes=nf)
                        P.op("dve", lambda e: e.tensor_scalar(out=den, in0=den, scalar1=colv[2], scalar2=None, op0=ALU.max), reads=[sb_, self.tg_b], writes=[sb_])
                        P.op("dve", lambda e: e.reciprocal(out=den, in_=den), reads=[sb_], writes=[sb_])
                        for dc in range(2):
                            cfd = Vw(cf.ap[:, dc, :], cf.bufs)
                            P.op("dve", lambda e, dc=dc, cfd=cfd: e.scalar_tensor_tensor(out=cfd.ap, in0=cfd.ap, scalar=colv[3], in1=pCs[dc].ap, op0=ALU.mult, op1=ALU.add),
                                 reads=[pCs[dc], cfd, self.tg_b], writes=cfd)
                        if not combine:
                            P.op("act", lambda e: e.activation(out=hst.ap, in_=pA.ap, func=AF.Copy, scale=den), reads=[pA, sb_], writes=hst)
                            P.dma("actq", lambda e: e.dma_start(out=self.HF[c0:c0 + 128, h * 512:(h + 1) * 512], in_=hst.ap), reads=hst, writes=[hfb])
                        else:
                            P.op("dve", lambda e: e.scalar_tensor_tensor(out=hst.ap, in0=pA.ap, scalar=den, in1=hfl.ap, op0=ALU.mult, op1=ALU.add), reads=[pA, sb_, hfl], writes=hst)

                    def stE2(cf=cf, cb=cb, nf=nf, nb=nb):
                        P.op("act", lambda e: e.activation(out=nb.ap, in_=nf.ap, func=AF.Copy), reads=nf, writes=nb)
                        for dc in range(2):
                            cfd = Vw(cf.ap[:, dc, :], cf.bufs)
                            cbd = Vw(cb.ap[:, dc, :], cb.bufs)
                            P.op("act", lambda e, cfd=cfd, cbd=cbd: e.activation(out=cbd.ap, in_=cfd.ap, func=AF.Copy), reads=cfd, writes=cbd)

                    def stG(hst=hst, st6=st6, mv=mv, rs=rs, sb_=sb_):
                        P.op("dve", lambda e: e.bn_stats(out=st6, in_=hst.ap), reads=hst, writes=[sb_])
                        P.op("dve", lambda e: e.bn_aggr(out=mv, in_=st6), reads=[sb_], writes=[sb_])
                        P.op("dve", lambda e: e.tensor_scalar(out=rs, in0=mv[:, 1:2], scalar1=LN_EPS, scalar2=None, op0=ALU.add), reads=[sb_], writes=[sb_])

                    def stH(rs=rs, sb_=sb_):
                        P.op("act", lambda e: e.activation(out=rs, in_=rs, func=AF.Sqrt), reads=[sb_], writes=[sb_])

                    def stI(rs=rs, mv=mv, nmr=nmr, sb_=sb_):
                        P.op("dve", lambda e: e.reciprocal(out=rs, in_=rs), reads=[sb_], writes=[sb_])
                        P.op("dve", lambda e: e.scalar_tensor_tensor(out=nmr, in0=mv[:, 0:1], scalar=-1.0, in1=rs, op0=ALU.mult, op1=ALU.mult), reads=[sb_], writes=[sb_])

                    def stJ(hst=hst, rs=rs, nmr=nmr, sb_=sb_, h=h):
                        P.op("act", lambda e: e.activation(out=hst.ap, in_=hst.ap, func=AF.Identity, scale=rs, bias=nmr), reads=[hst, sb_], writes=hst)

                    def stK(hst=hst, hmb=hmb, sgo=sgo):
                        P.op("dve", lambda e: e.tensor_tensor(out=hmb.ap, in0=hst.ap, in1=sgo.ap, op=ALU.mult), reads=[hst, sgo], writes=hmb)

                    def stL(hmb=hmb, psm_b=psm_b):
                        for fc in range(4):
                            P.op("pe", lambda e, fc=fc: e.transpose(out=psm_b.ap[:, fc * 128:(fc + 1) * 128], in_=hmb.ap[:, fc * 128:(fc + 1) * 128], identity=self.idb[:]),
                                 reads=[hmb, self.const_b], writes=psm_b)

                    def stM(hmt=hmt, psm_b=psm_b, c0=c0, h=h):
                        P.op("act", lambda e: e.activation(out=hmt.ap, in_=psm_b.ap[:, 0:512].rearrange("p (a b) -> p a b", a=4), func=AF.Copy), reads=psm_b, writes=hmt)
                        P.dma("actq", lambda e: e.dma_start(out=self.HMT[:, 4 * h:4 * h + 4, c0:c0 + 128], in_=hmt.ap), reads=hmt, writes=[self.hmt_b[c0 // 128]])

                    for k, f in (("p1", p1), ("A", stA), ("E1", stE1), ("E2", stE2), ("G", stG), ("H", stH), ("I", stI), ("J", stJ), ("K", stK), ("L", stL), ("M", stM)):
                        S_[k][ci_] = f
                for ci_ in range(4):
                    S_["p1"][ci_]()
                for pair in ((0, 1), (2, 3)):
                    for ci_ in pair:
                        S_["A"][ci_]()
                    for ci_ in pair:
                        S_["E1"][ci_]()
                for ci_ in range(4):
                    S_["E2"][ci_]()
                for fn in pending:
                    fn()
                pending = []
                if combine:
                    for k in ("G", "H", "I", "J", "K"):
                        for ci_ in range(4):
                            S_[k][ci_]()
                    for pair in ((0, 1), (2, 3)):
                        for ci_ in pair:
                            pending.append(S_["L"][ci_])
                        for ci_ in pair:
                            pending.append(S_["M"][ci_])
        for fn in pending:
            fn()

    def dump_rows(self, name, v, r0, ncol, dt=F32):
        if name not in self.dbg:
            return
        if name not in self.dbg_outs:
            self.dbg_outs[name] = self.nc.dram_tensor("dbg_" + name, [S, ncol], dt, kind="ExternalOutput").ap()
        d = self.dbg_outs[name]
        self.P.dma("sp", lambda e: e.dma_start(out=d[r0:r0 + 128, :], in_=v.ap), reads=v)

    def mem_kv(self, s, l):
        P = self.P
        mt = self.av_(self.O_XF, F32, (2, 1024))
        mT = self.av_(self.O_H, BF16, (8, 256))
        P.dma("sp", lambda e: e.dma_start(out=mt.ap, in_=self.mem_in[s].rearrange("(a p) d -> p a d", p=128)), writes=mt)
        sm = self.sm
        sb_ = self.sm_b[6]
        for a in range(2):
            for hh in range(2):
                P.op("dve", lambda e, a=a, hh=hh: e.bn_stats(out=sm[:, hh * 6:hh * 6 + 6], in_=mt.ap[:, a, hh * 512:(hh + 1) * 512]), reads=mt, writes=[sb_])
            P.op("dve", lambda e: e.bn_aggr(out=sm[:, 12:14], in_=sm[:, 0:12]), reads=[sb_], writes=[sb_])
            P.op("dve", lambda e: e.tensor_scalar(out=sm[:, 14:15], in0=sm[:, 13:14], scalar1=LN_EPS, scalar2=None, op0=ALU.add), reads=[sb_], writes=[sb_])
            P.op("act", lambda e: e.activation(out=sm[:, 14:15], in_=sm[:, 14:15], func=AF.Sqrt), reads=[sb_], writes=[sb_])
            P.op("dve", lambda e: e.reciprocal(out=sm[:, 14:15], in_=sm[:, 14:15]), reads=[sb_], writes=[sb_])
            P.op("dve", lambda e, a=a: e.tensor_scalar(out=mt.ap[:, a, :], in0=mt.ap[:, a, :], scalar1=sm[:, 12:13], scalar2=sm[:, 14:15], op0=ALU.subtract, op1=ALU.mult),
                 reads=[mt, sb_], writes=mt)
            for k in range(8):
                pv = self.ps(k % 6)
                P.op("pe", lambda e, pv=pv, a=a, k=k: e.transpose(out=pv.ap[:, 0:128], in_=mt.ap[:, a, k * 128:(k + 1) * 128], identity=self.idf[:]), reads=[mt, self.const_b], writes=pv)
                g = self.pc("mem_g", l, k)
                b = self.pc("mem_b", l, k)
                P.op("act", lambda e, pv=pv, a=a, k=k, g=g, b=b: e.activation(out=mT.ap[:, k, a * 128:(a + 1) * 128], in_=pv.ap[:, 0:128], func=AF.Identity, scale=g, bias=b),
                     reads=[pv, self.prm_b], writes=mT)
        mk = lambda k: Vw(mT.ap[:, k, :], mT.bufs)

        def ev_k(m, pv):
            P.op("act", lambda e: e.activation(out=self.akT[:, m, :], in_=pv.ap[:, 0:256], func=AF.Copy, scale=1.0 / 16.0), reads=pv, writes=[self.akv_b])
        self._proj_generic("xa_wkv", l, 0, 8, 8, mk, ev_k, n=256)
        for cg in range(2):
            wv = self.load_w("xa_wkv", l, 0, 8, D + cg * 512, 512)
            for a in range(2):
                pv = self.ps((cg * 2 + a) % 6)
                for k in range(8):
                    P.op("pe", lambda e, pv=pv, wv=wv, k=k, a=a: e.matmul(pv.ap, lhsT=mT.ap[:, k, a * 128:(a + 1) * 128], rhs=wv.ap[:, k, :], start=(k == 0), stop=(k == 7)),
                         reads=[wv, mT], writes=pv)
                P.op("act", lambda e, pv=pv, a=a, cg=cg: e.activation(out=self.av[:, a, cg * 512:(cg + 1) * 512], in_=pv.ap, func=AF.Copy), reads=pv, writes=[self.akv_b])

    def _proj_generic(self, key, l, c0, nchunks, kt, rhs_of, evac, n=512, r0=0):
        P = self.P
        per = max(1, min(self.WSLOT // (kt * 128), nchunks))
        m = 0
        pi = getattr(self, "_pi", 0)
        while m < nchunks:
            n_here = min(per, nchunks - m)
            wv = self.load_w(key, l, r0, kt, c0 + m * 128, n_here * 128)
            for j in range(n_here):
                pv = self.ps(pi % 6)
                pi += 1
                for k in range(kt):
                    rk = rhs_of(k)
                    P.op("pe", lambda e, pv=pv, wv=wv, rk=rk, k=k, j=j: e.matmul(pv.ap[:, 0:n], lhsT=wv.ap[:, k, j * 128:(j + 1) * 128], rhs=rk.ap,
                                                                                 start=(k == 0), stop=(k == kt - 1)), reads=[wv, rk], writes=pv)
                evac(m + j, pv)
            m += n_here
        self._pi = pi

    def phase_c(self, s, l, last):
        P = self.P
        self.mem_kv(s, l)
        self.xb_off = self.O_XB
        for tt in range(NT):
            t0 = tt * T
            xf = self.av_(self.O_XF, F32, (8, 512))
            x1b_off = self.O_X1B if tt % 2 == 0 else self.O_X1B2
            hm = self.av_(self.O_H, BF16, (16, 512))
            hr = self.av_(self.O_ST, BF16, (8, 512))
            x1bv = self.av_(x1b_off, BF16, (8, 512))
            P.dma("sp", lambda e, x1bv=x1bv, t0=t0: e.dma_start(out=x1bv.ap, in_=self.X1B[:, :, t0:t0 + T]), reads=[self.x1_b[tt]], writes=x1bv)
            P.dma("sp", lambda e, hr=hr, t0=t0: e.dma_start(out=hr.ap, in_=self.HRT[:, :, t0:t0 + T]), reads=[self.dbuf["HRT"]], writes=hr)
            x1k = lambda k, o_=x1b_off: self.avk(o_, BF16, (8, 512), k)

            def ev_yr(m, pv):
                tmp = self.av_(self.O_TMP if m % 2 else self.O_TMP2, F32, (512,))
                b = self.pc("b_yr", l, m)
                hk = self.avk(self.O_ST, BF16, (8, 512), m)
                P.op("act", lambda e: e.activation(out=tmp.ap, in_=pv.ap, func=AF.Gelu_apprx_tanh, bias=b), reads=[pv, self.prm_b], writes=tmp)
                P.op("dve", lambda e: e.tensor_tensor(out=hk.ap, in0=hk.ap, in1=tmp.ap, op=ALU.mult), reads=[hk, tmp], writes=hk)
            self.proj_fm("w_in", l, OFF_YR, 8, 8, x1k, ev_yr)
            P.dma("sp", lambda e, hm=hm, t0=t0: e.dma_start(out=hm.ap, in_=self.HMT[:, :, t0:t0 + T]), reads=self.hmt_b[4 * tt:4 * tt + 4], writes=hm)

            def ev_mg(m, pv):
                gk = self.avk(self.O_G, BF16, (16, 512), m)
                b = self.pc("b_mg", l, m)
                P.op("act", lambda e: e.activation(out=gk.ap, in_=pv.ap, func=AF.Sigmoid, bias=b), reads=[pv, self.prm_b], writes=gk)
            self.proj_fm("w_in", l, OFF_MG, 16, 8, x1k, ev_mg)
            P.dma("sp", lambda e, xf=xf, t0=t0: e.dma_start(out=xf.ap, in_=self.X1[:, :, t0:t0 + T]), reads=[self.x1_b[tt]], writes=xf)
            hmk = lambda k: self.avk(self.O_H, BF16, (16, 512), k)
            hrk = lambda k: self.avk(self.O_ST, BF16, (8, 512), k)
            mtmp = {}

            def ev_pm(m, pv):
                tmp = self.avk(self.O_ZB, F32, (8, 512), m)
                gk = self.avk(self.O_G, BF16, (16, 512), m)
                P.op("dve", lambda e: e.tensor_tensor(out=tmp.ap, in0=pv.ap, in1=gk.ap, op=ALU.mult), reads=[pv, gk], writes=tmp)

            def ev_pr(m, pv):
                tmp = self.avk(self.O_ZB, F32, (8, 512), m)
                gk = self.avk(self.O_G, BF16, (16, 512), 8 + m)
                t2 = self.av_(self.O_TMP if m % 2 else self.O_TMP2, F32, (512,))
                mk_ = self.avk(self.O_Q, BF16, (8, 512), m)
                P.op("dve", lambda e: e.tensor_tensor(out=t2.ap, in0=pv.ap, in1=gk.ap, op=ALU.mult), reads=[pv, gk], writes=t2)
                P.op("dve", lambda e: e.tensor_tensor(out=mk_.ap, in0=tmp.ap, in1=t2.ap, op=ALU.add), reads=[tmp, t2], writes=mk_)
            self.proj_fm("w_pm", l, 0, 8, 16, hmk, ev_pm)
            self.proj_fm("w_pr", l, 0, 8, 8, hrk, ev_pr)
            if "merged" in self.dbg and tt == 0:
                self.dump("merged", self.av_(self.O_Q, BF16, (8, 512)).ap, (128, 8, 512), BF16, self.av_(self.O_Q, BF16, (8, 512)))
            mgk = lambda k: self.avk(self.O_Q, BF16, (8, 512), k)

            def ev_res(c):
                def ev(m, pv):
                    xk = self.avk(self.O_XF, F32, (8, 512), m)
                    P.op("dve", lambda e: e.scalar_tensor_tensor(out=xk.ap, in0=pv.ap, scalar=c, in1=xk.ap, op0=ALU.mult, op1=ALU.add), reads=[pv, xk], writes=xk)
                return ev
            self.proj_fm("w_out", l, 0, 8, 8, mgk, ev_res(C_MIX))
            self.layer_norm_fm(l, 1)
            if "x2" in self.dbg and tt == 0:
                self.dump("x2", xf.ap, (128, 8, 512), F32, xf)
            if self.stop == "C1":
                continue
            xbk = lambda k: self.avk(self.O_XB, BF16, (8, 512), k)

            def ev_q(m, pv):
                qk_ = self.avk(self.O_Q, BF16, (8, 512), m)
                P.op("act", lambda e: e.activation(out=qk_.ap, in_=pv.ap, func=AF.Copy), reads=pv, writes=qk_)
            self.proj_fm("xa_wq", l, 0, 8, 8, xbk, ev_q)
            ao = lambda m, o_=x1b_off: self.avk(o_, BF16, (8, 512), m)
            for h in range(4):
                eT = self.av_(self.O_E, BF16, (2, 512))
                for mc in range(2):
                    pv = self.ps((h * 2 + mc) % 4)
                    for dc in range(2):
                        qv = self.avk(self.O_Q, BF16, (8, 512), h * 2 + dc)
                        P.op("pe", lambda e, pv=pv, qv=qv, h=h, dc=dc, mc=mc: e.matmul(pv.ap, lhsT=self.akT[:, h * 2 + dc, mc * 128:(mc + 1) * 128], rhs=qv.ap, start=(dc == 0), stop=(dc == 1)),
                             reads=[qv, self.akv_b], writes=pv)
                    P.op("act", lambda e, pv=pv, eT=eT, mc=mc: e.activation(out=eT.ap[:, mc, :], in_=pv.ap, func=AF.Exp), reads=pv, writes=eT)
                pd = self.ps(4)
                for mc in range(2):
                    P.op("pe", lambda e, eT=eT, mc=mc: e.matmul(pd.ap, lhsT=self.ones1[:], rhs=eT.ap[:, mc, :], start=(mc == 0), stop=(mc == 1)), reads=[eT, self.const_b], writes=pd)
                rden = self.av_(self.O_E + 2048, F32, (512,))
                P.op("dve", lambda e, rden=rden: e.reciprocal(out=rden.ap, in_=pd.ap), reads=pd, writes=rden)
                for dc in range(2):
                    po = self.ps(5 + dc)
                    for mc in range(2):
                        P.op("pe", lambda e, po=po, eT=eT, mc=mc, h=h, dc=dc: e.matmul(po.ap, lhsT=self.av[:, mc, h * 256 + dc * 128:h * 256 + (dc + 1) * 128], rhs=eT.ap[:, mc, :],
                                                                                        start=(mc == 0), stop=(mc == 1)), reads=[eT, self.akv_b], writes=po)
                    aok = ao(h * 2 + dc)
                    P.op("dve", lambda e, po=po, aok=aok, rden=rden: e.tensor_tensor(out=aok.ap, in0=po.ap, in1=rden.ap, op=ALU.mult), reads=[po, rden], writes=aok)
            self.proj_fm("xa_wo", l, 0, 8, 8, ao, ev_res(C_MIX))
            self.layer_norm_fm(l, 2)
            if "x3" in self.dbg and tt == 0:
                self.dump("x3", xf.ap, (128, 8, 512), F32, xf)
            self.ffn_fm(l, "ff2")
            self.layer_norm_fm(l, 3)
            if not last:
                P.dma("actq", lambda e, xf=xf, t0=t0: e.dma_start(out=self.XN[:, :, t0:t0 + T], in_=xf.ap), reads=xf, writes=[self.xn_b[tt]])
                xbv2 = self.av_(self.O_XB, BF16, (8, 512))
                P.dma("actq", lambda e, xbv2=xbv2, t0=t0: e.dma_start(out=self.XNB[:, :, t0:t0 + T], in_=xbv2.ap), reads=xbv2, writes=[self.xn_b[tt]])
            else:
                yt = self.av_(self.O_G, F32, (4, 1024))
                for a in range(4):
                    for kk in range(2):
                        pv = self.ps((a * 2 + kk) % 6)
                        for k4 in range(4):
                            k = kk * 4 + k4
                            P.op("pe", lambda e, pv=pv, xf=xf, a=a, k=k, k4=k4: e.transpose(out=pv.ap[:, k4 * 128:(k4 + 1) * 128], in_=xf.ap[:, k, a * 128:(a + 1) * 128], identity=self.idf[:]),
                                 reads=[xf, self.const_b], writes=pv)
                        P.op("act" if kk else "dve", (lambda e, pv=pv, yt=yt, a=a, kk=kk: e.activation(out=yt.ap[:, a, kk * 512:(kk + 1) * 512], in_=pv.ap, func=AF.Copy)) if kk else
                             (lambda e, pv=pv, yt=yt, a=a, kk=kk: e.tensor_copy(out=yt.ap[:, a, kk * 512:(kk + 1) * 512], in_=pv.ap)), reads=pv, writes=yt)
                P.dma("actq", lambda e, yt=yt, t0=t0: e.dma_start(out=self.y_out[s, t0:t0 + T, :].rearrange("(a p) d -> p a d", p=128), in_=yt.ap), reads=yt)

    def build(self):
        self.setup()
        stop = self.stop
        if stop == "S":
            self.P.final_wait_all("sp")
            self.P.emit()
            return self.nc
        for s in range(self.nseq):
            for l in range(self.nlayers):
                self.layer_setup(l)
                self.phase_a(s, l)
                if stop in ("A1", "A"):
                    break
                self.phase_b(s, l)
                if stop in ("B0", "B1", "B2g", "B"):
                    break
                self.phase_c(s, l, last=(l == self.nlayers - 1))
                if stop in ("C1", "C"):
                    break
            if stop:
                break
        for name, ap, shape, dt in (("X1", self.X1, (128, 8, S), F32), ("XN", self.XN, (128, 8, S), F32), ("QKP", self.QKP, (128, 16, S + 4), BF16),
                                    ("XRP", self.XRP, (128, 8, S + 4), BF16), ("QKC", self.QKC, (128, 16, S), BF16), ("VS", self.VS, (S, 2048), BF16),
                                    ("SIGO", self.SIGO, (S, 2048), BF16), ("HF", self.HF, (S, 2048), F32), ("HMT", self.HMT, (128, 16, S), BF16),
                                    ("HRT", self.HRT, (128, 8, S), BF16), ("GSC", self.GSC, (2, 64, S), F32)):
            bufs = [self.dbuf[name]] + (self.x1_b if name == "X1" else []) + (self.xn_b if name == "XN" else []) + (self.hmt_b if name == "HMT" else [])
            self.dump(name, ap, shape, dt, bufs)
        if "TG" in self.dbg:
            self.dump("TG", self.tg[:], (128, 32, 32), F32, [self.tg_b])
        self.P.final_wait_all("sp")
        self.P.emit()
        return self.nc


_W_KEYS = ("w_in", "b_in", "w_conv_qk", "b_conv_qk", "w_conv_r", "b_conv_r", "mh_gain", "lru_wa", "lru_ba", "lru_wx",
           "lru_bx", "lru_lam", "w_pm", "w_pr", "w_out", "xa_wq", "xa_wkv", "xa_wo", "mem_ln_g", "mem_ln_b",
           "ff1_in", "ff1_out", "ff2_in", "ff2_out", "ln_g", "ln_b")


def kernel(x_prompt, x_sample, mem_prompt, mem_sample, **weights):
    ncores = 8
    xs = np.concatenate([np.asarray(x_prompt, np.float32), np.asarray(x_sample, np.float32)], axis=0)
    ms = np.concatenate([np.asarray(mem_prompt, np.float32), np.asarray(mem_sample, np.float32)], axis=0)
    nseq = xs.shape[0] // ncores
    b = Builder(nseq=nseq)
    nc = b.build()
    wts = {k: np.ascontiguousarray(np.asarray(weights[k], np.float32)) for k in _W_KEYS}
    in_maps = []
    for c in range(ncores):
        m = {"x": np.ascontiguousarray(xs[c * nseq:(c + 1) * nseq]), "mem": np.ascontiguousarray(ms[c * nseq:(c + 1) * nseq])}
        m.update(wts)
        in_maps.append(m)
    res = run_bass_kernel_spmd(nc, in_maps, core_ids=list(range(ncores)))
    y = np.concatenate([np.asarray(r["y"], np.float32) for r in res.results], axis=0)
    nb = np.asarray(x_prompt).shape[0]
    return (np.ascontiguousarray(y[:nb]), np.ascontiguousarray(y[nb:]))
```

```python
import math
import os
import numpy as np
from contextlib import ExitStack
import concourse.bass as bass
import concourse.mybir as mybir
from concourse.bass_utils import run_bass_kernel_spmd

F32 = mybir.dt.float32
BF16 = mybir.dt.bfloat16
AF = mybir.ActivationFunctionType
ALU = mybir.AluOpType

D = 1024
S = 4096
T = 512
NT = S // T
DFF = 2816
DIN = 10256
OFF_V, OFF_O, OFF_GATE, OFF_XR, OFF_YR, OFF_MG = 2048, 4096, 6144, 6160, 7184, 8208
NMEM = 256
ALPHA = 4.0 ** 0.25
LN_EPS = 1e-5
EPS_S = LN_EPS / (ALPHA * ALPHA)
C_FF = 0.5 / ALPHA
C_MIX = 1.0 / ALPHA
LN16 = math.log(16.0)


class Buf:
    __slots__ = ("name", "lw", "rd", "excl")

    def __init__(self, name="", excl=False):
        self.name = name
        self.lw = None
        self.rd = {}
        self.excl = excl


def _flat(xs):
    out = []
    for x in xs:
        if x is None:
            continue
        if isinstance(x, Buf):
            out.append(x)
        elif isinstance(x, (list, tuple)):
            out.extend(_flat(x))
        else:
            out.extend(x.bufs)
    return out


class Prog:
    ENG = ("pe", "act", "dve", "pool")
    DMAQ = ("sp", "actq", "poolq")

    def __init__(self, nc, n_dma_sems=16):
        self.nc = nc
        self.es = ExitStack()
        self.ops = {e: [] for e in ("pe", "act", "dve", "pool", "sp")}
        self.sem = {}
        self.cnt = {}
        for e in self.ENG:
            self.sem[e] = self.es.enter_context(nc.semaphore("s_" + e))
            self.cnt[e] = 0
        self.dsem = {}
        self.dcnt = {}
        self.dnext = {}
        for q in self.DMAQ:
            self.dsem[q] = [self.es.enter_context(nc.semaphore(f"d_{q}_{i}")) for i in range(n_dma_sems)]
            self.dcnt[q] = [0] * n_dma_sems
            self.dnext[q] = 0
        self.waited = {}
        self.nops = 0
        self.nwaits = 0

    @staticmethod
    def _issuer(stream):
        return {"sp": "sp", "actq": "act", "poolq": "pool"}.get(stream, stream)

    def _need(self, issuer, waits, dep):
        if dep is None:
            return
        key, val = dep
        if issuer == "pe" and key == ("e", "pe"):
            return
        if self.waited.get((issuer, key), 0) >= val:
            return
        if waits.get(key, 0) < val:
            waits[key] = val

    def _deps(self, issuer, waits, reads, writes):
        for b in reads:
            self._need(issuer, waits, b.lw)
        for b in writes:
            self._need(issuer, waits, b.lw)
            for v in b.rd.values():
                self._need(issuer, waits, v)

    def op(self, eng, fn, reads=(), writes=()):
        reads = _flat(reads)
        writes = _flat(writes)
        ex = [b for b in reads if b.excl]
        if ex:
            reads = [b for b in reads if not b.excl]
            writes = writes + ex
        waits = {}
        self._deps(eng, waits, reads, writes)
        for key, val in waits.items():
            self.waited[(eng, key)] = val
        self.cnt[eng] += 1
        key = ("e", eng)
        me = (key, self.cnt[eng])
        self.ops[eng].append((fn, list(waits.items()), key, 1))
        for b in reads:
            b.rd[key] = me
        for b in writes:
            b.lw = me
            b.rd = {}
        self.nops += 1
        self.nwaits += len(waits)

    def dma(self, q, fn, reads=(), writes=()):
        reads = _flat(reads)
        writes = _flat(writes)
        issuer = self._issuer(q)
        waits = {}
        i = self.dnext[q]
        self.dnext[q] = (i + 1) % len(self.dsem[q])
        key = ("d", q, i)
        if self.dcnt[q][i] > 0:
            self._need(issuer, waits, (key, self.dcnt[q][i]))
        self._deps(issuer, waits, reads, writes)
        for k2, val in waits.items():
            self.waited[(issuer, k2)] = val
        self.dcnt[q][i] += 16
        me = (key, self.dcnt[q][i])
        self.ops[issuer].append((fn, list(waits.items()), key, 16))
        for b in reads:
            b.rd[key] = me
        for b in writes:
            b.lw = me
            b.rd = {}
        self.nops += 1
        self.nwaits += len(waits)

    def _semof(self, key):
        if key[0] == "e":
            return self.sem[key[1]]
        return self.dsem[key[1]][key[2]]

    def final_wait_all(self, eng="sp"):
        waits = []
        for e in self.ENG:
            if self.cnt[e]:
                waits.append((("e", e), self.cnt[e]))
        for q in self.DMAQ:
            for i, c in enumerate(self.dcnt[q]):
                if c:
                    waits.append((("d", q, i), c))
        self.ops[eng].append((None, waits, None, 0))

    def emit(self):
        nc = self.nc
        engobj = {"pe": "tensor", "act": "scalar", "dve": "vector", "pool": "gpsimd", "sp": "sync"}
        with nc.allow_non_contiguous_dma(reason="small strided parameter / halo transfers"), nc.Block() as block:
            for e, attr in engobj.items():
                lst = self.ops[e]
                if not lst:
                    continue

                def body(eng, lst=lst):
                    for fn, waits, key, inc in lst:
                        for k2, val in waits:
                            eng.wait_ge(self._semof(k2), val)
                        if fn is not None:
                            fn(eng).then_inc(self._semof(key), inc)

                getattr(block, attr)(body)


class Vw:
    __slots__ = ("ap", "bufs")

    def __init__(self, ap, bufs):
        self.ap = ap
        self.bufs = bufs if isinstance(bufs, list) else [bufs]

    def __getitem__(self, idx):
        return Vw(self.ap[idx], self.bufs)


class Builder:
    def __init__(self, nseq=3, nlayers=2, stop=None, dbg=()):
        self.nseq = nseq
        self.nlayers = nlayers
        self.stop = stop
        self.dbg = set(dbg)
        nc = self.nc = bass.Bass("TRN2", target_bir_lowering=False)
        self.P = Prog(nc)
        self.es = self.P.es
        self.dbg_outs = {}
        self._decl_dram()
        self._decl_sbuf()

    def _decl_dram(self):
        nc = self.nc
        n = self.nseq
        L = 2
        di = lambda name, shape: nc.dram_tensor(name, list(shape), F32, kind="ExternalInput").ap()
        self.x_in = di("x", (n, S, D))
        self.mem_in = di("mem", (n, NMEM, D))
        self.win = {}
        shapes = dict(w_in=(L, D, DIN), b_in=(L, DIN), w_conv_qk=(L, 4, 2048), b_conv_qk=(L, 2048),
                      w_conv_r=(L, 4, 1024), b_conv_r=(L, 1024), mh_gain=(L, 2048),
                      lru_wa=(L, 2, 8, 128, 128), lru_ba=(L, 2, 1024), lru_wx=(L, 2, 8, 128, 128),
                      lru_bx=(L, 2, 1024), lru_lam=(L, 2, 1024), w_pm=(L, 2048, D), w_pr=(L, D, D),
                      w_out=(L, D, D), xa_wq=(L, D, D), xa_wkv=(L, D, 2 * D), xa_wo=(L, D, D),
                      mem_ln_g=(L, D), mem_ln_b=(L, D), ff1_in=(L, D, 2 * DFF), ff1_out=(L, DFF, D),
                      ff2_in=(L, D, 2 * DFF), ff2_out=(L, DFF, D), ln_g=(L, 4, D), ln_b=(L, 4, D))
        for k, shp in shapes.items():
            self.win[k] = di(k, shp)
        self.y_out = nc.dram_tensor("y", [n, S, D], F32, kind="ExternalOutput").ap()
        self.wb = {}
        self.wb_buf = {}
        for k in ("w_in", "w_pm", "w_pr", "w_out", "xa_wq", "xa_wkv", "xa_wo", "ff1_in", "ff1_out", "ff2_in", "ff2_out"):
            shp = shapes[k]
            self.wb[k] = nc.dram_tensor("wb_" + k, list(shp), BF16, kind="Internal").ap()
            self.wb_buf[k] = {}
        for k in ("lru_wa", "lru_wx"):
            self.wb[k] = nc.dram_tensor("wb_" + k, [L, 2, 8, 128, 128], BF16, kind="Internal").ap()
            self.wb_buf[k] = {}
        sc = lambda name, shape, dt: nc.dram_tensor(name, list(shape), dt, kind="Internal").ap()
        self.X1 = sc("X1", (128, 8, S), F32)
        self.XN = sc("XN", (128, 8, S), F32)
        self.X1B = sc("X1B", (128, 8, S), BF16)
        self.XNB = sc("XNB", (128, 8, S), BF16)
        self.QKP = sc("QKP", (128, 16, S + 4), BF16)
        self.XRP = sc("XRP", (128, 8, S + 4), BF16)
        self.QKC = sc("QKC", (128, 16, S), BF16)
        self.VS = sc("VS", (S, 2048), BF16)
        self.SIGO = sc("SIGO", (S, 2048), BF16)
        self.HF = sc("HF", (S, 2048), F32)
        self.HMT = sc("HMT", (128, 16, S), BF16)
        self.HRT = sc("HRT", (128, 8, S), BF16)
        self.GSC = sc("GSC", (2, 64, S), F32)
        self.dbuf = {k: Buf(k) for k in ("X1", "XN", "QKP", "XRP", "QKC", "VS", "SIGO", "HF", "HMT", "HRT", "GSC")}
        self.x1_b = [Buf(f"X1_{i}") for i in range(NT)]
        self.hmt_b = [Buf(f"HMT_{i}") for i in range(32)]
        self.xn_b = [Buf(f"XN_{i}") for i in range(NT)]

    def dump(self, name, src_ap, shape, dt, bufs, q="sp"):
        if name not in self.dbg:
            return
        d = self.nc.dram_tensor("dbg_" + name, list(shape), dt, kind="ExternalOutput").ap()
        self.dbg_outs[name] = d
        self.P.dma(q, lambda e: e.dma_start(out=d, in_=src_ap), reads=bufs)

    def _decl_sbuf(self):
        nc = self.nc
        es = self.es
        sb = lambda name, shape, dt=F32: es.enter_context(nc.sbuf_tensor(name, list(shape), dt))
        self.AR_BYTES = 116736 + 16384
        self.arena = sb("arena", (128, self.AR_BYTES // 4))
        self.abuf = [Buf(f"ar{i}") for i in range(self.AR_BYTES // 2048)]
        self.NW = 3
        self.WSLOT = 8192
        self.wpool = sb("wpool", (128, self.NW * self.WSLOT), BF16)
        self.wbufs = [Buf(f"w{i}") for i in range(self.NW)]
        self.wnext = 0
        self.prm = sb("prm", (128, 2 * 320))
        self.prm_b = Buf("prm")
        self.prm_off = {}
        self.idb = sb("idb", (128, 128), BF16)
        self.idf = sb("idf", (128, 128))
        self.onesf = sb("onesf", (128, 128))
        self.onesm = sb("onesm", (128, 128), BF16)
        self.ones1 = sb("ones1", (128, 128), BF16)
        self.maskf = sb("maskf", (128, 128), BF16)
        self.maskb = sb("maskb", (128, 128), BF16)
        self.const_b = Buf("const")
        self.stg = sb("stg", (128, 128))
        self.stg_b = Buf("stg")
        self.gwi = sb("gwi", (128, 8, 64), BF16)
        self.gwf = sb("gwf", (128, 8, 64), BF16)
        self.gw_b = Buf("gw")
        self.gbias = sb("gbias", (64, 2))
        self.gb_b = Buf("gb")
        self.tg = sb("tg", (128, 32, 32))
        self.tg_b = Buf("tg")
        self.rr = sb("rr", (64, 2, 32))
        self.rr_b = Buf("rr")
        self.sm = sb("sm", (128, 128))
        self.sm_b = [Buf(f"sm{i}") for i in range(8)]
        self.dg = sb("dg", (128, 4, 128), BF16)
        self.dg_b = Buf("dg")
        self.lw = sb("lw", (128, 4, 128), BF16)
        self.lw_b = Buf("lw")
        self.lsc = sb("lsc", (128, 3, 16))
        self.lsc_b = Buf("lsc")
        self.lt = sb("lt", (128, 3, 16))
        self.lt_b = Buf("lt")
        self.akT = sb("akT", (128, 8, 256), BF16)
        self.av = sb("av", (128, 2, 1024), BF16)
        self.akv_b = Buf("akv")
        self.psum = [es.enter_context(nc.psum_tensor(f"ps{i}", [128, 512], F32)) for i in range(8)]
        self.ps_b = [Buf(f"ps{i}", excl=True) for i in range(8)]

    def av_(self, off, dt, shape):
        esz = 2 if dt == BF16 else 4
        nel = int(np.prod(shape))
        nb = nel * esz
        assert off % 4 == 0 and nb % 4 == 0 and off + nb <= self.AR_BYTES, (off, nb)
        ap = self.arena[:, off // 4:(off + nb) // 4]
        if dt == BF16:
            ap = ap.bitcast(BF16)
        if len(shape) == 2:
            ap = ap.rearrange("p (a b) -> p a b", a=shape[0])
        elif len(shape) == 3:
            ap = ap.rearrange("p (a b c) -> p a b c", a=shape[0], b=shape[1])
        bufs = self.abuf[off // 2048:(off + nb + 2047) // 2048]
        return Vw(ap, list(bufs))

    def avk(self, off, dt, shape, k):
        esz = 2 if dt == BF16 else 4
        inner = int(np.prod(shape[1:]))
        v = self.av_(off + k * inner * esz, dt, shape[1:])
        return v

    def ps(self, i, dt=F32):
        ap = self.psum[i][:]
        if dt == BF16:
            ap = ap.bitcast(BF16)
        return Vw(ap, [self.ps_b[i]])

    def wslot(self):
        i = self.wnext
        self.wnext = (i + 1) % self.NW
        return i

    def wview(self, i, shape):
        nel = int(np.prod(shape))
        assert nel <= self.WSLOT
        ap = self.wpool[:, i * self.WSLOT:i * self.WSLOT + nel]
        if len(shape) == 2:
            ap = ap.rearrange("p (a b) -> p a b", a=shape[0])
        elif len(shape) == 3:
            ap = ap.rearrange("p (a b c) -> p a b c", a=shape[0], b=shape[1])
        return Vw(ap, [self.wbufs[i]])

    def load_w(self, key, l, r0, kt, c0, nc_, q="sp"):
        i = self.wslot()
        v = self.wview(i, (kt, nc_))
        src = self.wb[key][l, r0:r0 + kt * 128, c0:c0 + nc_].rearrange("(k p) n -> p k n", p=128)
        self.P.dma(q, lambda e: e.dma_start(out=v.ap, in_=src), reads=[self.wb_buf[key][l]], writes=v)
        return v

    def setup(self):
        P = self.P
        cb = self.const_b
        P.op("pool", lambda e: e.memset(self.onesf[:], 1.0), writes=[cb])
        P.op("pool", lambda e: e.memset(self.ones1[:], 1.0), writes=[cb])
        P.op("pool", lambda e: e.memset(self.onesm[:], 1.0 / 1024.0), writes=[cb])
        sel = lambda out, op, cm, pat: (lambda e: e.affine_select(out=out[:], in_=self.onesf[:], pattern=[[pat, 128]],
                                                                    compare_op=op, fill=0.0, base=0, channel_multiplier=cm))
        P.op("pool", sel(self.idf, ALU.is_equal, -1, 1), reads=[cb], writes=[cb])
        P.op("pool", sel(self.idb, ALU.is_equal, -1, 1), reads=[cb], writes=[cb])
        P.op("pool", sel(self.maskf, ALU.is_ge, -1, 1), reads=[cb], writes=[cb])
        P.op("pool", sel(self.maskb, ALU.is_ge, 1, -1), reads=[cb], writes=[cb])
        z = self.av_(0, BF16, (16, 4))
        P.op("pool", lambda e: e.memset(z.ap, 0.0), writes=z)
        for dst, nj, bk in ((self.QKP, 16, "QKP"), (self.XRP, 8, "XRP")):
            P.dma("sp", lambda e, dst=dst, nj=nj: e.dma_start(out=dst[:, :, 0:1], in_=z.ap[:, 0:nj, 0:1]), reads=z, writes=[self.dbuf[bk]])
            P.dma("sp", lambda e, dst=dst, nj=nj: e.dma_start(out=dst[:, :, S + 1:S + 4], in_=z.ap[:, 0:nj, 0:3]), reads=z, writes=[self.dbuf[bk]])
        self.convert_weights(0)

    def convert_weights(self, l, after=()):
        P = self.P
        order = ("ff1_in", "ff1_out", "w_in", "lru_wa", "lru_wx", "w_pm", "w_pr", "w_out", "xa_wkv", "xa_wq", "xa_wo", "ff2_in", "ff2_out")
        if True:
            for key in order:
                src = self.win[key][l]
                dst = self.wb[key][l]
                if key in ("lru_wa", "lru_wx"):
                    s2 = src.rearrange("d n i j -> (d n i) j")
                    d2 = dst.rearrange("d n i j -> (d n i) j")
                else:
                    s2, d2 = src, dst
                R, C = s2.shape
                self.wb_buf[key][l] = []
                for r0 in range(0, R, 512):
                    r1 = min(R, r0 + 512)
                    for c0 in range(0, C, 2048):
                        c1 = min(C, c0 + 2048)
                        bb = Buf("wbc")
                        self.wb_buf[key][l].append(bb)
                        P.dma("poolq", lambda e, s2=s2, d2=d2, r0=r0, r1=r1, c0=c0, c1=c1:
                              e.dma_start(out=d2[r0:r1, c0:c1], in_=s2[r0:r1, c0:c1]), reads=list(after), writes=[bb])

    def setup_params(self):
        P = self.P
        cb = self.const_b
        segs = []
        for l in range(self.nlayers):
            w = self.win
            segs += [(("ln_g", l), w["ln_g"][l].rearrange("g (k p) -> (g k) p", p=128)),
                     (("ln_b", l), w["ln_b"][l].rearrange("g (k p) -> (g k) p", p=128)),
                     (("b_qk", l), w["b_in"][l, 0:2048].rearrange("(k p) -> k p", p=128)),
                     (("b_xr", l), w["b_in"][l, OFF_XR:OFF_YR].rearrange("(k p) -> k p", p=128)),
                     (("b_yr", l), w["b_in"][l, OFF_YR:OFF_MG].rearrange("(k p) -> k p", p=128)),
                     (("b_mg", l), w["b_in"][l, OFF_MG:DIN].rearrange("(k p) -> k p", p=128)),
                     (("wc_qk", l), w["w_conv_qk"][l].rearrange("t (k p) -> (t k) p", p=128)),
                     (("bc_qk", l), w["b_conv_qk"][l].rearrange("(k p) -> k p", p=128)),
                     (("wc_r", l), w["w_conv_r"][l].rearrange("t (k p) -> (t k) p", p=128)),
                     (("bc_r", l), w["b_conv_r"][l].rearrange("(k p) -> k p", p=128)),
                     (("lru_ba", l), w["lru_ba"][l].rearrange("d (k p) -> (d k) p", p=128)),
                     (("lru_bx", l), w["lru_bx"][l].rearrange("d (k p) -> (d k) p", p=128)),
                     (("lru_lam", l), w["lru_lam"][l].rearrange("d (k p) -> (d k) p", p=128)),
                     (("mem_g", l), w["mem_ln_g"][l].rearrange("(k p) -> k p", p=128)),
                     (("mem_b", l), w["mem_ln_b"][l].rearrange("(k p) -> k p", p=128))]
        off = 0
        for i, (key, src) in enumerate(segs):
            J = src.shape[0]
            self.prm_off[key] = off
            pi = i % 2
            P.dma("sp", lambda e, src=src, J=J: e.dma_start(out=self.stg[0:J, :], in_=src), writes=[self.stg_b])
            pv = self.ps(pi)
            P.op("pe", lambda e, J=J, pv=pv: e.transpose(out=pv.ap[:, 0:J], in_=self.stg[0:J, :], identity=self.idf[0:J, 0:J]),
                 reads=[self.stg_b, cb], writes=pv)
            P.op("dve", lambda e, J=J, pv=pv, off=off: e.tensor_copy(out=self.prm[:, off:off + J], in_=pv.ap[:, 0:J]),
                 reads=pv, writes=[self.prm_b])
            off += J
        assert off <= 640, off

    def pc(self, key, l, j=0, n=1):
        o = self.prm_off[(key, l)] + j
        return self.prm[:, o:o + n]

    def layer_setup(self, l):
        P = self.P
        w = self.win
        lam = self.pc("lru_lam", l, 0, 16)
        lt = self.lt
        P.op("act", lambda e: e.activation(out=lt[:, 0, :], in_=lam, func=AF.Exp, scale=-1.0), reads=[self.prm_b], writes=[self.lt_b])
        P.op("dve", lambda e: e.tensor_scalar(out=lt[:, 1, :], in0=lt[:, 0, :], scalar1=-0.25, scalar2=1.0 / 3.0, op0=ALU.mult, op1=ALU.add), reads=[self.lt_b], writes=[self.lt_b])
        P.op("dve", lambda e: e.tensor_tensor(out=lt[:, 1, :], in0=lt[:, 1, :], in1=lt[:, 0, :], op=ALU.mult), reads=[self.lt_b], writes=[self.lt_b])
        P.op("dve", lambda e: e.tensor_scalar(out=lt[:, 1, :], in0=lt[:, 1, :], scalar1=-0.5, scalar2=None, op0=ALU.add), reads=[self.lt_b], writes=[self.lt_b])
        P.op("dve", lambda e: e.tensor_tensor(out=lt[:, 1, :], in0=lt[:, 1, :], in1=lt[:, 0, :], op=ALU.mult), reads=[self.lt_b], writes=[self.lt_b])
        P.op("dve", lambda e: e.tensor_scalar(out=lt[:, 1, :], in0=lt[:, 1, :], scalar1=1.0, scalar2=None, op0=ALU.add), reads=[self.lt_b], writes=[self.lt_b])
        P.op("dve", lambda e: e.tensor_tensor(out=lt[:, 2, :], in0=lt[:, 1, :], in1=lt[:, 0, :], op=ALU.mult), reads=[self.lt_b], writes=[self.lt_b])
        P.op("dve", lambda e: e.tensor_scalar(out=self.lsc[:, 0, :], in0=lt[:, 2, :], scalar1=-8.0, scalar2=None, op0=ALU.mult), reads=[self.lt_b], writes=[self.lsc_b])
        P.op("dve", lambda e: e.tensor_scalar(out=self.lsc[:, 1, :], in0=lt[:, 2, :], scalar1=8.0, scalar2=None, op0=ALU.mult), reads=[self.lt_b], writes=[self.lsc_b])
        P.op("dve", lambda e: e.tensor_scalar(out=self.lsc[:, 2, :], in0=lt[:, 2, :], scalar1=-16.0, scalar2=None, op0=ALU.mult), reads=[self.lt_b], writes=[self.lsc_b])

    def gate_weight_loads(self, l):
        P = self.P
        w = self.win
        P.op("pool", lambda e: e.memset(self.gwi[:], 0.0), writes=[self.gw_b])
        P.op("pool", lambda e: e.memset(self.gwf[:], 0.0), writes=[self.gw_b])
        P.op("pool", lambda e: e.memset(self.gbias[:], 0.0), writes=[self.gb_b])
        gsrc = lambda c: self.wb["w_in"][l, :, OFF_GATE + c:OFF_GATE + c + 4].rearrange("(k p) n -> p k n", p=128)
        nc = self.nc
        if True:
            for (dst, c, col) in ((self.gwi, 0, 0), (self.gwf, 4, 0), (self.gwi, 8, 32), (self.gwf, 12, 32)):
                P.dma("sp", lambda e, dst=dst, c=c, col=col: e.dma_start(out=dst[:, :, col:col + 4], in_=gsrc(c)),
                      reads=[self.wb_buf["w_in"][l]], writes=[self.gw_b])
            for (bcol, c, row) in ((0, 0, 0), (1, 4, 0), (0, 8, 32), (1, 12, 32)):
                P.dma("sp", lambda e, bcol=bcol, c=c, row=row: e.dma_start(
                    out=self.gbias[row:row + 4, bcol:bcol + 1],
                    in_=w["b_in"][l, OFF_GATE + c:OFF_GATE + c + 4].rearrange("(p o) -> p o", o=1)), writes=[self.gb_b])


    O_XF = 0
    O_XB = 16384
    O_ZB = 24576
    O_ZQ = 32768
    O_MEAN = 40960
    O_RSTD = 43008
    O_TMP = 45056
    O_TMP2 = 47104
    O_H = 49152
    O_ST = 71680
    O_X1B = 79872
    O_G = 88064
    O_Q = 104448
    O_E = 112640
    O_XB2 = 116736
    O_X1B2 = 124928

    def layer_norm_fm(self, l, gi):
        P = self.P
        ps_s, ps_q = self.ps(6), self.ps(7)
        for k in range(8):
            zk = self.avk(self.O_XF, F32, (8, 512), k)
            zb = self.avk(self.O_ZB, BF16, (8, 512), k)
            zq = self.avk(self.O_ZQ, BF16, (8, 512), k)
            P.op("dve", lambda e, zb=zb, zk=zk: e.tensor_copy(out=zb.ap, in_=zk.ap), reads=zk, writes=zb)
            P.op("act", lambda e, zq=zq, zk=zk: e.activation(out=zq.ap, in_=zk.ap, func=AF.Square), reads=zk, writes=zq)
        for k in range(8):
            zb = self.avk(self.O_ZB, BF16, (8, 512), k)
            P.op("pe", lambda e, zb=zb, k=k: e.matmul(ps_s.ap, lhsT=self.onesm[:], rhs=zb.ap, start=(k == 0), stop=(k == 7)),
                 reads=[zb, self.const_b], writes=ps_s)
        for k in range(8):
            zq = self.avk(self.O_ZQ, BF16, (8, 512), k)
            P.op("pe", lambda e, zq=zq, k=k: e.matmul(ps_q.ap, lhsT=self.onesm[:], rhs=zq.ap, start=(k == 0), stop=(k == 7)),
                 reads=[zq, self.const_b], writes=ps_q)
        mean = self.av_(self.O_MEAN, F32, (512,))
        rstd = self.av_(self.O_RSTD, F32, (512,))
        tmp = self.av_(self.O_TMP, F32, (512,))
        P.op("act", lambda e: e.activation(out=mean.ap, in_=ps_s.ap, func=AF.Copy), reads=ps_s, writes=mean)
        P.op("act", lambda e: e.activation(out=tmp.ap, in_=ps_s.ap, func=AF.Square), reads=ps_s, writes=tmp)
        P.op("dve", lambda e: e.tensor_tensor(out=tmp.ap, in0=ps_q.ap, in1=tmp.ap, op=ALU.subtract), reads=[ps_q, tmp], writes=tmp)
        P.op("dve", lambda e: e.tensor_scalar(out=tmp.ap, in0=tmp.ap, scalar1=0.0, scalar2=EPS_S, op0=ALU.max, op1=ALU.add), reads=tmp, writes=tmp)
        P.op("act", lambda e: e.activation(out=tmp.ap, in_=tmp.ap, func=AF.Ln), reads=tmp, writes=tmp)
        P.op("act", lambda e: e.activation(out=rstd.ap, in_=tmp.ap, func=AF.Exp, scale=-0.5), reads=tmp, writes=rstd)
        for k in range(8):
            zk = self.avk(self.O_XF, F32, (8, 512), k)
            xb = self.avk(self.xb_off, BF16, (8, 512), k)
            P.op("dve", lambda e, zk=zk: e.tensor_tensor(out=zk.ap, in0=zk.ap, in1=mean.ap, op=ALU.subtract), reads=[zk, mean], writes=zk)
            P.op("dve", lambda e, zk=zk: e.tensor_tensor(out=zk.ap, in0=zk.ap, in1=rstd.ap, op=ALU.mult), reads=[zk, rstd], writes=zk)
            g = self.pc("ln_g", l, gi * 8 + k)
            b = self.pc("ln_b", l, gi * 8 + k)
            P.op("act", lambda e, zk=zk, xb=xb, g=g, b=b: e.activation(out=xb.ap, in_=zk.ap, func=AF.Identity, scale=g, bias=b),
                 reads=[zk, self.prm_b], writes=xb)
        for k in range(8):
            zk = self.avk(self.O_XF, F32, (8, 512), k)
            g = self.pc("ln_g", l, gi * 8 + k)
            b = self.pc("ln_b", l, gi * 8 + k)
            P.op("act", lambda e, zk=zk, g=g, b=b: e.activation(out=zk.ap, in_=zk.ap, func=AF.Identity, scale=g, bias=b),
                 reads=[zk, self.prm_b], writes=zk)

    def ffn_fm(self, l, which):
        P = self.P
        kin, kout = which + "_in", which + "_out"
        pi = 0
        for g in range(11):
            wg = self.load_w(kin, l, 0, 8, g * 256, 256, q="sp")
            wu = self.load_w(kin, l, 0, 8, DFF + g * 256, 256, q="sp")
            for j in range(2):
                m = g * 2 + j
                pg, pu = self.ps(pi % 6), self.ps((pi + 1) % 6)
                pi += 2
                for k in range(8):
                    xb = self.avk(self.xb_off, BF16, (8, 512), k)
                    P.op("pe", lambda e, pg=pg, wg=wg, xb=xb, k=k, j=j: e.matmul(pg.ap, lhsT=wg.ap[:, k, j * 128:(j + 1) * 128], rhs=xb.ap,
                                                                                 start=(k == 0), stop=(k == 7)), reads=[wg, xb], writes=pg)
                for k in range(8):
                    xb = self.avk(self.xb_off, BF16, (8, 512), k)
                    P.op("pe", lambda e, pu=pu, wu=wu, xb=xb, k=k, j=j: e.matmul(pu.ap, lhsT=wu.ap[:, k, j * 128:(j + 1) * 128], rhs=xb.ap,
                                                                                 start=(k == 0), stop=(k == 7)), reads=[wu, xb], writes=pu)
                tmp = self.av_(self.O_TMP2 if m % 2 else self.O_TMP, F32, (512,))
                hm = self.avk(self.O_H, BF16, (22, 512), m)
                P.op("act", lambda e, tmp=tmp, pg=pg: e.activation(out=tmp.ap, in_=pg.ap, func=AF.Silu), reads=pg, writes=tmp)
                P.op("dve", lambda e, hm=hm, tmp=tmp, pu=pu: e.tensor_tensor(out=hm.ap, in0=pu.ap, in1=tmp.ap, op=ALU.mult), reads=[pu, tmp], writes=hm)
        for n4 in range(4):
            wo = self.load_w(kout, l, 0, 22, n4 * 256, 256, q="sp")
            for j in range(2):
                m = n4 * 2 + j
                py = self.ps(pi % 6)
                pi += 1
                for k in range(22):
                    hk = self.avk(self.O_H, BF16, (22, 512), k)
                    P.op("pe", lambda e, py=py, wo=wo, hk=hk, k=k, j=j: e.matmul(py.ap, lhsT=wo.ap[:, k, j * 128:(j + 1) * 128], rhs=hk.ap,
                                                                                 start=(k == 0), stop=(k == 21)), reads=[wo, hk], writes=py)
                xk = self.avk(self.O_XF, F32, (8, 512), m)
                P.op("dve", lambda e, xk=xk, py=py: e.scalar_tensor_tensor(out=xk.ap, in0=py.ap, scalar=C_FF, in1=xk.ap, op0=ALU.mult, op1=ALU.add),
                     reads=[py, xk], writes=xk)

    def proj_fm(self, key, l, c0, nchunks, kt, rhs_of, evac, r0=0, ps_cycle=6):
        P = self.P
        per = min(self.WSLOT // (kt * 128), nchunks)
        per = max(1, per)
        m = 0
        pi = getattr(self, "_pi", 0)
        while m < nchunks:
            n_here = min(per, nchunks - m)
            wv = self.load_w(key, l, r0, kt, c0 + m * 128, n_here * 128)
            for j in range(n_here):
                pv = self.ps(pi % ps_cycle)
                pi += 1
                for k in range(kt):
                    rk = rhs_of(k)
                    P.op("pe", lambda e, pv=pv, wv=wv, rk=rk, k=k, j=j: e.matmul(pv.ap, lhsT=wv.ap[:, k, j * 128:(j + 1) * 128], rhs=rk.ap,
                                                                                 start=(k == 0), stop=(k == kt - 1)), reads=[wv, rk], writes=pv)
                evac(m + j, pv)
            m += n_here
        self._pi = pi

    def phase_a(self, s, l):
        P = self.P
        w = self.win
        bcv = self.av_(self.O_X1B, BF16, (2048,))
        bco = self.av_(self.O_X1B + 4096, BF16, (2048,))
        P.dma("poolq", lambda e: e.dma_start(out=bcv.ap, in_=w["b_in"][l, OFF_V:OFF_O].partition_broadcast(128)), writes=bcv)
        P.dma("poolq", lambda e: e.dma_start(out=bco.ap, in_=w["b_in"][l, OFF_O:OFF_GATE].partition_broadcast(128)), writes=bco)
        bcgA = self.av_(self.O_Q, F32, (2048,))
        P.dma("sp", lambda e: e.dma_start(out=bcgA.ap, in_=w["mh_gain"][l].partition_broadcast(128)), writes=bcgA)
        for tt in range(NT):
            t0 = tt * T
            xf = self.av_(self.O_XF, F32, (8, 512))
            self.xb_off = self.O_XB if tt % 2 == 0 else self.O_XB2
            if l == 0:
                xt = self.av_(self.O_G, F32, (4, 1024))
                P.dma("actq", lambda e, xt=xt, t0=t0: e.dma_start(out=xt.ap, in_=self.x_in[s, t0:t0 + T, :].rearrange("(a p) d -> p a d", p=128)), writes=xt)
                for k in range(8):
                    pv = self.ps(k % 6)
                    for a in range(4):
                        P.op("pe", lambda e, pv=pv, xt=xt, a=a, k=k: e.transpose(out=pv.ap[:, a * 128:(a + 1) * 128], in_=xt.ap[:, a, k * 128:(k + 1) * 128], identity=self.idf[:]),
                             reads=[xt, self.const_b], writes=pv)
                    xk = self.avk(self.O_XF, F32, (8, 512), k)
                    xb = self.avk(self.xb_off, BF16, (8, 512), k)
                    P.op("act", lambda e, xk=xk, pv=pv: e.activation(out=xk.ap, in_=pv.ap, func=AF.Copy), reads=pv, writes=xk)
                    P.op("dve", lambda e, xb=xb, pv=pv: e.tensor_copy(out=xb.ap, in_=pv.ap), reads=pv, writes=xb)
            else:
                xbv = self.av_(self.xb_off, BF16, (8, 512))
                P.dma("sp", lambda e, xbv=xbv, t0=t0: e.dma_start(out=xbv.ap, in_=self.XNB[:, :, t0:t0 + T]), reads=[self.xn_b[tt]], writes=xbv)
                P.dma("sp", lambda e, xf=xf, t0=t0: e.dma_start(out=xf.ap, in_=self.XN[:, :, t0:t0 + T]), reads=[self.xn_b[tt]], writes=xf)
            self.ffn_fm(l, "ff1")
            self.layer_norm_fm(l, 0)
            P.dma("actq", lambda e, xf=xf, t0=t0: e.dma_start(out=self.X1[:, :, t0:t0 + T], in_=xf.ap), reads=xf, writes=[self.x1_b[tt]])
            xbv_ = self.av_(self.xb_off, BF16, (8, 512))
            P.dma("actq", lambda e, xbv_=xbv_, t0=t0: e.dma_start(out=self.X1B[:, :, t0:t0 + T], in_=xbv_.ap), reads=xbv_, writes=[self.x1_b[tt]])
            if self.stop == "A1":
                continue
            xbk = lambda k, o_=self.xb_off: self.avk(o_, BF16, (8, 512), k)
            st = self.av_(self.O_H, BF16, (16, 512))

            def ev_qk(m, pv, st=st):
                o = self.avk(self.O_H, BF16, (16, 512), m)
                b = self.pc("b_qk", l, m)
                P.op("act", lambda e: e.activation(out=o.ap, in_=pv.ap, func=AF.Identity, bias=b), reads=[pv, self.prm_b], writes=o)
            self.proj_fm("w_in", l, 0, 16, 8, xbk, ev_qk)
            P.dma("actq", lambda e, st=st, t0=t0: e.dma_start(out=self.QKP[:, :, 1 + t0:1 + t0 + T], in_=st.ap), reads=st, writes=[self.dbuf["QKP"]])
            st2 = self.av_(self.O_ST, BF16, (8, 512))

            def ev_xr(m, pv):
                o = self.avk(self.O_ST, BF16, (8, 512), m)
                b = self.pc("b_xr", l, m)
                P.op("act", lambda e: e.activation(out=o.ap, in_=pv.ap, func=AF.Identity, bias=b), reads=[pv, self.prm_b], writes=o)
            self.proj_fm("w_in", l, OFF_XR, 8, 8, xbk, ev_xr)
            P.dma("actq", lambda e, st2=st2, t0=t0: e.dma_start(out=self.XRP[:, :, 1 + t0:1 + t0 + T], in_=st2.ap), reads=st2, writes=[self.dbuf["XRP"]])
            if tt == 0:
                self.gate_weight_loads(l)
            for (gw, col) in ((self.gwi, 0), (self.gwf, 1)):
                pv = self.ps(6)
                gst = self.av_(self.O_ZB + col * 2048, F32, (512,))
                for k in range(8):
                    xb = xbk(k)
                    P.op("pe", lambda e, pv=pv, gw=gw, xb=xb, k=k: e.matmul(pv.ap[0:64, :], lhsT=gw[:, k, :], rhs=xb.ap, start=(k == 0), stop=(k == 7)),
                         reads=[xb, self.gw_b], writes=pv)
                P.op("act", lambda e, pv=pv, gst=gst, col=col: e.activation(out=gst.ap[0:64, :], in_=pv.ap[0:64, :], func=AF.Identity,
                                                                              bias=self.gbias[:, col:col + 1]), reads=[pv, self.gb_b], writes=gst)
                P.dma("actq", lambda e, gst=gst, col=col, t0=t0: e.dma_start(out=self.GSC[col, :, t0:t0 + T], in_=gst.ap[0:64, :]), reads=gst, writes=[self.dbuf["GSC"]])
            for (c0, bc, dstd, bk, sig) in ((OFF_V, bcv.ap, self.VS, "VS", False), (OFF_O, bco.ap, self.SIGO, "SIGO", True)):
                stv = self.av_(self.O_H if not sig else self.O_G, BF16, (4, 2048))
                for cg in range(4):
                    wv = self.load_w("w_in", l, 0, 8, c0 + cg * 512, 512)
                    for a in range(4):
                        pv = self.ps((cg * 4 + a) % 6)
                        for k in range(8):
                            xb = xbk(k)
                            P.op("pe", lambda e, pv=pv, wv=wv, xb=xb, k=k, a=a: e.matmul(pv.ap, lhsT=xb.ap[:, a * 128:(a + 1) * 128], rhs=wv.ap[:, k, :],
                                                                                         start=(k == 0), stop=(k == 7)), reads=[wv, xb], writes=pv)
                        o = Vw(stv.ap[:, a, cg * 512:(cg + 1) * 512], stv.bufs)
                        if not sig:
                            P.op("dve", lambda e, o=o, pv=pv, bc=bc, cg=cg: e.tensor_tensor(out=o.ap, in0=pv.ap, in1=bc[:, cg * 512:(cg + 1) * 512], op=ALU.add),
                                 reads=[pv, bcv, bco], writes=o)
                        else:
                            tmp = self.av_(self.O_TMP if a % 2 else self.O_TMP2, F32, (512,))
                            P.op("dve", lambda e, tmp=tmp, pv=pv, bc=bc, cg=cg: e.tensor_tensor(out=tmp.ap, in0=pv.ap, in1=bc[:, cg * 512:(cg + 1) * 512], op=ALU.add),
                                 reads=[pv, bcv, bco], writes=tmp)
                            P.op("act", lambda e, tmp=tmp: e.activation(out=tmp.ap, in_=tmp.ap, func=AF.Sigmoid), reads=tmp, writes=tmp)
                            P.op("dve", lambda e, o=o, tmp=tmp, cg=cg: e.tensor_tensor(out=o.ap, in0=tmp.ap, in1=bcgA.ap[:, cg * 512:(cg + 1) * 512], op=ALU.mult), reads=[tmp, bcgA], writes=o)
                P.dma("actq", lambda e, stv=stv, dstd=dstd, t0=t0: e.dma_start(out=dstd[t0:t0 + T, :].rearrange("(a p) c -> p a c", p=128), in_=stv.ap),
                      reads=stv, writes=[self.dbuf[bk]])

    def phase_b(self, s, l):
        if s == 0 and l == 0 and self.nlayers > 1:
            self.convert_weights(1, after=[self.dbuf["GSC"]])
        self.conv_qk(l, self.gate_prep_steps())
        if self.stop == "B0":
            return
        self.rglru(l)
        if self.stop == "B1":
            return
        self.mlstm(l)

    def build_diag(self, wkey, l, j, nj):
        P = self.P
        for tap in range(4):
            wcol = self.pc(wkey, l, tap * nj + j)
            P.op("dve", lambda e, tap=tap, wcol=wcol: e.tensor_scalar(out=self.dg[:, tap, :], in0=self.idf[:], scalar1=wcol, scalar2=None, op0=ALU.mult),
                 reads=[self.prm_b, self.const_b], writes=[self.dg_b])

    def conv_qk(self, l, bg_steps=()):
        P = self.P
        bg = list(bg_steps)
        bgi = [0]
        slots = [16 * NT]

        def run_bg(final=False):
            n = len(bg) - bgi[0] if final else -(-(len(bg) - bgi[0]) // max(1, slots[0]))
            for kind, a, k in bg[bgi[0]:bgi[0] + n]:
                getattr(P, kind)(*a, **k)
            bgi[0] += n
            slots[0] -= 1
        for j in range(16):
            par = j % 2
            row = self.av_(par * 8208, BF16, (S + 4,))
            out = self.av_(16416 + par * 8192, BF16, (S,))
            P.dma("sp", lambda e, row=row, j=j: e.dma_start(out=row.ap, in_=self.QKP[:, j, :]), reads=[self.dbuf["QKP"]], writes=row)
            self.build_diag("wc_qk", l, j, 16)
            b = self.pc("bc_qk", l, j)
            for tt in range(NT):
                pv = self.ps(tt % 6)
                for tap in range(4):
                    P.op("pe", lambda e, pv=pv, row=row, tap=tap, tt=tt: e.matmul(pv.ap, lhsT=self.dg[:, tap, :], rhs=row.ap[:, tt * T + tap:tt * T + tap + T],
                                                                                  start=(tap == 0), stop=(tap == 3)), reads=[row, self.dg_b], writes=pv)
                P.op("act", lambda e, pv=pv, out=out, tt=tt, b=b: e.activation(out=out.ap[:, tt * T:(tt + 1) * T], in_=pv.ap, func=AF.Silu, bias=b),
                     reads=[pv, self.prm_b], writes=out)
                run_bg()
            P.dma("actq", lambda e, out=out, j=j: e.dma_start(out=self.QKC[:, j, :], in_=out.ap), reads=out, writes=[self.dbuf["QKC"]])
        run_bg(final=True)

    def rglru(self, l):
        P = self.P
        A = self.av_
        O_XCF, O_XCB, O_ROW, O_R, O_I, O_A, O_T1, O_HF = 0, 16384, 24576, 34816, 51200, 67584, 83968, 100352
        NQ = 4
        QN = S // NQ
        xcf = A(O_XCF, F32, (S,))
        xcb = A(O_XCB, BF16, (S,))
        row = A(O_ROW, BF16, (S + 4,))
        rT = A(O_R, F32, (S,))
        hf = A(O_HF, F32, (S,))
        f32q = lambda off, q: A(off + q * QN * 4, F32, (QN,))
        f32t = lambda off, tt: A(off + tt * T * 4, F32, (T,))
        for n in range(8):
            P.dma("sp", lambda e, n=n: e.dma_start(out=row.ap, in_=self.XRP[:, n, :]), reads=[self.dbuf["XRP"]], writes=row)
            self.build_diag("wc_r", l, n, 8)
            b = self.pc("bc_r", l, n)
            for tt in range(NT):
                pv = self.ps(tt % 6)
                for tap in range(4):
                    P.op("pe", lambda e, pv=pv, tap=tap, tt=tt: e.matmul(pv.ap, lhsT=self.dg[:, tap, :], rhs=row.ap[:, tt * T + tap:tt * T + tap + T],
                                                                         start=(tap == 0), stop=(tap == 3)), reads=[row, self.dg_b], writes=pv)
                xf_t = f32t(O_XCF, tt)
                xb_t = A(O_XCB + tt * T * 2, BF16, (T,))
                P.op("act", lambda e, pv=pv, xf_t=xf_t, b=b: e.activation(out=xf_t.ap, in_=pv.ap, func=AF.Identity, bias=b), reads=[pv, self.prm_b], writes=xf_t)
                P.op("dve", lambda e, pv=pv, xb_t=xb_t, b=b: e.tensor_scalar(out=xb_t.ap, in0=pv.ap, scalar1=b, scalar2=None, op0=ALU.add), reads=[pv, self.prm_b], writes=xb_t)
            if "xrc" in self.dbg and n == 0:
                self.dump("xrc", xcf.ap, (128, S), F32, xcf)
            for d in range(2):
                for wi, key in enumerate(("lru_wa", "lru_wx")):
                    P.dma("sp", lambda e, d=d, wi=wi, key=key, n=n: e.dma_start(out=self.lw[:, d * 2 + wi, :], in_=self.wb[key][l, d, n]),
                          reads=[self.wb_buf[key][l]], writes=[self.lw_b])
            for d in range(2):
                for wi, (doff, bkey) in enumerate(((O_R, "lru_ba"), (O_I, "lru_bx"))):
                    bb = self.pc(bkey, l, d * 8 + n)
                    for tt in range(NT):
                        pv = self.ps((tt + wi * 3) % 6)
                        xb_t = A(O_XCB + tt * T * 2, BF16, (T,))
                        dst = f32t(doff, tt)
                        P.op("pe", lambda e, pv=pv, d=d, wi=wi, xb_t=xb_t: e.matmul(pv.ap, lhsT=self.lw[:, d * 2 + wi, :], rhs=xb_t.ap, start=True, stop=True),
                             reads=[xb_t, self.lw_b], writes=pv)
                        P.op("act", lambda e, pv=pv, dst=dst, bb=bb: e.activation(out=dst.ap, in_=pv.ap, func=AF.Sigmoid, bias=bb), reads=[pv, self.prm_b], writes=dst)
                s1 = self.lsc[:, 0, d * 8 + n:d * 8 + n + 1]
                s1n = self.lsc[:, 1, d * 8 + n:d * 8 + n + 1]
                s2 = self.lsc[:, 2, d * 8 + n:d * 8 + n + 1]
                qs = list(range(NQ)) if d == 0 else list(range(NQ - 1, -1, -1))
                R = lambda q: f32q(O_R, q)
                I = lambda q: f32q(O_I, q)
                Aq = lambda q: f32q(O_A, q)
                T1 = lambda q: f32q(O_T1, q)
                X = lambda q: f32q(O_XCF, q)
                for q in qs:
                    P.op("act", lambda e, s1n=s1n, t=T1(q), r=R(q): e.activation(out=t.ap, in_=r.ap, func=AF.Tanh, scale=s1n), reads=[R(q), self.lsc_b], writes=T1(q))
                for q in qs:
                    P.op("act", lambda e, s1=s1, a=Aq(q), r=R(q): e.activation(out=a.ap, in_=r.ap, func=AF.Exp, scale=s1), reads=[R(q), self.lsc_b], writes=Aq(q))
                for q in qs:
                    P.op("act", lambda e, s2=s2, r=R(q): e.activation(out=r.ap, in_=r.ap, func=AF.Exp, scale=s2), reads=[R(q), self.lsc_b], writes=R(q))
                for q in qs:
                    P.op("dve", lambda e, t=T1(q), r=R(q): e.scalar_tensor_tensor(out=t.ap, in0=r.ap, scalar=1.0, in1=t.ap, op0=ALU.add, op1=ALU.mult), reads=[R(q), T1(q)], writes=T1(q))
                for q in qs:
                    P.op("dve", lambda e, i_=I(q), x=X(q): e.tensor_tensor(out=i_.ap, in0=i_.ap, in1=x.ap, op=ALU.mult), reads=[I(q), X(q)], writes=I(q))
                for q in qs:
                    P.op("act", lambda e, t=T1(q): e.activation(out=t.ap, in_=t.ap, func=AF.Sqrt), reads=T1(q), writes=T1(q))
                for q in qs:
                    P.op("dve", lambda e, t=T1(q), i_=I(q): e.tensor_tensor(out=t.ap, in0=t.ap, in1=i_.ap, op=ALU.mult), reads=[T1(q), I(q)], writes=T1(q))
                prev = None
                for q in qs:
                    if d == 0:
                        o = f32q(O_HF, q)
                        init = 0.0 if prev is None else prev.ap[:, QN - 1:QN]
                        P.op("dve", lambda e, o=o, a=Aq(q), t=T1(q), init=init: e.tensor_tensor_scan(out=o.ap, data0=a.ap, data1=t.ap, initial=init, op0=ALU.mult, op1=ALU.add),
                             reads=[Aq(q), T1(q), prev], writes=o)
                    else:
                        o = R(q)
                        init = 0.0 if prev is None else prev.ap[:, 0:1]
                        P.op("dve", lambda e, o=o, a=Aq(q), t=T1(q), init=init: e.tensor_tensor_scan(out=o.ap[:, ::-1], data0=a.ap[:, ::-1], data1=t.ap[:, ::-1], initial=init, op0=ALU.mult, op1=ALU.add),
                             reads=[Aq(q), T1(q), prev], writes=o)
                    prev = o
            hsum = A(O_ROW, BF16, (S,))
            for q in range(NQ):
                hq = A(O_ROW + q * QN * 2, BF16, (QN,))
                P.op("dve", lambda e, hq=hq, a=f32q(O_HF, q), r=f32q(O_R, q): e.tensor_tensor(out=hq.ap, in0=a.ap, in1=r.ap, op=ALU.add), reads=[f32q(O_HF, q), f32q(O_R, q)], writes=[hq])
            P.dma("actq", lambda e, n=n: e.dma_start(out=self.HRT[:, n, :], in_=hsum.ap), reads=hsum, writes=[self.dbuf["HRT"], row])

    def gate_prep_steps(self):
        steps = []

        class _Q:
            def op(self_, *a, **k):
                steps.append(("op", a, k))

            def dma(self_, *a, **k):
                steps.append(("dma", a, k))
        P = _Q()
        K16 = 16384
        GB = 40960
        GI = self.av_(GB + 2 * K16, F32, (S,))
        GF = self.av_(GB + 3 * K16, F32, (S,))
        P.dma("sp", lambda e: e.dma_start(out=GI.ap[0:64, :], in_=self.GSC[0]), reads=[self.dbuf["GSC"]], writes=GI)
        P.dma("sp", lambda e: e.dma_start(out=GF.ap[0:64, :], in_=self.GSC[1]), reads=[self.dbuf["GSC"]], writes=GF)
        MG = self.av_(GB, F32, (S,))
        TQ = self.av_(GB + K16, F32, (S,))
        R2 = lambda v: v.ap[0:64, :]
        P.op("act", lambda e: e.activation(out=R2(GF), in_=R2(GF), func=AF.Exp, scale=-1.0), reads=GF, writes=GF)
        P.op("act", lambda e: e.activation(out=R2(GF), in_=R2(GF), func=AF.Ln, bias=1.0), reads=GF, writes=GF)
        P.op("dve", lambda e: e.memset(R2(MG), 1.0), reads=[], writes=MG)
        P.op("dve", lambda e: e.tensor_tensor_scan(out=TQ.ap[0:4, :], data0=MG.ap[0:4, :], data1=GF.ap[0:4, :], initial=0.0, op0=ALU.mult, op1=ALU.add),
             reads=[GF, MG], writes=TQ)
        P.op("dve", lambda e: e.tensor_tensor_scan(out=TQ.ap[32:36, ::-1], data0=MG.ap[32:36, ::-1], data1=GF.ap[32:36, ::-1], initial=0.0, op0=ALU.mult, op1=ALU.add),
             reads=[GF, MG], writes=TQ)
        P.op("dve", lambda e: e.tensor_tensor(out=R2(GI), in0=R2(GI), in1=R2(TQ), op=ALU.add), reads=[GI, TQ], writes=GI)
        P.op("pool", lambda e: e.tensor_copy(out=R2(GF), in_=R2(TQ)), reads=TQ, writes=GF)
        P.op("dve", lambda e: e.tensor_tensor_scan(out=MG.ap[0:4, :], data0=GI.ap[0:4, :], data1=GI.ap[0:4, :], initial=0.0, op0=ALU.max, op1=ALU.max), reads=GI, writes=MG)
        P.op("dve", lambda e: e.tensor_tensor_scan(out=MG.ap[32:36, ::-1], data0=GI.ap[32:36, ::-1], data1=GI.ap[32:36, ::-1], initial=0.0, op0=ALU.max, op1=ALU.max), reads=GI, writes=MG)
        rr = self.rr
        mg3 = MG.ap.rearrange("p (c t) -> p c t", t=128)
        P.op("dve", lambda e: e.memset(rr[:], 0.0), writes=[self.rr_b])
        P.op("dve", lambda e: e.tensor_copy(out=rr[0:4, 1, :], in_=mg3[0:4, :, 127]), reads=MG, writes=[self.rr_b])
        P.op("dve", lambda e: e.tensor_copy(out=rr[32:36, 1, :], in_=mg3[32:36, :, 0]), reads=MG, writes=[self.rr_b])
        P.op("dve", lambda e: e.tensor_copy(out=rr[0:4, 0, 1:32], in_=rr[0:4, 1, 0:31]), reads=[self.rr_b], writes=[self.rr_b])
        P.op("dve", lambda e: e.tensor_copy(out=rr[32:36, 0, 0:31], in_=rr[32:36, 1, 1:32]), reads=[self.rr_b], writes=[self.rr_b])
        v3 = lambda v: v.ap[0:64, :].rearrange("p (c t) -> p c t", t=128)
        bc3 = lambda q: rr[0:64, q, :].rearrange("p (c o) -> p c o", o=1).to_broadcast([64, 32, 128])
        for q in range(4):
            if q == 0:
                P.op("dve", lambda e: e.tensor_tensor(out=v3(TQ), in0=v3(GI), in1=bc3(0), op=ALU.subtract), reads=[GI, self.rr_b], writes=TQ)
                P.op("dve", lambda e: e.tensor_scalar(out=R2(TQ), in0=R2(TQ), scalar1=-LN16, scalar2=None, op0=ALU.add), reads=TQ, writes=TQ)
                P.op("act", lambda e: e.activation(out=R2(TQ), in_=R2(TQ), func=AF.Exp), reads=TQ, writes=TQ)
            elif q == 1:
                P.op("dve", lambda e: e.tensor_tensor(out=v3(TQ), in0=v3(GI), in1=bc3(1), op=ALU.subtract), reads=[GI, self.rr_b], writes=TQ)
                P.op("dve", lambda e: e.tensor_scalar(out=R2(TQ), in0=R2(TQ), scalar1=-LN16, scalar2=None, op0=ALU.add), reads=TQ, writes=TQ)
                P.op("act", lambda e: e.activation(out=R2(TQ), in_=R2(TQ), func=AF.Exp), reads=TQ, writes=TQ)
            elif q == 2:
                P.op("dve", lambda e: e.tensor_tensor(out=v3(TQ), in0=v3(GF), in1=bc3(0), op=ALU.subtract), reads=[GF, self.rr_b], writes=TQ)
                P.op("act", lambda e: e.activation(out=R2(TQ), in_=R2(TQ), func=AF.Exp), reads=TQ, writes=TQ)
            else:
                P.op("dve", lambda e: e.memset(R2(TQ), 0.0), writes=TQ)
                P.op("dve", lambda e: e.tensor_tensor(out=v3(TQ), in0=v3(TQ), in1=bc3(0), op=ALU.add), reads=[TQ, self.rr_b], writes=TQ)
                P.op("dve", lambda e: e.tensor_tensor(out=v3(TQ), in0=v3(TQ), in1=bc3(1), op=ALU.subtract), reads=[TQ, self.rr_b], writes=TQ)
                P.op("act", lambda e: e.activation(out=R2(TQ), in_=R2(TQ), func=AF.Exp), reads=TQ, writes=TQ)
            for c in range(32):
                pv = self.ps(6 + c % 2)
                P.op("pe", lambda e, pv=pv, c=c: e.transpose(out=pv.ap[:, 0:64], in_=TQ.ap[0:64, c * 128:(c + 1) * 128], identity=self.idf[0:64, 0:64]),
                     reads=[TQ, self.const_b], writes=pv)
                P.op("dve", lambda e, pv=pv, c=c, q=q: e.tensor_copy(out=self.tg[:, c, q * 8:q * 8 + 4], in_=pv.ap[:, 0:4]), reads=pv, writes=[self.tg_b])
                P.op("act", lambda e, pv=pv, c=c, q=q: e.activation(out=self.tg[:, c, q * 8 + 4:q * 8 + 8], in_=pv.ap[:, 32:36], func=AF.Copy), reads=pv, writes=[self.tg_b])
        return steps

    def mlstm(self, l):
        P = self.P
        A = self.av_
        CH = 28672
        hf_b = [[Buf(f"hf{h}_{c}") for c in range(32)] for h in range(4)]
        sm = self.sm
        pending = []
        for hp in range(2):
            chains = [(hh, d) for hh in range(2) for d in range(2)]
            st = {}
            for ci_, (hh, d) in enumerate(chains):
                o = ci_ * CH
                cf = A(o, F32, (2, 512))
                cb = A(o + 4096, BF16, (2, 512))
                nf = A(o + 6144, F32, (2,))
                nb = A(o + 6152, BF16, (2,))
                st[ci_] = (cf, cb, nf, nb, o + 8192)
                P.op("pool", lambda e, cf=cf: e.memset(cf.ap, 0.0), writes=cf)
                P.op("pool", lambda e, cb=cb: e.memset(cb.ap, 0.0), writes=cb)
                P.op("pool", lambda e, nf=nf: e.memset(nf.ap, 0.0), writes=[nf, nb])
                P.op("pool", lambda e, nb=nb: e.memset(nb.ap, 0.0), writes=[nf, nb])
            for i in range(32):
                par = i % 2
                combine = i >= 16
                S_ = {k: {} for k in ("p1", "A", "E1", "E2", "G", "H", "I", "J", "K", "L", "M")}
                for ci_, (hh, d) in enumerate(chains):
                    h = hp * 2 + hh
                    c = i if d == 0 else 31 - i
                    c0 = c * 128
                    cf, cb, nf, nb, ot = st[ci_]
                    ot = ot + par * 10240
                    colv = [self.tg[:, c, q * 8 + d * 4 + h:q * 8 + d * 4 + h + 1] for q in range(4)]
                    qc = A(ot, BF16, (2, 128))
                    kc = A(ot + 512, BF16, (2, 128))
                    vch = A(ot + 1024, BF16, (512,))
                    ktl = A(ot + 5120, BF16, (256,))
                    pT = A(ot + 5632, BF16, (128,))
                    hst = A(ot + 6144, F32, (512,))
                    hfl = A(ot + 2048, F32, (512,))
                    sgo = A(ot + 4096, BF16, (512,))
                    hmb = A(ot + 8192, BF16, (512,))
                    hmt = A(ot + 9216, BF16, (4, 128))
                    pb = (ci_ % 2) * 4
                    psm_b = self.ps(pb, BF16)
                    psm = self.ps(pb)
                    pA = self.ps(pb + 1)
                    pCs = [self.ps(pb + 2), self.ps(pb + 3)]
                    mask = self.maskf if d == 0 else self.maskb
                    sbase = ci_ * 32 + par * 16
                    sb_ = self.sm_b[ci_ * 2 + par]
                    den = sm[:, sbase:sbase + 1]
                    st6 = sm[:, sbase + 2:sbase + 8]
                    mv = sm[:, sbase + 8:sbase + 10]
                    rs = sm[:, sbase + 10:sbase + 11]
                    nmr = sm[:, sbase + 11:sbase + 12]
                    hfb = hf_b[h][c]

                    def p1(qc=qc, kc=kc, vch=vch, ktl=ktl, pT=pT, psm=psm, psm_b=psm_b, colv=colv, mask=mask, c0=c0, h=h, hfl=hfl, sgo=sgo, hfb=hfb):
                        P.dma("sp", lambda e: e.dma_start(out=qc.ap, in_=self.QKC[:, 2 * h:2 * h + 2, c0:c0 + 128]), reads=[self.dbuf["QKC"]], writes=qc)
                        P.dma("sp", lambda e: e.dma_start(out=kc.ap, in_=self.QKC[:, 8 + 2 * h:8 + 2 * h + 2, c0:c0 + 128]), reads=[self.dbuf["QKC"]], writes=kc)
                        P.dma("sp", lambda e: e.dma_start(out=vch.ap, in_=self.VS[c0:c0 + 128, h * 512:(h + 1) * 512]), reads=[self.dbuf["VS"]], writes=vch)
                        if combine:
                            P.dma("sp", lambda e: e.dma_start(out=hfl.ap, in_=self.HF[c0:c0 + 128, h * 512:(h + 1) * 512]), reads=[hfb], writes=hfl)
                            P.dma("sp", lambda e: e.dma_start(out=sgo.ap, in_=self.SIGO[c0:c0 + 128, h * 512:(h + 1) * 512]), reads=[self.dbuf["SIGO"]], writes=sgo)
                        for dc in range(2):
                            P.op("pe", lambda e, dc=dc: e.transpose(out=psm_b.ap[:, dc * 128:(dc + 1) * 128], in_=kc.ap[:, dc, :], identity=self.idb[:]),
                                 reads=[kc, self.const_b], writes=psm_b)
                        for dc in range(2):
                            P.op("pe", lambda e, dc=dc: e.matmul(psm.ap[:, 256:384], lhsT=kc.ap[:, dc, :], rhs=qc.ap[:, dc, :], start=(dc == 0), stop=(dc == 1)),
                                 reads=[kc, qc], writes=psm)
                        P.op("act", lambda e: e.activation(out=ktl.ap, in_=psm_b.ap[:, 0:256], func=AF.Copy, scale=colv[1]), reads=[psm_b, self.tg_b], writes=ktl)
                        P.op("dve", lambda e: e.scalar_tensor_tensor(out=pT.ap, in0=psm.ap[:, 256:384], scalar=colv[0], in1=mask[:], op0=ALU.mult, op1=ALU.mult),
                             reads=[psm, self.tg_b, self.const_b], writes=pT)

                    def stA(qc=qc, vch=vch, ktl=ktl, pT=pT, psm=psm, pA=pA, pCs=pCs, cb=cb, nb=nb):
                        P.op("pe", lambda e: e.matmul(pA.ap, lhsT=pT.ap, rhs=vch.ap, start=True, stop=False), reads=[pT, vch], writes=pA)
                        for dc in range(2):
                            P.op("pe", lambda e, dc=dc: e.matmul(pA.ap, lhsT=qc.ap[:, dc, :], rhs=cb.ap[:, dc, :], start=False, stop=(dc == 1)), reads=[qc, cb], writes=pA)
                        P.op("pe", lambda e: e.matmul(psm.ap[:, 384:385], lhsT=pT.ap, rhs=self.ones1[:, 0:1], start=True, stop=False), reads=[pT, self.const_b], writes=psm)
                        for dc in range(2):
                            P.op("pe", lambda e, dc=dc: e.matmul(psm.ap[:, 384:385], lhsT=qc.ap[:, dc, :], rhs=nb.ap[:, dc:dc + 1], start=False, stop=(dc == 1)), reads=[qc, nb], writes=psm)
                        for dc in range(2):
                            P.op("pe", lambda e, dc=dc: e.matmul(psm.ap[:, 386 + dc:387 + dc], lhsT=ktl.ap[:, dc * 128:(dc + 1) * 128], rhs=self.ones1[:, 0:1], start=True, stop=True),
                                 reads=[ktl, self.const_b], writes=psm)
                        for dc in range(2):
                            P.op("pe", lambda e, dc=dc: e.matmul(pCs[dc].ap, lhsT=ktl.ap[:, dc * 128:(dc + 1) * 128], rhs=vch.ap, start=True, stop=True), reads=[ktl, vch], writes=pCs[dc])

                    def stE1(psm=psm, pA=pA, pCs=pCs, cf=cf, nf=nf, colv=colv, den=den, sb_=sb_, hst=hst, hfl=hfl, c0=c0, h=h, hfb=hfb):
                        P.op("act", lambda e: e.activation(out=den, in_=psm.ap[:, 384:385], func=AF.Abs), reads=psm, writes=[sb_])
                        P.op("dve", lambda e: e.scalar_tensor_tensor(out=nf.ap, in0=nf.ap, scalar=colv[3], in1=psm.ap[:, 386:388], op0=ALU.mult, op1=ALU.add),
                             reads=[psm, nf, self.tg_b], writes=nf)
                        P.op("dve", lambda e: e.tensor_scalar(out=den, in0=den, scalar1=colv[2], scalar2=None, op0=ALU.max), reads=[sb_, self.tg_b], writes=[sb_])
                        P.op("dve", lambda e: e.reciprocal(out=den, in_=den), reads=[sb_], writes=[sb_])
                        for dc in range(2):
                            cfd = Vw(cf.ap[:, dc, :], cf.bufs)
                            P.op("dve", lambda e, dc=dc, cfd=cfd: e.scalar_tensor_tensor(out=cfd.ap, in0=cfd.ap, scalar=colv[3], in1=pCs[dc].ap, op0=ALU.mult, op1=ALU.add),
                                 reads=[pCs[dc], cfd, self.tg_b], writes=cfd)
                        if not combine:
                            P.op("act", lambda e: e.activation(out=hst.ap, in_=pA.ap, func=AF.Copy, scale=den), reads=[pA, sb_], writes=hst)
                            P.dma("actq", lambda e: e.dma_start(out=self.HF[c0:c0 + 128, h * 512:(h + 1) * 512], in_=hst.ap), reads=hst, writes=[hfb])
                        else:
                            P.op("dve", lambda e: e.scalar_tensor_tensor(out=hst.ap, in0=pA.ap, scalar=den, in1=hfl.ap, op0=ALU.mult, op1=ALU.add), reads=[pA, sb_, hfl], writes=hst)

                    def stE2(cf=cf, cb=cb, nf=nf, nb=nb):
                        P.op("act", lambda e: e.activation(out=nb.ap, in_=nf.ap, func=AF.Copy), reads=nf, writes=nb)
                        P.op("act", lambda e: e.activation(out=cb.ap, in_=cf.ap, func=AF.Copy), reads=cf, writes=cb)

                    def stG(hst=hst, st6=st6, mv=mv, rs=rs, sb_=sb_):
                        P.op("dve", lambda e: e.bn_stats(out=st6, in_=hst.ap), reads=hst, writes=[sb_])
                        P.op("dve", lambda e: e.bn_aggr(out=mv, in_=st6), reads=[sb_], writes=[sb_])
                        P.op("dve", lambda e: e.tensor_scalar(out=rs, in0=mv[:, 1:2], scalar1=LN_EPS, scalar2=None, op0=ALU.add), reads=[sb_], writes=[sb_])

                    def stH(rs=rs, sb_=sb_):
                        P.op("act", lambda e: e.activation(out=rs, in_=rs, func=AF.Sqrt), reads=[sb_], writes=[sb_])

                    def stI(rs=rs, mv=mv, nmr=nmr, sb_=sb_):
                        P.op("dve", lambda e: e.reciprocal(out=rs, in_=rs), reads=[sb_], writes=[sb_])
                        P.op("dve", lambda e: e.scalar_tensor_tensor(out=nmr, in0=mv[:, 0:1], scalar=-1.0, in1=rs, op0=ALU.mult, op1=ALU.mult), reads=[sb_], writes=[sb_])

                    def stJ(hst=hst, rs=rs, nmr=nmr, sb_=sb_, h=h):
                        P.op("act", lambda e: e.activation(out=hst.ap, in_=hst.ap, func=AF.Identity, scale=rs, bias=nmr), reads=[hst, sb_], writes=hst)

                    def stK(hst=hst, hmb=hmb, sgo=sgo):
                        P.op("dve", lambda e: e.tensor_tensor(out=hmb.ap, in0=hst.ap, in1=sgo.ap, op=ALU.mult), reads=[hst, sgo], writes=hmb)

                    def stL(hmb=hmb, psm_b=psm_b):
                        for fc in range(4):
                            P.op("pe", lambda e, fc=fc: e.transpose(out=psm_b.ap[:, fc * 128:(fc + 1) * 128], in_=hmb.ap[:, fc * 128:(fc + 1) * 128], identity=self.idb[:]),
                                 reads=[hmb, self.const_b], writes=psm_b)

                    def stM(hmt=hmt, psm_b=psm_b, c0=c0, h=h):
                        P.op("act", lambda e: e.activation(out=hmt.ap, in_=psm_b.ap[:, 0:512].rearrange("p (a b) -> p a b", a=4), func=AF.Copy), reads=psm_b, writes=hmt)
                        P.dma("actq", lambda e: e.dma_start(out=self.HMT[:, 4 * h:4 * h + 4, c0:c0 + 128], in_=hmt.ap), reads=hmt, writes=[self.hmt_b[c0 // 128]])

                    for k, f in (("p1", p1), ("A", stA), ("E1", stE1), ("E2", stE2), ("G", stG), ("H", stH), ("I", stI), ("J", stJ), ("K", stK), ("L", stL), ("M", stM)):
                        S_[k][ci_] = f
                for ci_ in range(4):
                    S_["p1"][ci_]()
                for ci_ in (0, 1):
                    S_["A"][ci_]()
                for ci_ in (0, 1):
                    S_["E1"][ci_]()
                for ci_ in (0, 1):
                    S_["E2"][ci_]()
                for ci_ in (2, 3):
                    S_["A"][ci_]()
                if combine:
                    for ci_ in (0, 1):
                        S_["G"][ci_]()
                for ci_ in (2, 3):
                    S_["E1"][ci_]()
                for ci_ in (2, 3):
                    S_["E2"][ci_]()
                for fn in pending:
                    fn()
                pending = []
                if combine:
                    seq = [("H", (0, 1)), ("I", (0, 1)), ("G", (2, 3)), ("J", (0, 1)), ("H", (2, 3)), ("K", (0, 1)), ("I", (2, 3)), ("J", (2, 3)), ("K", (2, 3))]
                    for k, cs in seq:
                        for ci_ in cs:
                            S_[k][ci_]()
                    for pair in ((0, 1), (2, 3)):
                        for ci_ in pair:
                            pending.append(S_["L"][ci_])
                        for ci_ in pair:
                            pending.append(S_["M"][ci_])
        for fn in pending:
            fn()

    def dump_rows(self, name, v, r0, ncol, dt=F32):
        if name not in self.dbg:
            return
        if name not in self.dbg_outs:
            self.dbg_outs[name] = self.nc.dram_tensor("dbg_" + name, [S, ncol], dt, kind="ExternalOutput").ap()
        d = self.dbg_outs[name]
        self.P.dma("sp", lambda e: e.dma_start(out=d[r0:r0 + 128, :], in_=v.ap), reads=v)

    def mem_kv(self, s, l):
        P = self.P
        mt = self.av_(self.O_XF, F32, (2, 1024))
        mT = self.av_(self.O_H, BF16, (8, 256))
        P.dma("sp", lambda e: e.dma_start(out=mt.ap, in_=self.mem_in[s].rearrange("(a p) d -> p a d", p=128)), writes=mt)
        sm = self.sm
        sb_ = self.sm_b[6]
        for a in range(2):
            for hh in range(2):
                P.op("dve", lambda e, a=a, hh=hh: e.bn_stats(out=sm[:, hh * 6:hh * 6 + 6], in_=mt.ap[:, a, hh * 512:(hh + 1) * 512]), reads=mt, writes=[sb_])
            P.op("dve", lambda e: e.bn_aggr(out=sm[:, 12:14], in_=sm[:, 0:12]), reads=[sb_], writes=[sb_])
            P.op("dve", lambda e: e.tensor_scalar(out=sm[:, 14:15], in0=sm[:, 13:14], scalar1=LN_EPS, scalar2=None, op0=ALU.add), reads=[sb_], writes=[sb_])
            P.op("act", lambda e: e.activation(out=sm[:, 14:15], in_=sm[:, 14:15], func=AF.Sqrt), reads=[sb_], writes=[sb_])
            P.op("dve", lambda e: e.reciprocal(out=sm[:, 14:15], in_=sm[:, 14:15]), reads=[sb_], writes=[sb_])
            P.op("dve", lambda e, a=a: e.tensor_scalar(out=mt.ap[:, a, :], in0=mt.ap[:, a, :], scalar1=sm[:, 12:13], scalar2=sm[:, 14:15], op0=ALU.subtract, op1=ALU.mult),
                 reads=[mt, sb_], writes=mt)
            for k in range(8):
                pv = self.ps(k % 6)
                P.op("pe", lambda e, pv=pv, a=a, k=k: e.transpose(out=pv.ap[:, 0:128], in_=mt.ap[:, a, k * 128:(k + 1) * 128], identity=self.idf[:]), reads=[mt, self.const_b], writes=pv)
                g = self.pc("mem_g", l, k)
                b = self.pc("mem_b", l, k)
                P.op("act", lambda e, pv=pv, a=a, k=k, g=g, b=b: e.activation(out=mT.ap[:, k, a * 128:(a + 1) * 128], in_=pv.ap[:, 0:128], func=AF.Identity, scale=g, bias=b),
                     reads=[pv, self.prm_b], writes=mT)
        mk = lambda k: Vw(mT.ap[:, k, :], mT.bufs)

        def ev_k(m, pv):
            P.op("act", lambda e: e.activation(out=self.akT[:, m, :], in_=pv.ap[:, 0:256], func=AF.Copy, scale=1.0 / 16.0), reads=pv, writes=[self.akv_b])
        self._proj_generic("xa_wkv", l, 0, 8, 8, mk, ev_k, n=256)
        for cg in range(2):
            wv = self.load_w("xa_wkv", l, 0, 8, D + cg * 512, 512)
            for a in range(2):
                pv = self.ps((cg * 2 + a) % 6)
                for k in range(8):
                    P.op("pe", lambda e, pv=pv, wv=wv, k=k, a=a: e.matmul(pv.ap, lhsT=mT.ap[:, k, a * 128:(a + 1) * 128], rhs=wv.ap[:, k, :], start=(k == 0), stop=(k == 7)),
                         reads=[wv, mT], writes=pv)
                P.op("act", lambda e, pv=pv, a=a, cg=cg: e.activation(out=self.av[:, a, cg * 512:(cg + 1) * 512], in_=pv.ap, func=AF.Copy), reads=pv, writes=[self.akv_b])

    def _proj_generic(self, key, l, c0, nchunks, kt, rhs_of, evac, n=512, r0=0):
        P = self.P
        per = max(1, min(self.WSLOT // (kt * 128), nchunks))
        m = 0
        pi = getattr(self, "_pi", 0)
        while m < nchunks:
            n_here = min(per, nchunks - m)
            wv = self.load_w(key, l, r0, kt, c0 + m * 128, n_here * 128)
            for j in range(n_here):
                pv = self.ps(pi % 6)
                pi += 1
                for k in range(kt):
                    rk = rhs_of(k)
                    P.op("pe", lambda e, pv=pv, wv=wv, rk=rk, k=k, j=j: e.matmul(pv.ap[:, 0:n], lhsT=wv.ap[:, k, j * 128:(j + 1) * 128], rhs=rk.ap,
                                                                                 start=(k == 0), stop=(k == kt - 1)), reads=[wv, rk], writes=pv)
                evac(m + j, pv)
            m += n_here
        self._pi = pi

    def phase_c(self, s, l, last):
        P = self.P
        self.mem_kv(s, l)
        self.xb_off = self.O_XB
        for tt in range(NT):
            t0 = tt * T
            xf = self.av_(self.O_XF, F32, (8, 512))
            x1b_off = self.O_X1B if tt % 2 == 0 else self.O_X1B2
            hm = self.av_(self.O_H, BF16, (16, 512))
            hr = self.av_(self.O_ST, BF16, (8, 512))
            x1bv = self.av_(x1b_off, BF16, (8, 512))
            P.dma("sp", lambda e, x1bv=x1bv, t0=t0: e.dma_start(out=x1bv.ap, in_=self.X1B[:, :, t0:t0 + T]), reads=[self.x1_b[tt]], writes=x1bv)
            P.dma("sp", lambda e, hr=hr, t0=t0: e.dma_start(out=hr.ap, in_=self.HRT[:, :, t0:t0 + T]), reads=[self.dbuf["HRT"]], writes=hr)
            x1k = lambda k, o_=x1b_off: self.avk(o_, BF16, (8, 512), k)

            def ev_yr(m, pv):
                tmp = self.av_(self.O_TMP if m % 2 else self.O_TMP2, F32, (512,))
                b = self.pc("b_yr", l, m)
                hk = self.avk(self.O_ST, BF16, (8, 512), m)
                P.op("act", lambda e: e.activation(out=tmp.ap, in_=pv.ap, func=AF.Gelu_apprx_tanh, bias=b), reads=[pv, self.prm_b], writes=tmp)
                P.op("dve", lambda e: e.tensor_tensor(out=hk.ap, in0=hk.ap, in1=tmp.ap, op=ALU.mult), reads=[hk, tmp], writes=hk)
            self.proj_fm("w_in", l, OFF_YR, 8, 8, x1k, ev_yr)
            P.dma("sp", lambda e, hm=hm, t0=t0: e.dma_start(out=hm.ap, in_=self.HMT[:, :, t0:t0 + T]), reads=self.hmt_b[4 * tt:4 * tt + 4], writes=hm)

            def ev_mg(m, pv):
                gk = self.avk(self.O_G, BF16, (16, 512), m)
                b = self.pc("b_mg", l, m)
                P.op("act", lambda e: e.activation(out=gk.ap, in_=pv.ap, func=AF.Sigmoid, bias=b), reads=[pv, self.prm_b], writes=gk)
            self.proj_fm("w_in", l, OFF_MG, 16, 8, x1k, ev_mg)
            P.dma("sp", lambda e, xf=xf, t0=t0: e.dma_start(out=xf.ap, in_=self.X1[:, :, t0:t0 + T]), reads=[self.x1_b[tt]], writes=xf)
            hmk = lambda k: self.avk(self.O_H, BF16, (16, 512), k)
            hrk = lambda k: self.avk(self.O_ST, BF16, (8, 512), k)
            mtmp = {}

            def ev_pm(m, pv):
                tmp = self.avk(self.O_ZB, F32, (8, 512), m)
                gk = self.avk(self.O_G, BF16, (16, 512), m)
                P.op("dve", lambda e: e.tensor_tensor(out=tmp.ap, in0=pv.ap, in1=gk.ap, op=ALU.mult), reads=[pv, gk], writes=tmp)

            def ev_pr(m, pv):
                tmp = self.avk(self.O_ZB, F32, (8, 512), m)
                gk = self.avk(self.O_G, BF16, (16, 512), 8 + m)
                t2 = self.av_(self.O_TMP if m % 2 else self.O_TMP2, F32, (512,))
                mk_ = self.avk(self.O_Q, BF16, (8, 512), m)
                P.op("dve", lambda e: e.tensor_tensor(out=t2.ap, in0=pv.ap, in1=gk.ap, op=ALU.mult), reads=[pv, gk], writes=t2)
                P.op("dve", lambda e: e.tensor_tensor(out=mk_.ap, in0=tmp.ap, in1=t2.ap, op=ALU.add), reads=[tmp, t2], writes=mk_)
            self.proj_fm("w_pm", l, 0, 8, 16, hmk, ev_pm)
            self.proj_fm("w_pr", l, 0, 8, 8, hrk, ev_pr)
            if "merged" in self.dbg and tt == 0:
                self.dump("merged", self.av_(self.O_Q, BF16, (8, 512)).ap, (128, 8, 512), BF16, self.av_(self.O_Q, BF16, (8, 512)))
            mgk = lambda k: self.avk(self.O_Q, BF16, (8, 512), k)

            def ev_res(c):
                def ev(m, pv):
                    xk = self.avk(self.O_XF, F32, (8, 512), m)
                    P.op("dve", lambda e: e.scalar_tensor_tensor(out=xk.ap, in0=pv.ap, scalar=c, in1=xk.ap, op0=ALU.mult, op1=ALU.add), reads=[pv, xk], writes=xk)
                return ev
            self.proj_fm("w_out", l, 0, 8, 8, mgk, ev_res(C_MIX))
            self.layer_norm_fm(l, 1)
            if "x2" in self.dbg and tt == 0:
                self.dump("x2", xf.ap, (128, 8, 512), F32, xf)
            if self.stop == "C1":
                continue
            xbk = lambda k: self.avk(self.O_XB, BF16, (8, 512), k)

            def ev_q(m, pv):
                qk_ = self.avk(self.O_Q, BF16, (8, 512), m)
                P.op("act", lambda e: e.activation(out=qk_.ap, in_=pv.ap, func=AF.Copy), reads=pv, writes=qk_)
            self.proj_fm("xa_wq", l, 0, 8, 8, xbk, ev_q)
            ao = lambda m, o_=x1b_off: self.avk(o_, BF16, (8, 512), m)
            for h in range(4):
                eT = self.av_(self.O_E, BF16, (2, 512))
                for mc in range(2):
                    pv = self.ps((h * 2 + mc) % 4)
                    for dc in range(2):
                        qv = self.avk(self.O_Q, BF16, (8, 512), h * 2 + dc)
                        P.op("pe", lambda e, pv=pv, qv=qv, h=h, dc=dc, mc=mc: e.matmul(pv.ap, lhsT=self.akT[:, h * 2 + dc, mc * 128:(mc + 1) * 128], rhs=qv.ap, start=(dc == 0), stop=(dc == 1)),
                             reads=[qv, self.akv_b], writes=pv)
                    P.op("act", lambda e, pv=pv, eT=eT, mc=mc: e.activation(out=eT.ap[:, mc, :], in_=pv.ap, func=AF.Exp), reads=pv, writes=eT)
                pd = self.ps(4)
                for mc in range(2):
                    P.op("pe", lambda e, eT=eT, mc=mc: e.matmul(pd.ap, lhsT=self.ones1[:], rhs=eT.ap[:, mc, :], start=(mc == 0), stop=(mc == 1)), reads=[eT, self.const_b], writes=pd)
                rden = self.av_(self.O_E + 2048, F32, (512,))
                P.op("dve", lambda e, rden=rden: e.reciprocal(out=rden.ap, in_=pd.ap), reads=pd, writes=rden)
                for dc in range(2):
                    po = self.ps(5 + dc)
                    for mc in range(2):
                        P.op("pe", lambda e, po=po, eT=eT, mc=mc, h=h, dc=dc: e.matmul(po.ap, lhsT=self.av[:, mc, h * 256 + dc * 128:h * 256 + (dc + 1) * 128], rhs=eT.ap[:, mc, :],
                                                                                        start=(mc == 0), stop=(mc == 1)), reads=[eT, self.akv_b], writes=po)
                    aok = ao(h * 2 + dc)
                    P.op("dve", lambda e, po=po, aok=aok, rden=rden: e.tensor_tensor(out=aok.ap, in0=po.ap, in1=rden.ap, op=ALU.mult), reads=[po, rden], writes=aok)
            self.proj_fm("xa_wo", l, 0, 8, 8, ao, ev_res(C_MIX))
            self.layer_norm_fm(l, 2)
            if "x3" in self.dbg and tt == 0:
                self.dump("x3", xf.ap, (128, 8, 512), F32, xf)
            self.ffn_fm(l, "ff2")
            self.layer_norm_fm(l, 3)
            if not last:
                P.dma("actq", lambda e, xf=xf, t0=t0: e.dma_start(out=self.XN[:, :, t0:t0 + T], in_=xf.ap), reads=xf, writes=[self.xn_b[tt]])
                xbv2 = self.av_(self.O_XB, BF16, (8, 512))
                P.dma("actq", lambda e, xbv2=xbv2, t0=t0: e.dma_start(out=self.XNB[:, :, t0:t0 + T], in_=xbv2.ap), reads=xbv2, writes=[self.xn_b[tt]])
            else:
                yt = self.av_(self.O_G, F32, (4, 1024))
                for a in range(4):
                    for kk in range(2):
                        pv = self.ps((a * 2 + kk) % 6)
                        for k4 in range(4):
                            k = kk * 4 + k4
                            P.op("pe", lambda e, pv=pv, xf=xf, a=a, k=k, k4=k4: e.transpose(out=pv.ap[:, k4 * 128:(k4 + 1) * 128], in_=xf.ap[:, k, a * 128:(a + 1) * 128], identity=self.idf[:]),
                                 reads=[xf, self.const_b], writes=pv)
                        P.op("act" if kk else "dve", (lambda e, pv=pv, yt=yt, a=a, kk=kk: e.activation(out=yt.ap[:, a, kk * 512:(kk + 1) * 512], in_=pv.ap, func=AF.Copy)) if kk else
                             (lambda e, pv=pv, yt=yt, a=a, kk=kk: e.tensor_copy(out=yt.ap[:, a, kk * 512:(kk + 1) * 512], in_=pv.ap)), reads=pv, writes=yt)
                P.dma("actq", lambda e, yt=yt, t0=t0: e.dma_start(out=self.y_out[s, t0:t0 + T, :].rearrange("(a p) d -> p a d", p=128), in_=yt.ap), reads=yt)

    def build(self):
        self.setup()
        self.setup_params()
        stop = self.stop
        if stop == "S":
            self.P.final_wait_all("sp")
            self.P.emit()
            return self.nc
        for s in range(self.nseq):
            for l in range(self.nlayers):
                self.layer_setup(l)
                self.phase_a(s, l)
                if stop in ("A1", "A"):
                    break
                self.phase_b(s, l)
                if stop in ("B0", "B1", "B2g", "B"):
                    break
                self.phase_c(s, l, last=(l == self.nlayers - 1))
                if stop in ("C1", "C"):
                    break
            if stop:
                break
        for name, ap, shape, dt in (("X1", self.X1, (128, 8, S), F32), ("XN", self.XN, (128, 8, S), F32), ("QKP", self.QKP, (128, 16, S + 4), BF16),
                                    ("XRP", self.XRP, (128, 8, S + 4), BF16), ("QKC", self.QKC, (128, 16, S), BF16), ("VS", self.VS, (S, 2048), BF16),
                                    ("SIGO", self.SIGO, (S, 2048), BF16), ("HF", self.HF, (S, 2048), F32), ("HMT", self.HMT, (128, 16, S), BF16),
                                    ("HRT", self.HRT, (128, 8, S), BF16), ("GSC", self.GSC, (2, 64, S), F32)):
            bufs = [self.dbuf[name]] + (self.x1_b if name == "X1" else []) + (self.xn_b if name == "XN" else []) + (self.hmt_b if name == "HMT" else [])
            self.dump(name, ap, shape, dt, bufs)
        if "TG" in self.dbg:
            self.dump("TG", self.tg[:], (128, 32, 32), F32, [self.tg_b])
        self.P.final_wait_all("sp")
        self.P.emit()
        return self.nc


_W_KEYS = ("w_in", "b_in", "w_conv_qk", "b_conv_qk", "w_conv_r", "b_conv_r", "mh_gain", "lru_wa", "lru_ba", "lru_wx",
           "lru_bx", "lru_lam", "w_pm", "w_pr", "w_out", "xa_wq", "xa_wkv", "xa_wo", "mem_ln_g", "mem_ln_b",
           "ff1_in", "ff1_out", "ff2_in", "ff2_out", "ln_g", "ln_b")


def kernel(x_prompt, x_sample, mem_prompt, mem_sample, **weights):
    ncores = 8
    xs = np.concatenate([np.asarray(x_prompt, np.float32), np.asarray(x_sample, np.float32)], axis=0)
    ms = np.concatenate([np.asarray(mem_prompt, np.float32), np.asarray(mem_sample, np.float32)], axis=0)
    nseq = xs.shape[0] // ncores
    b = Builder(nseq=nseq)
    nc = b.build()
    wts = {k: np.ascontiguousarray(np.asarray(weights[k], np.float32)) for k in _W_KEYS}
    in_maps = []
    for c in range(ncores):
        m = {"x": np.ascontiguousarray(xs[c * nseq:(c + 1) * nseq]), "mem": np.ascontiguousarray(ms[c * nseq:(c + 1) * nseq])}
        m.update(wts)
        in_maps.append(m)
    res = run_bass_kernel_spmd(nc, in_maps, core_ids=list(range(ncores)))
    y = np.concatenate([np.asarray(r["y"], np.float32) for r in res.results], axis=0)
    nb = np.asarray(x_prompt).shape[0]
    return (np.ascontiguousarray(y[:nb]), np.ascontiguousarray(y[nb:]))
```
